# Optimizing a Trainium2 kernel written in Bass

```python
import jax, jax.numpy as jnp
from jax import lax
import numpy as np

D_MODEL = 1024
BATCH = 8
SEQ = 4096
DEPTH = 1

PLE_DIM = 256
RMS_EPS = 1e-6

DN_HEADS = 4
DN_HEAD_DIM = 128
DN_WIDTH = DN_HEADS * DN_HEAD_DIM
DN_CONV = 4
DN_CHUNK = 64

SWA_Q_HEADS = 8
SWA_KV_HEADS = 2
SWA_HEAD_DIM = 64
SWA_WIDTH = SWA_Q_HEADS * SWA_HEAD_DIM
SWA_KV_WIDTH = SWA_KV_HEADS * SWA_HEAD_DIM
WINDOW = 128
ROPE_THETA = 500000.0
ROT_DIM = SWA_HEAD_DIM // 4

MIX_WIDTH = DN_WIDTH + SWA_WIDTH
IN_COLS = 3 * DN_WIDTH + DN_WIDTH + 2 * DN_HEADS + SWA_WIDTH + 2 * SWA_KV_WIDTH

PEER_HEADS = 8
PEER_N_KEYS = 128
PEER_EXPERTS = PEER_N_KEYS * PEER_N_KEYS
PEER_TOPK = 16
PEER_KEY_DIM = 128
PEER_QUERY_DIM = 2 * PEER_KEY_DIM
PEER_BLOCK = 128

kernel_name = "hybrid_deltanet_swa_peer_block"


def rmsnorm(x, w):
    xf = x.astype(jnp.float32)
    y = xf * lax.rsqrt(jnp.mean(xf * xf, axis=-1, keepdims=True) + RMS_EPS)
    return (y * w.astype(jnp.float32)).astype(x.dtype)


def l2norm(t):
    return t * lax.rsqrt(jnp.sum(t * t, axis=-1, keepdims=True) + RMS_EPS)


def partial_rotary(t, positions):
    half = ROT_DIM // 2
    inv_freq = ROPE_THETA ** (-jnp.arange(0, ROT_DIM, 2, dtype=jnp.float32) / ROT_DIM)
    ang = positions.astype(jnp.float32)[..., None] * inv_freq
    cos = jnp.cos(ang)[:, :, None, :]
    sin = jnp.sin(ang)[:, :, None, :]
    tf = t.astype(jnp.float32)
    x1 = tf[..., :half]
    x2 = tf[..., half:ROT_DIM]
    out = jnp.concatenate([x1 * cos - x2 * sin, x2 * cos + x1 * sin, tf[..., ROT_DIM:]], axis=-1)
    return out.astype(t.dtype)


def gated_deltanet(qkv, zg, b, a, conv_w, dt_bias, a_log, out_norm):
    B, S, _ = qkv.shape
    H, d, C = DN_HEADS, DN_HEAD_DIM, DN_CHUNK
    NC = S // C
    out_dtype = qkv.dtype
    qkv = lax.conv_general_dilated(
        qkv, conv_w[:, None, :], window_strides=(1,), padding=[(DN_CONV - 1, 0)],
        dimension_numbers=("NWC", "WIO", "NWC"), feature_group_count=3 * DN_WIDTH)
    qkv = jax.nn.silu(qkv.astype(jnp.float32))
    q, k, v = jnp.split(qkv, 3, axis=-1)
    q = l2norm(q.reshape(B, S, H, d)) * (d ** -0.5)
    k = l2norm(k.reshape(B, S, H, d))
    v = v.reshape(B, S, H, d)
    beta = jax.nn.sigmoid(b.astype(jnp.float32))
    g = -jnp.exp(a_log.astype(jnp.float32)) * jax.nn.softplus(
        a.astype(jnp.float32) + dt_bias.astype(jnp.float32))

    def to_chunks(t):
        return t.reshape(B, NC, C, H, d).transpose(0, 3, 1, 2, 4)

    q, k, v = to_chunks(q), to_chunks(k), to_chunks(v)
    beta = beta.reshape(B, NC, C, H).transpose(0, 3, 1, 2)
    gc = jnp.cumsum(g.reshape(B, NC, C, H).transpose(0, 3, 1, 2), axis=-1)

    pos = jnp.arange(C)
    causal = pos[:, None] >= pos[None, :]
    strict = pos[:, None] > pos[None, :]
    decay = jnp.exp(jnp.where(causal, gc[..., :, None] - gc[..., None, :], -jnp.inf))

    kb = k * beta[..., None]
    a_kk = jnp.where(strict, jnp.einsum("bhncd,bhnkd->bhnck", kb, k) * decay, 0.0)
    rhs = jnp.concatenate([v * beta[..., None], kb * jnp.exp(gc)[..., None]], axis=-1)
    uw = lax.linalg.triangular_solve(a_kk, rhs, left_side=True, lower=True, unit_diagonal=True)
    u, w = uw[..., :d], uw[..., d:]
    a_qk = jnp.einsum("bhncd,bhnkd->bhnck", q, k) * decay

    def lead(t):
        return jnp.moveaxis(t, 2, 0)

    xs = (lead(q), lead(k), lead(u), lead(w), lead(a_qk), lead(gc))

    def step(state, inp):
        qc, kc, uc, wc, aqk, gcc = inp
        v_new = uc - jnp.einsum("bhcd,bhde->bhce", wc, state)
        o = (jnp.einsum("bhcd,bhde->bhce", qc * jnp.exp(gcc)[..., None], state)
             + jnp.einsum("bhck,bhke->bhce", aqk, v_new))
        g_last = gcc[..., -1]
        state = (state * jnp.exp(g_last)[..., None, None]
                 + jnp.einsum("bhcd,bhce->bhde", kc * jnp.exp(g_last[..., None] - gcc)[..., None], v_new))
        return state, o

    state0 = jnp.zeros((B, H, d, d), jnp.float32)
    _, o = lax.scan(step, state0, xs)
    o = o.transpose(1, 0, 3, 2, 4).reshape(B, S, H, d)
    o = o * lax.rsqrt(jnp.mean(o * o, axis=-1, keepdims=True) + RMS_EPS) * out_norm.astype(jnp.float32)
    o = o * jax.nn.silu(zg.astype(jnp.float32).reshape(B, S, H, d))
    return o.reshape(B, S, DN_WIDTH).astype(out_dtype)


def swa_sink_attention(q, k, v, sinks):
    B, S = q.shape[0], q.shape[1]
    NB = S // WINDOW
    G = SWA_Q_HEADS // SWA_KV_HEADS
    qb = q.reshape(B, NB, WINDOW, SWA_KV_HEADS, G, SWA_HEAD_DIM)

    def with_prev(t):
        tb = t.reshape(B, NB, WINDOW, SWA_KV_HEADS, SWA_HEAD_DIM)
        prev = jnp.pad(tb[:, :-1], ((0, 0), (1, 0), (0, 0), (0, 0), (0, 0)))
        return jnp.concatenate([prev, tb], axis=2)

    kk, vv = with_prev(k), with_prev(v)
    s = jnp.einsum("bnqhgd,bnkhd->bnhgqk", qb, kk,
                   preferred_element_type=jnp.float32) * (SWA_HEAD_DIM ** -0.5)
    qi = jnp.arange(WINDOW)[:, None] + WINDOW
    ki = jnp.arange(2 * WINDOW)[None, :]
    rel = qi - ki
    band = (rel >= 0) & (rel < WINDOW)
    blk = jnp.arange(NB)[:, None, None]
    mask = band[None] & ((blk > 0) | (ki[None] >= WINDOW))
    s = jnp.where(mask[None, :, None, None], s, -jnp.inf)
    sk = sinks.astype(jnp.float32).reshape(1, 1, SWA_KV_HEADS, G, 1, 1)
    m = jnp.maximum(jnp.max(s, axis=-1, keepdims=True), sk)
    pexp = jnp.exp(s - m)
    attn = pexp / (jnp.sum(pexp, axis=-1, keepdims=True) + jnp.exp(sk - m))
    o = jnp.einsum("bnhgqk,bnkhd->bnqhgd", attn.astype(v.dtype), vv)
    return o.reshape(B, S, SWA_WIDTH)


def peer(h, wq, sub_keys, u_tab, v_tab):
    B, S, D = h.shape
    T = B * S
    K = PEER_TOPK
    ht = h.reshape(T, D)
    qry = (ht @ wq).reshape(T, PEER_HEADS, 2, PEER_KEY_DIM)
    sc = jnp.einsum("thpd,hpnd->thpn", qry, sub_keys).astype(jnp.float32)
    top_s, top_i = lax.top_k(sc, K)
    cand_s = (top_s[:, :, 0, :, None] + top_s[:, :, 1, None, :]).reshape(T, PEER_HEADS, K * K)
    cand_i = (top_i[:, :, 0, :, None] * PEER_N_KEYS + top_i[:, :, 1, None, :]).reshape(T, PEER_HEADS, K * K)
    best_s, best_pos = lax.top_k(cand_s, K)
    expert = jnp.take_along_axis(cand_i, best_pos, axis=-1)
    gates = jax.nn.softmax(best_s, axis=-1).astype(h.dtype)

    def block(args):
        xb, eb, gb = args
        act = jax.nn.gelu(jnp.einsum("td,thkd->thk", xb, u_tab[eb]))
        return jnp.einsum("thk,thkd->td", gb * act, v_tab[eb])

    nb = T // PEER_BLOCK
    y = lax.map(block, (ht.reshape(nb, PEER_BLOCK, D),
                        expert.reshape(nb, PEER_BLOCK, PEER_HEADS, K),
                        gates.reshape(nb, PEER_BLOCK, PEER_HEADS, K)))
    return y.reshape(B, S, D)


def setup_inputs(seed: int = 0) -> dict:
    key = jax.random.key(seed)
    ks = jax.random.split(key, 24)
    f32 = jnp.float32
    L, D = DEPTH, D_MODEL

    def nrm(k, shape, scale):
        return jax.random.normal(k, shape, f32) * scale

    def gain(k, shape):
        return 1.0 + 0.01 * jax.random.normal(k, shape, f32)

    x = jax.random.normal(ks[0], (BATCH, SEQ, D), f32)
    p = jax.random.normal(ks[1], (DEPTH, BATCH, SEQ, PLE_DIM), f32)
    offset = jax.random.randint(ks[2], (BATCH, 1), 0, 1024, dtype=jnp.int32)
    positions = offset + jnp.arange(SEQ, dtype=jnp.int32)[None, :]
    return {
        "x": x,
        "p": p,
        "positions": positions,
        "mix_norm": gain(ks[3], (L, D)),
        "w_in": nrm(ks[4], (L, D, IN_COLS), D ** -0.5),
        "conv_w": nrm(ks[5], (L, DN_CONV, 3 * DN_WIDTH), 0.5),
        "dn_dt_bias": nrm(ks[6], (L, DN_HEADS), 0.1),
        "dn_a_log": jnp.log(jax.random.uniform(ks[7], (L, DN_HEADS), f32, 1.0, 16.0)),
        "dn_out_norm": gain(ks[8], (L, DN_HEAD_DIM)),
        "attn_sinks": nrm(ks[9], (L, SWA_Q_HEADS), 0.5),
        "w_out": nrm(ks[10], (L, MIX_WIDTH, D), MIX_WIDTH ** -0.5),
        "ffn_norm": gain(ks[11], (L, D)),
        "peer_wq": nrm(ks[12], (L, D, PEER_HEADS * PEER_QUERY_DIM), D ** -0.5),
        "peer_keys": nrm(ks[13], (L, PEER_HEADS, 2, PEER_N_KEYS, PEER_KEY_DIM), PEER_KEY_DIM ** -0.5),
        "peer_u": nrm(ks[14], (L, PEER_EXPERTS, D), D ** -0.5),
        "peer_v": nrm(ks[15], (L, PEER_EXPERTS, D), (PEER_HEADS * PEER_TOPK) ** -0.5),
        "ple_norm": gain(ks[16], (L, D)),
        "ple_gate": nrm(ks[17], (L, D, D), D ** -0.5),
        "ple_proj": nrm(ks[18], (L, PLE_DIM, D), PLE_DIM ** -0.5),
        "final_norm": gain(ks[19], (D,)),
    }


def reference(x, p, positions, mix_norm, w_in, conv_w, dn_dt_bias, dn_a_log, dn_out_norm,
              attn_sinks, w_out, ffn_norm, peer_wq, peer_keys, peer_u, peer_v,
              ple_norm, ple_gate, ple_proj, final_norm):
    B, S, _ = x.shape
    split_points = [3 * DN_WIDTH,
                    4 * DN_WIDTH,
                    4 * DN_WIDTH + DN_HEADS,
                    4 * DN_WIDTH + 2 * DN_HEADS,
                    4 * DN_WIDTH + 2 * DN_HEADS + SWA_WIDTH,
                    4 * DN_WIDTH + 2 * DN_HEADS + SWA_WIDTH + SWA_KV_WIDTH]
    for i in range(DEPTH):
        h = rmsnorm(x, mix_norm[i])
        zc = h @ w_in[i]
        dn_qkv, dn_z, dn_b, dn_a, sw_q, sw_k, sw_v = jnp.split(zc, split_points, axis=-1)
        o_dn = gated_deltanet(dn_qkv, dn_z, dn_b, dn_a, conv_w[i], dn_dt_bias[i],
                              dn_a_log[i], dn_out_norm[i])
        sw_q = partial_rotary(sw_q.reshape(B, S, SWA_Q_HEADS, SWA_HEAD_DIM), positions)
        sw_k = partial_rotary(sw_k.reshape(B, S, SWA_KV_HEADS, SWA_HEAD_DIM), positions)
        sw_v = sw_v.reshape(B, S, SWA_KV_HEADS, SWA_HEAD_DIM)
        o_sw = swa_sink_attention(sw_q, sw_k, sw_v, attn_sinks[i])
        x = x + jnp.concatenate([o_dn, o_sw], axis=-1) @ w_out[i]
        x = x + peer(rmsnorm(x, ffn_norm[i]), peer_wq[i], peer_keys[i], peer_u[i], peer_v[i])
        gate = jax.nn.sigmoid(rmsnorm(x, ple_norm[i]) @ ple_gate[i])
        x = x + gate * (p[i] @ ple_proj[i])
    return rmsnorm(x, final_norm)
```

```python
import numpy as np
from contextlib import ExitStack
import concourse.bass as bass
import concourse.mybir as mybir
from concourse.bass_utils import run_bass_kernel_spmd

F32 = mybir.dt.float32
BF16 = mybir.dt.bfloat16
I32 = mybir.dt.int32
U32 = mybir.dt.uint32
ALU = mybir.AluOpType
AF = mybir.ActivationFunctionType
AX = mybir.AxisListType

D = 1024
SEQ = 4096
EPS = 1e-6
UVB_KIND = "Internal"


class _Op:
    __slots__ = ("eng", "fn", "is_dma", "deps", "signal", "tok", "dsem", "dval")

    def __init__(self, eng, fn, is_dma):
        self.eng = eng
        self.fn = fn
        self.is_dma = is_dma
        self.deps = []
        self.signal = False
        self.tok = None
        self.dsem = None
        self.dval = 0


class _Rec:
    def __init__(self):
        self.call = None

    def __getattr__(self, name):
        def f(*a, **k):
            self.call = (name, a, k)
            return self
        return f


class Prog:
    ENGS = ("pe", "act", "dve", "pool", "sp")
    NDMA = {"sp": 24, "act": 8, "pool": 40}

    def __init__(self, nc):
        self.nc = nc
        self.ops = {e: [] for e in self.ENGS}
        self.res = {}
        self.dk = {e: 0 for e in self.ENGS}
        self.last_dma = {}
        self.pending = None
        self.bankmap = list(range(8))

    def _add(self, eng, fn, r, w, is_dma):
        bm = self.bankmap
        r = [(("ps", bm[k[1]]) if (isinstance(k, tuple) and len(k) == 2 and k[0] == "ps") else k) for k in r]
        w = [(("ps", bm[k[1]]) if (isinstance(k, tuple) and len(k) == 2 and k[0] == "ps") else k) for k in w]
        if fn is not None:
            rec = _Rec()
            fn(rec)
            name, a, k = rec.call
            fn = (lambda engine, name=name, a=a, k=k: getattr(engine, name)(*a, **k))
        if self.pending is not None:
            self.pending.append((eng, fn, list(r), list(w), is_dma))
            return None
        return self._commit(eng, fn, r, w, is_dma)

    def begin(self):
        self.pending = []

    def end(self):
        lst, self.pending = self.pending, None
        return lst

    def merge(self, lists, spans=None, starts=None):
        items = []
        for li, lst in enumerate(lists):
            span = 1.0 if spans is None else spans[li]
            st = 0.0 if starts is None else starts[li]
            for j, it in enumerate(lst):
                items.append((st + (j + 0.5) / len(lst) * (span - st), li, j, it))
        items.sort(key=lambda t: t[:3])
        for _, _, _, (eng, fn, r, w, is_dma) in items:
            self._commit(eng, fn, r, w, is_dma)

    def _commit(self, eng, fn, r, w, is_dma):
        op = _Op(eng, fn, is_dma)
        if is_dma:
            op.dval = self.dk[eng] % self.NDMA[eng]
            self.dk[eng] += 1
            self.last_dma[(eng, op.dval)] = op
        deps = []
        for k in r:
            st = self.res.get(k)
            if st is not None and st[0] is not None:
                deps.append(st[0])
        for k in w:
            st = self.res.get(k)
            if st is not None:
                if st[0] is not None:
                    deps.append(st[0])
                deps.extend(st[1])
        for k in r:
            st = self.res.get(k)
            if st is None:
                st = self.res[k] = [None, []]
            st[1].append(op)
        for k in w:
            self.res[k] = [op, []]
        seen = set()
        for d in deps:
            if d is op or id(d) in seen:
                continue
            seen.add(id(d))
            if d.eng == "pe" and eng == "pe" and not d.is_dma and not is_dma:
                continue
            op.deps.append(d)
            d.signal = True
        self.ops[eng].append(op)
        return op

    def op(self, eng, fn, r=(), w=()):
        return self._add(eng, fn, r, w, False)

    def dma(self, q, fn, r=(), w=()):
        return self._add(q, fn, r, w, True)

    def barrier(self):
        lasts = list(self.last_dma.values())
        for e in self.ENGS:
            for op in reversed(self.ops[e]):
                if not op.is_dma and op.fn is not None:
                    lasts.append(op)
                    break
        for d in lasts:
            d.signal = True
        for e in self.ENGS:
            op = _Op(e, None, False)
            op.deps = list(lasts)
            self.ops[e].append(op)

    def emit(self, es):
        nc = self.nc
        csem = {e: es.enter_context(nc.semaphore("c_" + e)) for e in self.ENGS}
        dsem = {e: [es.enter_context(nc.semaphore("d_%s_%d" % (e, i))) for i in range(n)]
                for e, n in self.NDMA.items()}
        for e in self.ENGS:
            cnt = 0
            k = 0
            vals = [0] * self.NDMA.get(e, 0)
            for op in self.ops[e]:
                if op.is_dma:
                    s = op.dval
                    op.dval = vals[s]
                    vals[s] += 16
                    op.dsem = dsem[e][s]
                    op.tok = (op.dsem, vals[s])
                elif op.signal:
                    cnt += 1
                    op.tok = (csem[e], cnt)
        ops = self.ops

        def make(e):
            def body(engine):
                waited = {}
                for op in ops[e]:
                    waits = [d.tok for d in op.deps]
                    if op.is_dma and op.dval > 0:
                        waits.append((op.dsem, op.dval))
                    for (s, v) in waits:
                        if waited.get(s, 0) >= v:
                            continue
                        engine.wait_ge(s, v)
                        waited[s] = v
                    if op.fn is None:
                        continue
                    ins = op.fn(engine)
                    if op.is_dma:
                        ins.then_inc(op.dsem, 16)
                    elif op.signal:
                        ins.then_inc(csem[e], 1)
            return body

        with nc.Block() as block:
            block.tensor(make("pe"))
            block.scalar(make("act"))
            block.vector(make("dve"))
            block.gpsimd(make("pool"))
            block.sync(make("sp"))


C_ID = 0
C_IOTA16 = 128
C_IOTA16x16 = 144
C_TRI = 160
C_NEGI = 288
C_NEGS = 416
C_MCUR = 544
C_MPREV = 672
C_PERM = 800
C_INVF = 928
C_END = 932
NEG = -30000.0

W_Q, W_K, W_V, W_Z, W_B, W_A, W_SQ, W_SK, W_SV, W_COLS = 0, 512, 1024, 1536, 2048, 2052, 2056, 2568, 2696, 2824


def make_consts():
    c = np.zeros((128, C_END), np.float32)
    i = np.arange(128)
    c[:, C_ID:C_ID + 128] = np.eye(128, dtype=np.float32)
    c[:, C_IOTA16:C_IOTA16 + 16] = np.arange(16, dtype=np.float32)[None, :]
    c[:, C_IOTA16x16:C_IOTA16x16 + 16] = (16.0 * np.arange(16, dtype=np.float32))[None, :]
    le = (i[:, None] <= i[None, :])
    lt = (i[:, None] < i[None, :])
    c[:, C_TRI:C_TRI + 128] = le.astype(np.float32)
    c[:, C_NEGI:C_NEGI + 128] = np.where(le, 0.0, NEG)
    c[:, C_NEGS:C_NEGS + 128] = np.where(lt, 0.0, NEG)
    c[:, C_MCUR:C_MCUR + 128] = le.astype(np.float32)
    c[:, C_MPREV:C_MPREV + 128] = (~le).astype(np.float32)
    perm = np.zeros((128, 128), np.float32)
    invf = np.zeros((128,), np.float32)
    for m in range(128):
        d = m % 64
        if d < 8:
            perm[m + 8, m] = -1.0
        elif d < 16:
            perm[m - 8, m] = 1.0
        if d < 16:
            invf[m] = np.float32(500000.0) ** np.float32(-(2.0 * (d % 8)) / 16.0)
    c[:, C_PERM:C_PERM + 128] = perm
    c[:, C_INVF] = invf
    return c


class _Cut(Exception):
    pass


def build(nc, NT, dbg=False, cut=99, phase2=True):
    T = NT * 128
    P = Prog(nc)
    es = ExitStack()
    TWO_PI = 6.283185307179586
    PI = 3.141592653589793

    def dram(name, shape, dt, kind):
        return nc.dram_tensor(name, list(shape), dt, kind=kind).ap()

    x_d = dram("x", [T, D], F32, "ExternalInput")
    p_d = dram("p", [T, 256], F32, "ExternalInput")
    pos_d = dram("positions", [T], I32, "ExternalInput")
    consts_d = dram("consts", [128, C_END], F32, "ExternalInput")
    mix_norm_d = dram("mix_norm", [D], F32, "ExternalInput")
    win_d = dram("w_in", [D, W_COLS], F32, "ExternalInput")
    convT_d = dram("convT", [128, 48], F32, "ExternalInput")
    dtb_d = dram("dn_dt_bias", [4], F32, "ExternalInput")
    alog_d = dram("dn_a_log", [4], F32, "ExternalInput")
    onorm_d = dram("dn_out_norm", [128], F32, "ExternalInput")
    sinks_d = dram("attn_sinks", [8], F32, "ExternalInput")
    wout_d = dram("w_out", [D, D], F32, "ExternalInput")
    ffn_norm_d = dram("ffn_norm", [D], F32, "ExternalInput")
    ple_norm_d = dram("ple_norm", [D], F32, "ExternalInput")
    final_norm_d = dram("final_norm", [D], F32, "ExternalInput")
    wq_d = dram("peer_wq", [D, 2048], F32, "ExternalInput")
    keysT_d = dram("peer_keysT", [16, 128, 128], F32, "ExternalInput")
    uv_d = dram("peer_uv", [16384, 2 * D], F32, "ExternalInput")
    uvb_d = dram("peer_uvb", [16384, 2 * D], BF16, UVB_KIND)
    gate_d = dram("ple_gate", [D, D], F32, "ExternalInput")
    proj_d = dram("ple_proj", [256, D], F32, "ExternalInput")
    out_d = dram("out", [T, D], F32, "ExternalOutput")
    x1_d = dram("x1_scratch", [T, D], F32, "ExternalOutput" if dbg else "Internal")
    dbg_d = {}

    def sb(name, shape, dt):
        return es.enter_context(nc.sbuf_tensor("s_" + name, list(shape), dt))

    ps = [es.enter_context(nc.psum_tensor("ps%d" % i, [128, 512], F32)) for i in range(8)]
    PRIV_WORDS = 36378
    priv = sb("priv", [128, PRIV_WORDS], F32)
    cur = [0]

    def pv(name, shape, dt):
        nel = int(np.prod(shape[1:]))
        esz = 4 if dt in (F32, I32, U32) else 2
        nw = (nel * esz + 3) // 4
        assert cur[0] + nw <= PRIV_WORDS, (name, cur[0], nw)
        ap = priv[:, cur[0]:cur[0] + nw]
        cur[0] += nw
        if dt != F32:
            ap = ap.bitcast(dt)
        ap = ap[:, 0:nel]
        if len(shape) == 3:
            ap = ap.rearrange("p (a b) -> p a b", a=shape[1])
        elif len(shape) == 4:
            ap = ap.rearrange("p (a b c) -> p a b c", a=shape[1], b=shape[2])
        return ap

    consts = sb("consts", [128, C_END], F32)
    ident_bf = sb("ident_bf", [128, 128], BF16)
    ones_bf = sb("ones_bf", [128, 128], BF16)
    perm_bf = sb("perm_bf", [128, 128], BF16)
    mcur_bf = sb("mcur_bf", [128, 128], BF16)
    mprev_bf = sb("mprev_bf", [128, 128], BF16)
    arena = sb("arena", [128, 24576], BF16)
    convT = sb("convT", [128, 12, 4], F32)
    dtb_b = sb("dtb_b", [128, 4], F32)
    nea_b = sb("nea_b", [128, 4], F32)
    onorm_b = sb("onorm_b", [128, 128], F32)
    esink_b = sb("esink_b", [128, 8], F32)

    win = arena[:, 0:8 * W_COLS].rearrange("p (k c) -> p k c", k=8)
    wq = arena[:, 0:16384].rearrange("p (k c) -> p k c", k=8)
    gate_w = arena[:, 16384:24576].rearrange("p (k c) -> p k c", k=8)

    x1 = sb("x1t", [128, D], F32)
    y = sb("y", [128, D], F32)
    ss = sb("ss", [128, 8], F32)
    hbf = sb("hbf", [128, D], BF16)
    hT = sb("hT", [128, 8, 128], BF16)
    wout = pv("wout", [128, 8, D], BF16)
    cvb = [pv("cvb%d" % i, [128, D], BF16) for i in range(2)]
    mix_b = pv("mix_b", [128, D], F32)
    zq = pv("zq", [128, 12, 131], F32)
    cacc = pv("cacc", [128, 12, 128], F32)
    qkvc = pv("qkvc", [128, 12, 128], F32)
    sqbf = pv("sqbf", [128, 8, 128], BF16)
    rs8 = pv("rs8", [128, 8, 128], F32)
    qTn = pv("qTn", [128, 4, 128], BF16)
    kTn = pv("kTn", [128, 4, 128], BF16)
    vTb = pv("vTb", [128, 4, 128], BF16)
    bv = pv("bv", [128, 4, 128], BF16)
    tpb = pv("tpb", [128, 4, 128], BF16)
    tpb2 = pv("tpb2", [128, 4, 128], BF16)
    kd = pv("kd", [128, 4, 128], BF16)
    qgT = pv("qgT", [128, 4, 128], BF16)
    g8 = pv("g8", [128, 24], F32)
    q8 = pv("q8", [128, 8], F32)
    nq8 = pv("nq8", [128, 4], F32)
    qrep = pv("qrep", [128, 8, 128], F32)
    egl = pv("egl", [128, 4], F32)
    kds = pv("kds", [128, 4], F32)
    nbeg = pv("nbeg", [128, 4], F32)
    beta = pv("beta", [128, 4], F32)
    egcrow = pv("egcrow", [128, 4, 128], F32)
    tmpI = pv("tmpI", [128, 4, 128], F32)
    tmpS = pv("tmpS", [128, 4, 128], F32)
    DTI = pv("DTI", [128, 4, 128], F32)
    EB = pv("EB", [128, 4, 128], F32)
    aqkT = pv("aqkT", [128, 4, 128], BF16)
    Wm = [pv("Wm%d" % i, [128, 4, 128], BF16) for i in range(2)]
    Vm = [pv("Vm%d" % i, [128, 4, 128], BF16) for i in range(2)]
    VI = pv("VI", [128, 4, 128], BF16)
    Pm = [pv("Pm%d" % i, [128, 4, 128], BF16) for i in range(2)]
    Sst = [pv("Sst%d" % i, [128, 4, 128], BF16) for i in range(2)]
    rr = pv("rr", [128, 4, 128], BF16)
    vnew = pv("vnew", [128, 4, 128], BF16)
    osq = pv("osq", [128, 4, 128], F32)
    o4 = pv("o4", [128, 8], F32)
    ot = pv("ot", [128, 4, 128], F32)
    zs = pv("zs", [128, 512], F32)
    mix = pv("mix", [128, D], BF16)
    mixT = pv("mixT", [128, 8, 128], BF16)
    qraw = pv("qraw", [128, 5, 128], F32)
    qrb = pv("qrb", [128, 5, 128], BF16)
    qrot = pv("qrot", [128, 5, 128], BF16)
    qtmp = pv("qtmp", [128, 5, 128], F32)
    krot = [pv("krot%d" % i, [128, 128], BF16) for i in range(2)]
    v1 = [pv("v1_%d" % i, [128, 2, 65], BF16) for i in range(2)]
    posi = pv("posi", [128, 128], I32)
    ang = pv("ang", [128, 2, 128], F32)
    angk = pv("angk", [128, 2, 128], I32)
    angf = pv("angf", [128, 2, 128], F32)
    sincos = pv("sincos", [128, 2, 128], F32)
    Ecur = [pv("Ecur%d" % i, [128, 512], BF16) for i in range(2)]
    Eprev = [pv("Eprev%d" % i, [128, 512], BF16) for i in range(2)]
    den = pv("den", [128, 8], F32)
    x1I = [x1, pv("x1I2", [128, D], F32)]
    zsp = [zs, pv("zs2", [128, 512], F32)]
    mixp = [mix, pv("mix2", [128, D], BF16)]
    kTnp = [kTn, pv("kTn2", [128, 4, 128], BF16)]
    qgTp = [qgT, pv("qgT2", [128, 4, 128], BF16)]
    kdp = [kd, pv("kd2", [128, 4, 128], BF16)]
    bvp = [bv, pv("bv2", [128, 4, 128], BF16)]
    aqkTp = [aqkT, pv("aqkT2", [128, 4, 128], BF16)]
    nbegp = [nbeg, pv("nbeg2", [128, 4], F32)]
    eglp = [egl, pv("egl2", [128, 4], F32)]
    PTp = [pv("PTp%d" % i, [128, 4, 128], BF16) for i in range(2)]
    if dbg:
        print("priv words phase I:", cur[0])
    cur[0] = 0
    keysT = pv("keysT", [128, 16, 128], BF16)
    ffn_b = pv("ffn_b", [128, D], F32)
    ple_b = pv("ple_b", [128, D], F32)
    fin_b = pv("fin_b", [128, D], F32)
    proj_w = pv("proj_w", [128, 2, D], BF16)
    qT = pv("qT", [128, 16, 128], BF16)
    sc = pv("sc", [128, 16, 128], F32)
    sc2 = sc
    top = pv("top", [128, 16, 16], F32)
    tidx = pv("tidx", [128, 16, 16], U32)
    tidxf = pv("tidxf", [128, 16, 16], F32)
    cand = sc2.rearrange("p (h two) k -> p h (two k)", two=2)
    cand2 = cand
    best = pv("best", [128, 8, 16], F32)
    bpos = pv("bpos", [128, 8, 16], U32)
    bposf = pv("bposf", [128, 8, 16], F32)
    big4 = cand2.rearrange("p h (a b) -> p h a b", b=16)
    asel = pv("asel", [128, 8, 16], F32)
    bsel = pv("bsel", [128, 8, 16], F32)
    isel = pv("isel", [128, 8, 16], F32)
    jsel = pv("jsel", [128, 8, 16], F32)
    ef = pv("ef", [128, 128], F32)
    eidx = pv("eidx", [128, 128], I32)
    gates = pv("gates", [128, 8, 16], F32)
    gz = pv("gz", [128, 8], F32)
    actpre = pv("actpre", [128, 128], F32)
    gl_a = pv("gl_a", [128, 128], F32)
    gl_b = pv("gl_b", [128, 128], F32)
    wts = pv("wts", [128, 128], F32)
    NG = 20
    GS = 4
    gbw = [pv("gb%d" % i, [128, D], F32) for i in range(NG)]
    gb = [w_.bitcast(BF16) for w_ in gbw]
    prod = [pv("prod%d" % i, [128, D], BF16) for i in range(3)]
    dg = [pv("dg%d" % i, [128, 4, 128], BF16) for i in range(2)]
    pt = pv("pt", [128, 256], F32)
    ptbf = pv("ptbf", [128, 256], BF16)
    pT = pv("pT", [128, 2, 128], BF16)
    if dbg:
        print("priv words phase II:", cur[0])
    x1s = [x1, pv("x1b", [128, D], F32)]
    hbfs = [hbf, pv("hbf2", [128, D], BF16)]
    eidxs = [eidx, pv("eidx2", [128, 128], I32)]
    gatess = [gates, pv("gates2", [128, 8, 16], F32)]
    tbf = pv("tbf", [128, D], BF16)
    sig = gbw[1]
    x3 = gbw[2]
    outt = gbw[3]

    def psm(i):
        return ps[P.bankmap[i]]

    def psbf(i):
        return psm(i)[:].bitcast(BF16)

    def ps4(i):
        return psm(i)[:].rearrange("p (h t) -> p h t", h=4)

    def cst(off, n=128):
        return consts[:, off:off + n]

    P.dma("sp", lambda e: e.dma_start(out=consts[:], in_=consts_d), w=["consts"])
    for (tile_, src, key) in ((mix_b, mix_norm_d, "mix_b"),
                              (dtb_b, dtb_d, "dtb_b"), (nea_b, alog_d, "nea_b"),
                              (onorm_b, onorm_d, "onorm_b"), (esink_b, sinks_d, "esink_b")):
        P.dma("sp", lambda e, tile_=tile_, src=src: e.dma_start(out=tile_[:], in_=src.partition_broadcast(128)),
              w=[key])
    P.dma("sp", lambda e: e.dma_start(out=convT[:].rearrange("p c j -> p (c j)"), in_=convT_d), w=["convT"])
    for kc in range(8):
        P.dma("pool", lambda e, kc=kc: e.dma_start(out=arena[:, kc * W_COLS:(kc + 1) * W_COLS],
                                                   in_=win_d[kc * 128:(kc + 1) * 128, :]), w=["arena"])
    for kc in range(8):
        P.dma("pool", lambda e, kc=kc: e.dma_start(out=wout[:, kc, :], in_=wout_d[kc * 128:(kc + 1) * 128, :]),
              w=["wout"])
    P.op("dve", lambda e: e.tensor_copy(out=ident_bf[:], in_=cst(C_ID)), r=["consts"], w=["ident_bf"])
    P.op("dve", lambda e: e.tensor_copy(out=perm_bf[:], in_=cst(C_PERM)), r=["consts"], w=["perm_bf"])
    P.op("dve", lambda e: e.tensor_copy(out=mcur_bf[:], in_=cst(C_MCUR)), r=["consts"], w=["mcur_bf"])
    P.op("dve", lambda e: e.tensor_copy(out=mprev_bf[:], in_=cst(C_MPREV)), r=["consts"], w=["mprev_bf"])
    P.op("dve", lambda e: e.memset(ones_bf[:], 1.0), w=["ones_bf"])
    P.op("dve", lambda e: e.memset(zq[:], 0.0), w=["zq"])
    P.op("dve", lambda e: e.memset(Sst[0][:], 0.0), w=[("S", 0)])
    for i in range(2):
        P.op("dve", lambda e, i=i: e.memset(v1[i][:], 1.0), w=[("v1", i)])
    P.op("act", lambda e: e.activation(out=nea_b[:], in_=nea_b[:], func=AF.Exp), r=["nea_b"], w=["nea_b"])
    P.op("dve", lambda e: e.tensor_scalar(out=nea_b[:], in0=nea_b[:], scalar1=-1.0, scalar2=None, op0=ALU.mult),
         r=["nea_b"], w=["nea_b"])
    P.op("act", lambda e: e.activation(out=esink_b[:], in_=esink_b[:], func=AF.Exp), r=["esink_b"], w=["esink_b"])

    iota16 = consts[:, C_IOTA16:C_IOTA16 + 16]
    iota16x16 = consts[:, C_IOTA16x16:C_IOTA16x16 + 16]

    def rmsnorm(src, src_key, gain_b, gain_key, dst_f32, dst_key, col, dst_bf=None, dst_bf_key=None):
        P.op("act", lambda e: e.activation(out=dst_f32[:], in_=src[:], func=AF.Square,
                                           accum_out=ss[:, col:col + 1]),
             r=[src_key], w=[dst_key, ("ss", col)])
        P.op("act", lambda e: e.activation(out=ss[:, col + 1:col + 2], in_=ss[:, col:col + 1], func=AF.Sqrt,
                                           scale=1.0 / D, bias=EPS),
             r=[("ss", col)], w=[("ss", col + 1)])
        P.op("dve", lambda e: e.reciprocal(out=ss[:, col + 1:col + 2], in_=ss[:, col + 1:col + 2]),
             r=[("ss", col + 1)], w=[("ss", col + 1)])
        P.op("dve", lambda e: e.scalar_tensor_tensor(out=dst_f32[:], in0=src[:], scalar=ss[:, col + 1:col + 2],
                                                     in1=gain_b[:], op0=ALU.mult, op1=ALU.mult),
             r=[src_key, ("ss", col + 1), gain_key], w=[dst_key])
        if dst_bf is not None:
            P.op("act", lambda e: e.copy(out=dst_bf[:], in_=dst_f32[:]), r=[dst_key], w=[dst_bf_key])

    def transpose8(src_bf, src_key, dst, dst_key, nch, bank):
        pb = psbf(bank)
        for c in range(nch):
            P.op("pe", lambda e, c=c: e.transpose(out=pb[:, c * 128:(c + 1) * 128],
                                                  in_=src_bf[:, c * 128:(c + 1) * 128], identity=ident_bf[:]),
                 r=(list(src_key) if isinstance(src_key, list) else [src_key]) + ["ident_bf"], w=[("ps", bank)])
        P.op("dve", lambda e: e.tensor_copy(out=dst[:].rearrange("p c t -> p (c t)"),
                                            in_=pb[:, 0:nch * 128]),
             r=[("ps", bank)], w=[dst_key])

    def mm(out, lhsT, rhs, start, stop, r, bank):
        P.op("pe", lambda e: e.matmul(out=out, lhsT=lhsT, rhs=rhs, start=start, stop=stop),
             r=r, w=[("ps", bank)])

    def dve(fn, r, w):
        P.op("dve", fn, r=r, w=w)

    def ck(n):
        if cut == n:
            raise _Cut()

    def act(fn, r, w):
        P.op("act", fn, r=r, w=w)

    def proj_fm(col, bank, slot):
        for kc in range(8):
            mm(psm(bank)[:, slot * 128:(slot + 1) * 128], win[:, kc, col:col + 128], hT[:, kc, :],
               kc == 0, kc == 7, ["arena", "hT"], bank)

    def frontA(ti):
        P.bankmap[:] = [0, 1, 2, 0, 1, 0, 1, 2]
        t0 = ti * 128
        par = ti % 2
        x1, zs, mix = x1I[par], zsp[par], mixp[par]
        kTn, qgT, kd, bv, aqkT = kTnp[par], qgTp[par], kdp[par], bvp[par], aqkTp[par]
        nbeg, egl = nbegp[par], eglp[par]
        for cc in range(256 // NT):
            ci = ti * (256 // NT) + cc
            rows = slice((ci // 2) * 128, (ci // 2) * 128 + 128)
            cols = slice((ci % 2) * D, (ci % 2) * D + D)
            cb = ci % 2
            P.dma("pool", lambda e: e.dma_start(out=cvb[cb][:], in_=uv_d[rows, cols]), w=[("cvb", cb)])
            P.dma("pool", lambda e: e.dma_start(out=uvb_d[rows, cols], in_=cvb[cb][:]), r=[("cvb", cb)],
                  w=[("uvb", ci)])
        P.dma("sp", lambda e, t0=t0: e.dma_start(out=x1[:], in_=x_d[t0:t0 + 128, :]), w=[("x1", par)])
        P.dma("sp", lambda e, t0=t0: e.dma_start(out=posi[:], in_=pos_d[t0:t0 + 128].partition_broadcast(128)),
              w=["posi"])
        rmsnorm(x1, ("x1", par), mix_b, "mix_b", hbf, "hbf", 0)
        transpose8(hbf, "hbf", hT, "hT", 8, 0)

        for grp in range(3):
            bank = 1 + grp
            for s4 in range(4):
                proj_fm((grp * 4 + s4) * 128, bank, s4)
            act(lambda e, grp=grp, bank=bank: e.copy(out=zq[:, grp * 4:(grp + 1) * 4, 3:131], in_=ps4(bank)),
                [("ps", bank)], ["zq"])
        for kc in range(8):
            mm(psm(6)[:, 0:512], hT[:, kc, :], win[:, kc, W_Z:W_Z + 512], kc == 0, kc == 7, ["arena", "hT"], 6)
        for kc in range(8):
            mm(psm(7)[:, 0:8], hT[:, kc, :], win[:, kc, W_B:W_B + 8], kc == 0, kc == 7, ["arena", "hT"], 7)
        act(lambda e: e.activation(out=zs[:], in_=psm(6)[:], func=AF.Silu), [("ps", 6)], [("zs", par)])
        act(lambda e: e.activation(out=beta[:], in_=psm(7)[:, 0:4], func=AF.Sigmoid), [("ps", 7)], ["beta"])
        dve(lambda e: e.tensor_tensor(out=g8[:, 4:8], in0=dtb_b[:], in1=psm(7)[:, 4:8], op=ALU.add),
            [("ps", 7), "dtb_b"], ["g8a"])
        act(lambda e: e.activation(out=g8[:, 8:12], in_=g8[:, 4:8], func=AF.Abs), ["g8a"], ["g8b"])
        act(lambda e: e.activation(out=g8[:, 12:16], in_=g8[:, 8:12], func=AF.Exp, scale=-1.0), ["g8b"], ["g8c"])
        act(lambda e: e.activation(out=g8[:, 12:16], in_=g8[:, 12:16], func=AF.Ln, bias=1.0), ["g8c"], ["g8c"])
        act(lambda e: e.activation(out=g8[:, 20:24], in_=beta[:], func=AF.Ln), ["beta"], ["g8e"])
        dve(lambda e: e.scalar_tensor_tensor(out=g8[:, 16:20], in0=g8[:, 4:8], scalar=0.0, in1=g8[:, 12:16],
                                             op0=ALU.max, op1=ALU.add), ["g8a", "g8c"], ["g8d"])
        dve(lambda e: e.tensor_tensor(out=g8[:, 16:20], in0=g8[:, 16:20], in1=nea_b[:], op=ALU.mult),
            ["g8d", "nea_b"], ["g8d"])
        mm(psm(7)[:, 256:260], cst(C_TRI), g8[:, 16:20], True, True, ["consts", "g8d"], 7)
        dve(lambda e: e.tensor_copy(out=q8[:, 0:4], in_=psm(7)[:, 256:260]), [("ps", 7)], ["q8a"])
        dve(lambda e: e.tensor_tensor(out=q8[:, 4:8], in0=q8[:, 0:4], in1=g8[:, 20:24], op=ALU.add),
            ["q8a", "g8e"], ["q8b"])
        dve(lambda e: e.tensor_copy(out=qrep[:], in_=q8[:].unsqueeze(2).to_broadcast([128, 8, 128])),
            ["q8a", "q8b"], ["qrep"])
        for h in range(4):
            mm(psm(6)[:, h * 128:(h + 1) * 128], qrep[:, h, :], cst(C_ID), True, True, ["qrep", "consts", ("zs", par)], 6)
        for h in range(4):
            mm(psm(7)[:, h * 128:(h + 1) * 128], qrep[:, 4 + h, :], cst(C_ID), True, True,
               ["qrep", "consts", "q8a", ("v1", par)], 7)
        gl_last = ps4(6)[:, :, 127]
        act(lambda e: e.activation(out=egl[:], in_=gl_last, func=AF.Exp), [("ps", 6)], [("egl", par)])
        dve(lambda e: e.tensor_tensor(out=kds[:], in0=q8[:, 0:4], in1=gl_last, op=ALU.subtract),
            [("ps", 6), "q8a"], ["kds"])
        act(lambda e: e.activation(out=kds[:], in_=kds[:], func=AF.Exp, scale=-1.0), ["kds"], ["kds"])
        act(lambda e: e.activation(out=nbeg[:], in_=q8[:, 0:4], func=AF.Exp), ["q8a"], [("nbeg", par)])
        dve(lambda e: e.scalar_tensor_tensor(out=nbeg[:], in0=nbeg[:], scalar=-1.0, in1=beta[:],
                                             op0=ALU.mult, op1=ALU.mult), [("nbeg", par), "beta"], [("nbeg", par)])
        act(lambda e: e.activation(out=egcrow[:], in_=ps4(6), func=AF.Exp), [("ps", 6)], ["egcrow"])
        dve(lambda e: e.tensor_scalar(out=nq8[:], in0=q8[:, 0:4], scalar1=-1.0, scalar2=None, op0=ALU.mult),
            ["q8a"], ["nq8"])
        for h in range(4):
            dve(lambda e, h=h: e.scalar_tensor_tensor(out=tmpI[:, h, :], in0=cst(C_NEGI),
                                                      scalar=nq8[:, h:h + 1], in1=psm(6)[:, h * 128:(h + 1) * 128],
                                                      op0=ALU.add, op1=ALU.add),
                [("ps", 6), "nq8", "consts"], [("tmpI", h)])
            dve(lambda e, h=h: e.scalar_tensor_tensor(out=tmpS[:, h, :], in0=cst(C_NEGS),
                                                      scalar=nq8[:, h:h + 1], in1=psm(7)[:, h * 128:(h + 1) * 128],
                                                      op0=ALU.add, op1=ALU.add),
                [("ps", 7), "nq8", "consts"], [("tmpS", h)])
        act(lambda e: e.activation(out=DTI[:], in_=tmpI[:], func=AF.Exp), [("tmpI", h) for h in range(4)], ["DTI"])
        act(lambda e: e.activation(out=EB[:], in_=tmpS[:], func=AF.Exp), [("tmpS", h) for h in range(4)], ["EB"])

        for ch in range(12):
            dve(lambda e, ch=ch: e.tensor_scalar(out=cacc[:, ch, :], in0=zq[:, ch, 3:131],
                                                 scalar1=convT[:, ch, 3:4], scalar2=None, op0=ALU.mult),
                ["zq", "convT"], [("cacc", ch)])
            for j in range(3):
                dve(lambda e, ch=ch, j=j: e.scalar_tensor_tensor(
                    out=cacc[:, ch, :], in0=zq[:, ch, j:j + 128], scalar=convT[:, ch, j:j + 1],
                    in1=cacc[:, ch, :], op0=ALU.mult, op1=ALU.add),
                    ["zq", "convT", ("cacc", ch)], [("cacc", ch)])
        dve(lambda e: e.tensor_copy(out=zq[:, :, 0:3], in_=zq[:, :, 128:131]), ["zq"], ["zq"])
        cacc_all = [("cacc", ch) for ch in range(12)]
        act(lambda e: e.activation(out=qkvc[:], in_=cacc[:], func=AF.Silu), cacc_all, ["qkvc"])
        act(lambda e: e.activation(out=sqbf[:], in_=qkvc[:, 0:8, :], func=AF.Square), ["qkvc"], ["sqbf"])
        for half in range(2):
            bank = 1 + half
            for s4 in range(4):
                mm(psm(bank)[:, s4 * 128:(s4 + 1) * 128], ones_bf[:], sqbf[:, half * 4 + s4, :], True, True,
                   ["ones_bf", "sqbf"], bank)
        act(lambda e: e.activation(out=rs8[:, 0:4, :], in_=ps4(1), func=AF.Sqrt, scale=128.0, bias=128.0 * EPS),
            [("ps", 1)], [("rs8", 0)])
        act(lambda e: e.activation(out=rs8[:, 4:8, :], in_=ps4(2), func=AF.Sqrt, scale=1.0, bias=EPS),
            [("ps", 2)], [("rs8", 1)])
        dve(lambda e: e.reciprocal(out=rs8[:], in_=rs8[:]), [("rs8", 0), ("rs8", 1)], ["rs8"])
        dve(lambda e: e.tensor_tensor(out=qTn[:], in0=qkvc[:, 0:4, :], in1=rs8[:, 0:4, :], op=ALU.mult),
            ["qkvc", "rs8"], ["qTn"])
        dve(lambda e: e.tensor_tensor(out=kTn[:], in0=qkvc[:, 4:8, :], in1=rs8[:, 4:8, :], op=ALU.mult),
            ["qkvc", "rs8"], [("kTn", par)])
        act(lambda e: e.copy(out=vTb[:], in_=qkvc[:, 8:12, :]), ["qkvc"], ["vTb"])
        dve(lambda e: e.tensor_tensor(out=qgT[:], in0=qTn[:], in1=egcrow[:], op=ALU.mult),
            ["qTn", "egcrow"], [("qgT", par)])
        pb1 = psbf(1)
        for h in range(4):
            P.op("pe", lambda e, h=h: e.transpose(out=pb1[:, h * 128:(h + 1) * 128], in_=vTb[:, h, :],
                                                  identity=ident_bf[:]),
                 r=["vTb", "ident_bf", ("rs8", 0)], w=[("ps", 1)])
        dve(lambda e: e.tensor_copy(out=tpb[:].rearrange("p h d -> p (h d)"), in_=pb1[:, 0:512]),
            [("ps", 1)], ["tpb"])
        dve(lambda e: e.tensor_tensor(out=bv[:], in0=tpb[:],
                                      in1=beta[:].unsqueeze(2).to_broadcast([128, 4, 128]), op=ALU.mult),
            ["tpb", "beta"], [("bv", par)])
        pb2 = psbf(2)
        for h in range(4):
            P.op("pe", lambda e, h=h: e.transpose(out=pb2[:, h * 128:(h + 1) * 128], in_=kTn[:, h, :],
                                                  identity=ident_bf[:]),
                 r=[("kTn", par), "ident_bf", ("rs8", 1)], w=[("ps", 2)])
        dve(lambda e: e.tensor_copy(out=tpb2[:].rearrange("p h d -> p (h d)"), in_=pb2[:, 0:512]),
            [("ps", 2)], ["tpb2"])
        dve(lambda e: e.tensor_tensor(out=kd[:], in0=tpb2[:],
                                      in1=kds[:].unsqueeze(2).to_broadcast([128, 4, 128]), op=ALU.mult),
            ["tpb2", "kds"], [("kd", par)])
        for h in range(4):
            mm(psm(3)[:, h * 128:(h + 1) * 128], kTn[:, h, :], kTn[:, h, :], True, True, [("kTn", par), "zq"], 3)
        for h in range(4):
            mm(psm(4)[:, h * 128:(h + 1) * 128], kTn[:, h, :], qTn[:, h, :], True, True, [("kTn", par), "qTn", "qraw"], 4)
        dve(lambda e: e.tensor_tensor(out=Wm[0][:], in0=EB[:], in1=ps4(3), op=ALU.mult),
            [("ps", 3), "EB"], [("Wm", 0)])
        dve(lambda e: e.tensor_tensor(out=aqkT[:], in0=DTI[:], in1=ps4(4), op=ALU.mult),
            [("ps", 4), "DTI"], [("aqkT", par)])
        pb3 = psbf(3)
        for h in range(4):
            P.op("pe", lambda e, h=h: e.transpose(out=pb3[:, h * 128:(h + 1) * 128], in_=Wm[0][:, h, :],
                                                  identity=ident_bf[:]),
                 r=[("Wm", 0), "ident_bf"], w=[("ps", 3)])
        dve(lambda e: e.tensor_copy(out=Vm[0][:].rearrange("p h d -> p (h d)"), in_=pb3[:, 0:512]),
            [("ps", 3)], [("Vm", 0)])
        dve(lambda e: e.scalar_tensor_tensor(out=Pm[0][:], in0=Wm[0][:], scalar=-1.0,
                                             in1=ident_bf[:].unsqueeze(1).to_broadcast([128, 4, 128]),
                                             op0=ALU.mult, op1=ALU.add), [("Wm", 0), "ident_bf"], [("Pm", 0)])
        cw, cp = 0, 0
        for m in range(6):
            nw = 1 - cw
            last = (m == 5)
            if not last:
                for h in range(4):
                    mm(psm(1)[:, h * 128:(h + 1) * 128], Vm[cw][:, h, :], Wm[cw][:, h, :], True, True,
                       [("Vm", cw), ("Wm", cw), ("bv", par)], 1)
            for h in range(4):
                mm(psm(2)[:, h * 128:(h + 1) * 128], Wm[cw][:, h, :], Vm[cw][:, h, :], True, True,
                   [("Vm", cw), ("Wm", cw), ("kd", par)], 2)
            if not last:
                act(lambda e, nw=nw: e.copy(out=Wm[nw][:], in_=ps4(1)), [("ps", 1)], [("Wm", nw)])
            act(lambda e, nw=nw: e.copy(out=Vm[nw][:], in_=ps4(2)), [("ps", 2)], [("Vm", nw)])
            dve(lambda e, nw=nw: e.tensor_tensor(out=VI[:], in0=Vm[nw][:],
                                                 in1=ident_bf[:].unsqueeze(1).to_broadcast([128, 4, 128]),
                                                 op=ALU.add), [("Vm", nw), "ident_bf"], ["VI"])
            for h in range(4):
                mm(psm(3)[:, h * 128:(h + 1) * 128], VI[:, h, :], Pm[cp][:, h, :], True, True,
                   ["VI", ("Pm", cp)], 3)
            dve(lambda e, cp=cp: e.tensor_copy(out=Pm[1 - cp][:], in_=ps4(3)), [("ps", 3)], [("Pm", 1 - cp)])
            cw, cp = nw, 1 - cp
        dve(lambda e: e.tensor_copy(out=PTp[par][:], in_=Pm[cp][:]), [("Pm", cp)], [("PT", par)])

    def swa(ti):
        P.bankmap[:] = [3, 3, 3, 3, 3, 4, 4, 4]
        t0 = ti * 128
        par = ti % 2
        x1, zs, mix = x1I[par], zsp[par], mixp[par]
        kTn, qgT, kd, bv, aqkT = kTnp[par], qgTp[par], kdp[par], bvp[par], aqkTp[par]
        nbeg, egl = nbegp[par], eglp[par]
        for s4 in range(4):
            proj_fm(W_SQ + s4 * 128, 4, s4)
        proj_fm(W_SK, 5, 0)
        act(lambda e: e.copy(out=qraw[:, 0:4, :], in_=ps4(4)), [("ps", 4)], ["qraw"])
        act(lambda e: e.copy(out=qraw[:, 4, :], in_=psm(5)[:, 0:128]), [("ps", 5)], ["qraw"])
        for kc in range(8):
            mm(psm(7)[:, 128:256], hT[:, kc, :], win[:, kc, W_SV:W_SV + 128], kc == 0, kc == 7, ["arena", "hT"], 7)
        dve(lambda e, par=par: e.tensor_copy(out=v1[par][:, :, 0:64],
                                             in_=psm(7)[:, 128:256].rearrange("p (g d) -> p g d", g=2)),
            [("ps", 7)], [("v1", par)])
        dve(lambda e: e.tensor_copy(out=ang[:, 0, :], in_=posi[:]), ["posi"], ["ang0"])
        dve(lambda e: e.tensor_scalar(out=ang[:, 0, :], in0=ang[:, 0, :], scalar1=cst(C_INVF, 1), scalar2=None,
                                      op0=ALU.mult), ["ang0", "consts"], ["ang0"])
        dve(lambda e: e.tensor_scalar(out=ang[:, 1, :], in0=ang[:, 0, :], scalar1=PI / 2, scalar2=None,
                                      op0=ALU.add), ["ang0"], ["ang1"])
        dve(lambda e: e.tensor_scalar(out=angk[:], in0=ang[:], scalar1=1.0 / TWO_PI, scalar2=None, op0=ALU.mult),
            ["ang0", "ang1"], ["angk"])
        dve(lambda e: e.tensor_copy(out=angf[:], in_=angk[:]), ["angk"], ["angf"])
        dve(lambda e: e.scalar_tensor_tensor(out=ang[:], in0=angf[:], scalar=-TWO_PI, in1=ang[:],
                                             op0=ALU.mult, op1=ALU.add), ["angf", "ang0", "ang1"], ["ang"])
        dve(lambda e: e.tensor_single_scalar(out=angf[:], in_=ang[:], scalar=PI, op=ALU.is_gt), ["ang"], ["angf"])
        dve(lambda e: e.scalar_tensor_tensor(out=ang[:], in0=angf[:], scalar=-TWO_PI, in1=ang[:],
                                             op0=ALU.mult, op1=ALU.add), ["angf", "ang"], ["ang"])
        act(lambda e: e.activation(out=sincos[:], in_=ang[:], func=AF.Sin), ["ang"], ["sincos"])
        act(lambda e: e.copy(out=qrb[:], in_=qraw[:]), ["qraw"], ["qrb"])
        for c in range(4):
            mm(psm(5)[:, c * 128:(c + 1) * 128], perm_bf[:], qrb[:, c, :], True, True, ["perm_bf", "qrb"], 5)
        mm(psm(1)[:, 0:128], perm_bf[:], qrb[:, 4, :], True, True, ["perm_bf", "qrb", ("rr", 0), ("rr", 1), ("rr", 2), ("rr", 3)], 1)
        dve(lambda e: e.tensor_tensor(out=qtmp[:, 0:4, :], in0=sincos[:, 0:1, :].to_broadcast([128, 4, 128]),
                                      in1=ps4(5), op=ALU.mult),
            [("ps", 5), "sincos"], ["qtmp"])
        dve(lambda e: e.tensor_tensor(out=qtmp[:, 4, :], in0=sincos[:, 0, :], in1=psm(1)[:, 0:128], op=ALU.mult),
            [("ps", 1), "sincos"], ["qtmp"])
        dve(lambda e: e.tensor_tensor(out=qraw[:], in0=qraw[:],
                                      in1=sincos[:, 1:2, :].to_broadcast([128, 5, 128]), op=ALU.mult),
            ["qraw", "sincos", "qrb"], ["qraw"])
        dve(lambda e: e.tensor_tensor(out=qrot[:, 0:4, :], in0=qraw[:, 0:4, :], in1=qtmp[:, 0:4, :], op=ALU.add),
            ["qraw", "qtmp"], ["qrot"])
        dve(lambda e, par=par: e.tensor_tensor(out=krot[par][:], in0=qraw[:, 4, :], in1=qtmp[:, 4, :], op=ALU.add),
            ["qraw", "qtmp"], [("krot", par)])
        for g in range(2):
            lo, hi = 64 * g, 64 * g + 64
            mm(psm(1 + g)[:], krot[par][lo:hi, :], qrot[lo:hi, 0:4, :].rearrange("p c t -> p (c t)"), True, True,
               [("krot", par), "qrot", "vnew", ("qtmp")], 1 + g)
            act(lambda e, g=g: e.activation(out=Ecur[g][:], in_=psm(1 + g)[:], func=AF.Exp, scale=0.125),
                [("ps", 1 + g)], [("Ecur", g)])
            dve(lambda e, g=g: e.tensor_tensor(out=Ecur[g][:].rearrange("p (c t) -> p c t", c=4),
                                               in0=Ecur[g][:].rearrange("p (c t) -> p c t", c=4),
                                               in1=mcur_bf[:].unsqueeze(1).to_broadcast([128, 4, 128]), op=ALU.mult),
                [("Ecur", g), "mcur_bf"], [("Ecur", g)])
            if ti > 0:
                mm(psm(5 + g)[:], krot[1 - par][lo:hi, :], qrot[lo:hi, 0:4, :].rearrange("p c t -> p (c t)"),
                   True, True, [("krot", 1 - par), "qrot", ("S", 1 - par), "qtmp"], 5 + g)
                act(lambda e, g=g: e.activation(out=Eprev[g][:], in_=psm(5 + g)[:], func=AF.Exp, scale=0.125),
                    [("ps", 5 + g)], [("Eprev", g)])
                dve(lambda e, g=g: e.tensor_tensor(out=Eprev[g][:].rearrange("p (c t) -> p c t", c=4),
                                                   in0=Eprev[g][:].rearrange("p (c t) -> p c t", c=4),
                                                   in1=mprev_bf[:].unsqueeze(1).to_broadcast([128, 4, 128]),
                                                   op=ALU.mult),
                    [("Eprev", g), "mprev_bf"], [("Eprev", g)])
        for g in range(2):
            bank = 3 if g == 0 else 7
            for c in range(4):
                dst = psm(bank)[:, c * 65:(c + 1) * 65]
                deps = [("Ecur", g), ("v1", par), "ot", "tmpS_all"]
                if ti > 0:
                    mm(dst, Eprev[g][:, c * 128:(c + 1) * 128], v1[1 - par][:, g, :], True, False,
                       deps + [("Eprev", g), ("v1", 1 - par)], bank)
                    mm(dst, Ecur[g][:, c * 128:(c + 1) * 128], v1[par][:, g, :], False, True, deps, bank)
                else:
                    mm(dst, Ecur[g][:, c * 128:(c + 1) * 128], v1[par][:, g, :], True, True, deps, bank)
            pv = psm(bank)[:, 0:260].rearrange("p (c d) -> p c d", c=4)
            dve(lambda e, g=g, pv=pv: e.tensor_tensor(out=den[:, g * 4:(g + 1) * 4], in0=esink_b[:, g * 4:(g + 1) * 4],
                                                      in1=pv[:, :, 64], op=ALU.add),
                [("ps", bank), "esink_b"], [("den", g)])
            dve(lambda e, g=g: e.reciprocal(out=den[:, g * 4:(g + 1) * 4], in_=den[:, g * 4:(g + 1) * 4]),
                [("den", g)], [("den", g)])
            dve(lambda e, g=g, pv=pv: e.tensor_tensor(
                out=mix[:, 512 + g * 256:512 + (g + 1) * 256].rearrange("p (c d) -> p c d", c=4),
                in0=den[:, g * 4:(g + 1) * 4].unsqueeze(2).to_broadcast([128, 4, 64]), in1=pv[:, :, 0:64],
                op=ALU.mult), [("ps", bank), ("den", g)], [("mix", par, 1 + g)])

    def back(ti):
        P.bankmap[:] = [6, 5, 6, 3, 7, 5, 6, 7]
        t0 = ti * 128
        par = ti % 2
        x1, zs, mix = x1I[par], zsp[par], mixp[par]
        kTn, qgT, kd, bv, aqkT = kTnp[par], qgTp[par], kdp[par], bvp[par], aqkTp[par]
        nbeg, egl = nbegp[par], eglp[par]
        PT = PTp[par]
        PTk = ("PT", par)
        So, Sn = Sst[par], Sst[1 - par]
        for h in range(4):
            mm(psm(1)[:, h * 128:(h + 1) * 128], kTn[:, h, :], So[:, h, :], True, True, [("kTn", par), ("S", par)], 1)
        dve(lambda e: e.tensor_tensor(out=osq[:], in0=nbeg[:].unsqueeze(2).to_broadcast([128, 4, 128]),
                                      in1=ps4(1), op=ALU.mult),
            [("ps", 1), ("nbeg", par)], ["osq"])
        dve(lambda e: e.tensor_tensor(out=rr[:], in0=osq[:], in1=bv[:], op=ALU.add),
            ["osq", ("bv", par)], [("rr", h) for h in range(4)])
        for h in range(4):
            mm(psm(2)[:, h * 128:(h + 1) * 128], PT[:, h, :], rr[:, h, :], True, True, [PTk, ("rr", h)], 2)
        act(lambda e: e.copy(out=vnew[:], in_=ps4(2)), [("ps", 2)], ["vnew"])
        for h in range(4):
            mm(psm(4)[:, h * 128:(h + 1) * 128], qgT[:, h, :], So[:, h, :], True, False, [("qgT", par), ("S", par), ("aqkT", par)], 4)
            mm(psm(4)[:, h * 128:(h + 1) * 128], aqkT[:, h, :], vnew[:, h, :], False, True, [("aqkT", par), "vnew"], 4)
        for h in range(4):
            mm(psm(5)[:, h * 128:(h + 1) * 128], kd[:, h, :], vnew[:, h, :], True, True, [("kd", par), "vnew", "qraw"], 5)
        for h in range(4):
            dve(lambda e, h=h: e.scalar_tensor_tensor(out=Sn[:, h, :], in0=So[:, h, :], scalar=egl[:, h:h + 1],
                                                      in1=psm(5)[:, h * 128:(h + 1) * 128],
                                                      op0=ALU.mult, op1=ALU.add),
                [("S", par), ("egl", par), ("ps", 5)], [("S", 1 - par)])
        act(lambda e: e.activation(out=osq[:], in_=ps4(4), func=AF.Square), [("ps", 4)], ["osq"])
        dve(lambda e: e.tensor_reduce(out=o4[:, 0:4], in_=osq[:], axis=AX.X, op=ALU.add), ["osq"], ["o4"])
        act(lambda e: e.activation(out=o4[:, 4:8], in_=o4[:, 0:4], func=AF.Sqrt, scale=1.0 / 128.0, bias=EPS),
            ["o4"], ["o4b"])
        dve(lambda e: e.reciprocal(out=o4[:, 4:8], in_=o4[:, 4:8]), ["o4b"], ["o4b"])
        dve(lambda e: e.tensor_tensor(out=ot[:], in0=o4[:, 4:8].unsqueeze(2).to_broadcast([128, 4, 128]),
                                      in1=ps4(4), op=ALU.mult),
            [("ps", 4), "o4b"], ["ot"])
        dve(lambda e: e.tensor_tensor(out=ot[:], in0=ot[:],
                                      in1=onorm_b[:].unsqueeze(1).to_broadcast([128, 4, 128]), op=ALU.mult),
            ["ot", "onorm_b"], ["ot"])
        dve(lambda e: e.tensor_tensor(out=mix[:, 0:512], in0=ot[:].rearrange("p h d -> p (h d)"), in1=zs[:],
                                      op=ALU.mult), ["ot", ("zs", par)], [("mix", par, 0)])

        transpose8(mix, [("mix", par, 0), ("mix", par, 1), ("mix", par, 2)], mixT, "mixT", 8, 0)
        for half in range(2):
            bank = 4 + half
            for kc in range(8):
                mm(psm(bank)[:], mixT[:, kc, :], wout[:, kc, half * 512:(half + 1) * 512], kc == 0, kc == 7,
                   ["mixT", "wout", "osq", "ot", "qtmp"], bank)
            dve(lambda e, half=half, bank=bank: e.tensor_tensor(
                out=y[:, half * 512:(half + 1) * 512], in0=x1[:, half * 512:(half + 1) * 512], in1=psm(bank)[:],
                op=ALU.add), [("ps", bank), ("x1", par)], ["y"])
        P.dma("sp", lambda e, t0=t0: e.dma_start(out=x1_d[t0:t0 + 128, :], in_=y[:]),
              r=["y"], w=[("x1_d", ti)])

    def rec(fn, ti):
        P.begin()
        fn(ti)
        return P.end()

    def swa_start(fa):
        idx = max(j for j, it in enumerate(fa) if ("hT" in it[3] or "posi" in it[3]))
        return (idx + 1.0) / len(fa)

    fa = rec(frontA, 0)
    P.merge([fa, rec(swa, 0)], starts=[0.0, swa_start(fa)])
    for ti in range(NT):
        if ti + 1 < NT:
            fa = rec(frontA, ti + 1)
            P.merge([fa, rec(swa, ti + 1), rec(back, ti)], starts=[0.0, swa_start(fa), 0.0])
        else:
            back(ti)
    P.bankmap[:] = list(range(8))


    if phase2:
        P.barrier()
        for (tile_, src, key) in ((ffn_b, ffn_norm_d, "ffn_b"), (ple_b, ple_norm_d, "ple_b"),
                                  (fin_b, final_norm_d, "fin_b")):
            P.dma("sp", lambda e: e.dma_start(out=tile_[:], in_=src.partition_broadcast(128)), w=[key])
        P.dma("pool", lambda e: e.dma_start(out=keysT[:], in_=keysT_d.rearrange("g d n -> d g n")), w=["keysT"])
        for kc in range(2):
            P.dma("pool", lambda e, kc=kc: e.dma_start(out=proj_w[:, kc, :], in_=proj_d[kc * 128:(kc + 1) * 128, :]),
                  w=["proj_w"])
        for kc in range(8):
            P.dma("pool", lambda e, kc=kc: e.dma_start(out=arena[:, kc * 2048:(kc + 1) * 2048],
                                                       in_=wq_d[kc * 128:(kc + 1) * 128, :]), w=["arena"])
        for kc in range(8):
            P.dma("pool", lambda e, kc=kc: e.dma_start(out=arena[:, 16384 + kc * 1024:16384 + (kc + 1) * 1024],
                                                       in_=gate_d[kc * 128:(kc + 1) * 128, :]), w=["arena"])
        def head(ti):
            t0 = ti * 128
            pq = ti % 2
            x1c, hbfc, eidxc, gatesc = x1s[pq], hbfs[pq], eidxs[pq], gatess[pq]
            kx1, khbf, keidx, kgates = ("x1", pq), ("hbfp", pq), ("eidx", pq), ("gates", pq)
            P.dma("sp", lambda e: e.dma_start(out=x1c[:], in_=x1_d[t0:t0 + 128, :]), r=[("x1_d", ti)], w=[kx1])
            rmsnorm(x1c, kx1, ffn_b, "ffn_b", hbfc, khbf, 0)
            transpose8(hbfc, khbf, hT, "hT", 8, 0)
            for gq in range(4):
                bank = 1 + (gq % 2)
                for gi in range(4):
                    g = gq * 4 + gi
                    for kc in range(8):
                        P.op("pe", lambda e, g=g, gi=gi, kc=kc, bank=bank: e.matmul(
                            out=ps[bank][:, gi * 128:(gi + 1) * 128], lhsT=wq[:, kc, g * 128:(g + 1) * 128],
                            rhs=hT[:, kc, :], start=(kc == 0), stop=(kc == 7)),
                            r=["arena", "hT"], w=[("ps", bank)])
                eng = "act" if gq % 2 == 0 else "dve"
                if eng == "act":
                    P.op("act", lambda e, gq=gq, bank=bank: e.copy(
                        out=qT[:, gq * 4:(gq + 1) * 4, :].rearrange("p g t -> p (g t)"), in_=ps[bank][:]),
                        r=[("ps", bank)], w=[("qT", gq)])
                else:
                    P.op("dve", lambda e, gq=gq, bank=bank: e.tensor_copy(
                        out=qT[:, gq * 4:(gq + 1) * 4, :].rearrange("p g t -> p (g t)"), in_=ps[bank][:]),
                        r=[("ps", bank)], w=[("qT", gq)])
            for gq in range(4):
                bank = 1 + (gq % 2)
                for gi in range(4):
                    g = gq * 4 + gi
                    P.op("pe", lambda e, g=g, gi=gi, bank=bank: e.matmul(
                        out=ps[bank][:, gi * 128:(gi + 1) * 128], lhsT=qT[:, g, :], rhs=keysT[:, g, :],
                        start=True, stop=True),
                        r=[("qT", gq), "keysT"], w=[("ps", bank)])
                P.op("act", lambda e, gq=gq, bank=bank: e.copy(
                    out=sc[:, gq * 4:(gq + 1) * 4, :].rearrange("p g n -> p (g n)"), in_=ps[bank][:]),
                    r=[("ps", bank)], w=[("scg", gq * 4 + q_) for q_ in range(4)])
            if False:
                P.dma("sp", lambda e, t0=t0: e.dma_start(out=dbg_d["sc"][t0:t0 + 128, :],
                                                         in_=sc[:].rearrange("p g n -> p (g n)")),
                      r=[("sc", q) for q in range(4)], w=["dbg_sc"])
            for g in range(16):
                Kg = ("scg", g)
                P.op("dve", lambda e: e.max(out=top[:, g, 0:8], in_=sc[:, g, :]), r=[Kg], w=[("top", g, 0)])
                P.op("dve", lambda e: e.max_index(out=tidx[:, g, 0:8], in_max=top[:, g, 0:8], in_values=sc[:, g, :]),
                     r=[Kg, ("top", g, 0)], w=[("tidx", g, 0)])
                P.op("dve", lambda e: e.match_replace(out=sc[:, g, :], in_to_replace=top[:, g, 0:8],
                                                      in_values=sc[:, g, :], imm_value=-1e30),
                     r=[Kg, ("top", g, 0)], w=[Kg])
                P.op("dve", lambda e: e.max(out=top[:, g, 8:16], in_=sc[:, g, :]), r=[Kg], w=[("top", g, 1)])
                P.op("dve", lambda e: e.max_index(out=tidx[:, g, 8:16], in_max=top[:, g, 8:16],
                                                  in_values=sc[:, g, :]),
                     r=[Kg, ("top", g, 1)], w=[("tidx", g, 1)])
            sc_all = [("scg", g) for g in range(16)]
            topk_all = [("top", g, k) for g in range(16) for k in range(2)]
            tidx_all = [("tidx", g, k) for g in range(16) for k in range(2)]
            top4 = top[:].rearrange("p (h two) k -> p h two k", two=2)
            P.op("dve", lambda e: e.tensor_tensor(
                out=cand[:].rearrange("p h (a b) -> p h a b", b=16),
                in0=top4[:, :, 0, :].unsqueeze(3).to_broadcast([128, 8, 16, 16]),
                in1=top4[:, :, 1, :].unsqueeze(2).to_broadcast([128, 8, 16, 16]), op=ALU.add),
                r=topk_all, w=sc_all)
            P.op("dve", lambda e: e.tensor_copy(out=tidxf[:], in_=tidx[:]), r=tidx_all, w=["tidxf"])
            for h in range(8):
                Kc = [("scg", 2 * h), ("scg", 2 * h + 1)]
                P.op("dve", lambda e: e.max(out=best[:, h, 0:8], in_=cand[:, h, :]), r=Kc, w=[("best", h, 0)])
                P.op("dve", lambda e: e.max_index(out=bpos[:, h, 0:8], in_max=best[:, h, 0:8],
                                                  in_values=cand[:, h, :]),
                     r=Kc + [("best", h, 0)], w=[("bpos", h, 0)])
                P.op("dve", lambda e: e.match_replace(out=cand[:, h, :], in_to_replace=best[:, h, 0:8],
                                                      in_values=cand[:, h, :], imm_value=-1e30),
                     r=Kc + [("best", h, 0)], w=Kc)
                P.op("dve", lambda e: e.max(out=best[:, h, 8:16], in_=cand[:, h, :]), r=Kc, w=[("best", h, 1)])
                P.op("dve", lambda e: e.max_index(out=bpos[:, h, 8:16], in_max=best[:, h, 8:16],
                                                  in_values=cand[:, h, :]),
                     r=Kc + [("best", h, 1)], w=[("bpos", h, 1)])
            best_all = [("best", h, k) for h in range(8) for k in range(2)]
            bpos_all = [("bpos", h, k) for h in range(8) for k in range(2)]
            P.op("dve", lambda e: e.tensor_copy(out=bposf[:], in_=bpos[:]), r=bpos_all, w=["bposf"])
            bc_s = lambda t: t[:].unsqueeze(3).to_broadcast([128, 8, 16, 16])
            bc_c = lambda ap: ap.unsqueeze(1).unsqueeze(1).to_broadcast([128, 8, 16, 16])
            P.op("dve", lambda e: e.tensor_tensor(out=big4[:], in0=bc_s(bposf), in1=bc_c(iota16x16), op=ALU.is_ge),
                 r=["bposf", "consts"], w=sc_all)
            P.op("dve", lambda e: e.tensor_reduce(out=asel[:], in_=big4[:], axis=AX.X, op=ALU.add),
                 r=sc_all, w=["asel"])
            P.op("dve", lambda e: e.tensor_scalar(out=asel[:], in0=asel[:], scalar1=-1.0, scalar2=None, op0=ALU.add),
                 r=["asel"], w=["asel"])
            P.op("dve", lambda e: e.scalar_tensor_tensor(out=bsel[:], in0=asel[:], scalar=-16.0, in1=bposf[:],
                                                         op0=ALU.mult, op1=ALU.add),
                 r=["asel", "bposf"], w=["bsel"])
            tf4 = tidxf[:].rearrange("p (h two) k -> p h two k", two=2)
            for (sel, half, dst, dkey) in ((asel, 0, isel, "isel"), (bsel, 1, jsel, "jsel")):
                skey = "asel" if half == 0 else "bsel"
                P.op("dve", lambda e, sel=sel: e.tensor_tensor(out=big4[:], in0=bc_s(sel), in1=bc_c(iota16),
                                                               op=ALU.is_equal),
                     r=[skey, "consts"], w=sc_all)
                P.op("dve", lambda e, half=half: e.tensor_tensor(
                    out=big4[:], in0=big4[:],
                    in1=tf4[:, :, half, :].unsqueeze(2).to_broadcast([128, 8, 16, 16]), op=ALU.mult),
                    r=sc_all + ["tidxf"], w=sc_all)
                P.op("dve", lambda e, dst=dst: e.tensor_reduce(out=dst[:], in_=big4[:], axis=AX.X, op=ALU.add),
                     r=sc_all, w=[dkey])
            P.op("dve", lambda e: e.scalar_tensor_tensor(
                out=ef[:], in0=isel[:].rearrange("p h s -> p (h s)"), scalar=128.0,
                in1=jsel[:].rearrange("p h s -> p (h s)"), op0=ALU.mult, op1=ALU.add),
                r=["isel", "jsel"], w=["ef"])
            P.op("dve", lambda e: e.tensor_copy(out=eidxc[:], in_=ef[:]), r=["ef"], w=[keidx])
            P.op("dve", lambda e: e.tensor_tensor(out=gatesc[:], in0=best[:],
                                                  in1=best[:, :, 0:1].to_broadcast([128, 8, 16]), op=ALU.subtract),
                 r=best_all, w=[kgates])
            P.op("act", lambda e: e.activation(out=gatesc[:], in_=gatesc[:], func=AF.Exp), r=[kgates], w=[kgates])
            P.op("dve", lambda e: e.tensor_reduce(out=gz[:], in_=gatesc[:], axis=AX.X, op=ALU.add),
                 r=[kgates], w=["gz"])
            P.op("dve", lambda e: e.reciprocal(out=gz[:], in_=gz[:]), r=["gz"], w=["gz"])
            P.op("dve", lambda e: e.tensor_tensor(out=gatesc[:], in0=gatesc[:],
                                                  in1=gz[:].unsqueeze(2).to_broadcast([128, 8, 16]), op=ALU.mult),
                 r=[kgates, "gz"], w=[kgates])
            if False:
                P.dma("sp", lambda e, t0=t0: e.dma_start(out=dbg_d["eidx"][t0:t0 + 128, :], in_=eidx[:]),
                      r=["eidx"], w=["dbg_eidx"])
                P.dma("sp", lambda e, t0=t0: e.dma_start(out=dbg_d["gates"][t0:t0 + 128, :],
                                                         in_=gatesc[:].rearrange("p h s -> p (h s)")),
                      r=["gates"], w=["dbg_gates"])
        def gather(ti):
            pq = ti % 2
            accb = 5 if pq == 0 else 3
            x1c, hbfc, eidxc, gatesc = x1s[pq], hbfs[pq], eidxs[pq], gatess[pq]
            kx1, khbf, keidx, kgates = ("x1", pq), ("hbfp", pq), ("eidx", pq), ("gates", pq)
            def fin(g):
                gs = slice(g * GS, g * GS + GS)
                dk = (ti * (128 // GS) + g) % 2
                apk = [("actpre", q) for q in range(g * GS, g * GS + GS)]
                ga, gbb = gl_a[:, gs], gl_b[:, gs]
                gflat = gatesc[:].rearrange("p h s -> p (h s)")
                P.op("dve", lambda e: e.tensor_tensor(out=ga, in0=actpre[:, gs], in1=actpre[:, gs], op=ALU.mult),
                     r=apk, w=[("gla", g)])
                P.op("dve", lambda e: e.tensor_scalar(out=ga, in0=ga, scalar1=0.044715, scalar2=1.0,
                                                      op0=ALU.mult, op1=ALU.add), r=[("gla", g)], w=[("gla", g)])
                P.op("dve", lambda e: e.tensor_tensor(out=ga, in0=ga, in1=actpre[:, gs], op=ALU.mult),
                     r=[("gla", g)] + apk, w=[("gla", g)])
                P.op("act", lambda e: e.activation(out=gbb, in_=ga, func=AF.Sigmoid, scale=1.5957691216057308),
                     r=[("gla", g)], w=[("glb", g)])
                P.op("dve", lambda e: e.tensor_tensor(out=wts[:, gs], in0=actpre[:, gs], in1=gflat[:, gs],
                                                      op=ALU.mult), r=apk + [kgates], w=[("wts", g)])
                P.op("dve", lambda e: e.tensor_tensor(out=wts[:, gs], in0=wts[:, gs], in1=gbb, op=ALU.mult),
                     r=[("glb", g), ("wts", g)], w=[("wts", g)])
                for j in range(GS):
                    P.op("act", lambda e: e.activation(out=dg[dk][:, j, :], in_=ident_bf[:], func=AF.Copy,
                                                       scale=wts[:, g * GS + j:g * GS + j + 1]),
                         r=[("wts", g), "ident_bf"], w=[("dg", dk)])
                for j in range(GS):
                    sj = g * GS + j
                    bj = (ti * 128 + sj) % NG
                    for half in range(2):
                        mm(ps[accb + half][:], dg[dk][:, j, :], gb[bj][:, D + half * 512:D + (half + 1) * 512],
                           sj == 0, sj == 127, [("dg", dk), ("gb", bj)], accb + half)

            for s in range(128):
                gidx = ti * 128 + s
                b = gidx % NG
                pb_ = gidx % 3
                P.dma("pool", lambda e: e.indirect_dma_start(
                    out=gb[b], out_offset=None, in_=uvb_d,
                    in_offset=bass.IndirectOffsetOnAxis(ap=eidxc[:, s:s + 1], axis=0)),
                    r=[keidx], w=[("gb", b)])
                if s % 4 == 3:
                    P.op("dve", lambda e: e.scalar_tensor_tensor(
                        out=prod[pb_][:], in0=gb[b][:, 0:D], scalar=1.0, in1=hbfc[:], op0=ALU.mult, op1=ALU.mult,
                        accum_out=actpre[:, s:s + 1]),
                        r=[("gb", b), khbf], w=[("prod", pb_), ("actpre", s)])
                else:
                    P.op("dve", lambda e: e.tensor_tensor(out=prod[pb_][:], in0=gb[b][:, 0:D], in1=hbfc[:],
                                                          op=ALU.mult),
                         r=[("gb", b), khbf], w=[("prod", pb_)])
                    P.op("act", lambda e: e.activation(out=prod[pb_][:], in_=prod[pb_][:], func=AF.Copy,
                                                       accum_out=actpre[:, s:s + 1]),
                         r=[("prod", pb_)], w=[("prod", pb_), ("actpre", s)])
                if s % GS == GS - 1:
                    if s // GS >= 1:
                        fin(s // GS - 1)
            fin(128 // GS - 1)
            for half in range(2):
                P.op("dve", lambda e: e.tensor_tensor(out=y[:, half * 512:(half + 1) * 512],
                                                      in0=x1c[:, half * 512:(half + 1) * 512], in1=ps[accb + half][:],
                                                      op=ALU.add), r=[kx1, ("ps", accb + half)], w=["y"])
            if False:
                P.dma("sp", lambda e, t0=t0: e.dma_start(out=dbg_d["x2"][t0:t0 + 128, :], in_=y[:]),
                      r=["y"], w=["dbg_x2"])
        def tail(ti):
            t0 = ti * 128
            rmsnorm(y, "y", ple_b, "ple_b", tbf, "tbf", 2)
            transpose8(tbf, "tbf", hT, "hT", 8, 0)
            for half in range(2):
                bank = 1 + half
                for kc in range(8):
                    P.op("pe", lambda e, half=half, kc=kc, bank=bank: e.matmul(
                        out=ps[bank][:], lhsT=hT[:, kc, :], rhs=gate_w[:, kc, half * 512:(half + 1) * 512],
                        start=(kc == 0), stop=(kc == 7)), r=["hT", "arena"], w=[("ps", bank)])
                P.op("act", lambda e, half=half, bank=bank: e.activation(
                    out=sig[:, half * 512:(half + 1) * 512], in_=ps[bank][:], func=AF.Sigmoid),
                    r=[("ps", bank)], w=[("gb", 1)])
            P.dma("sp", lambda e, t0=t0: e.dma_start(out=pt[:], in_=p_d[t0:t0 + 128, :]), w=["pt"])
            P.op("act", lambda e: e.copy(out=ptbf[:], in_=pt[:]), r=["pt"], w=["ptbf"])
            transpose8(ptbf, "ptbf", pT, "pT", 2, 0)
            for half in range(2):
                bank = 1 + half
                for kc in range(2):
                    P.op("pe", lambda e, half=half, kc=kc, bank=bank: e.matmul(
                        out=ps[bank][:], lhsT=pT[:, kc, :], rhs=proj_w[:, kc, half * 512:(half + 1) * 512],
                        start=(kc == 0), stop=(kc == 1)), r=["pT", "proj_w"], w=[("ps", bank)])
                P.op("dve", lambda e, half=half, bank=bank: e.tensor_tensor(
                    out=x3[:, half * 512:(half + 1) * 512], in0=sig[:, half * 512:(half + 1) * 512],
                    in1=ps[bank][:], op=ALU.mult), r=[("gb", 1), ("ps", bank)], w=[("gb", 2)])
            P.op("dve", lambda e: e.tensor_tensor(out=x3[:], in0=x3[:], in1=y[:], op=ALU.add),
                 r=[("gb", 2), "y"], w=[("gb", 2)])
            rmsnorm(x3, ("gb", 2), fin_b, "fin_b", outt, ("gb", 3), 4)
            P.dma("sp", lambda e, t0=t0: e.dma_start(out=out_d[t0:t0 + 128, :], in_=outt[:]),
                  r=[("gb", 3)], w=["out_d"])
        head(0)
        for ti in range(NT):
            P.begin()
            gather(ti)
            G = P.end()
            H = []
            if ti + 1 < NT:
                P.begin()
                head(ti + 1)
                H = P.end()
            P.merge([G, H], [1.0, 0.55])
            tail(ti)
    P.op("sp", None, r=["out_d"] + [("x1_d", i) for i in range(NT)], w=[])
    P.emit(es)
    es.close()
    return nc


def kernel(**inputs):
    NT = SEQ // 128
    nc = bass.Bass("TRN2", target_bir_lowering=False)
    build(nc, NT)
    shared = core_inputs(inputs, 0)
    in_maps = []
    for b in range(8):
        m = dict(shared)
        m["x"] = np.ascontiguousarray(np.asarray(inputs["x"][b], dtype=np.float32))
        m["p"] = np.ascontiguousarray(np.asarray(inputs["p"][0, b], dtype=np.float32))
        m["positions"] = np.ascontiguousarray(np.asarray(inputs["positions"][b], dtype=np.int32))
        in_maps.append(m)
    res = run_bass_kernel_spmd(nc, in_maps, core_ids=list(range(8)))
    return np.stack([np.asarray(r["out"], dtype=np.float32) for r in res.results], axis=0)


def core_inputs(inputs, b, T=SEQ):
    f = lambda a: np.ascontiguousarray(np.asarray(a, dtype=np.float32))
    w_in = np.asarray(inputs["w_in"][0], dtype=np.float32)
    o_sq = 4 * 512 + 8
    swq = w_in[:, o_sq:o_sq + 512].reshape(D, 8, 64)
    swq_p = np.stack([np.concatenate([swq[:, c], swq[:, 4 + c]], axis=1) for c in range(4)], axis=1).reshape(D, 512)
    w_in_p = np.concatenate([w_in[:, :o_sq], swq_p, w_in[:, o_sq + 512:]], axis=1)
    convT = np.asarray(inputs["conv_w"][0], dtype=np.float32).reshape(4, 12, 128).transpose(2, 1, 0).reshape(128, 48)
    return {
        "x": f(inputs["x"][b, :T]),
        "p": f(inputs["p"][0, b, :T]),
        "positions": np.ascontiguousarray(np.asarray(inputs["positions"][b, :T], dtype=np.int32)),
        "consts": make_consts(),
        "mix_norm": f(inputs["mix_norm"][0]),
        "w_in": f(w_in_p),
        "convT": f(convT),
        "dn_dt_bias": f(inputs["dn_dt_bias"][0]),
        "dn_a_log": f(inputs["dn_a_log"][0]),
        "dn_out_norm": f(inputs["dn_out_norm"][0]),
        "attn_sinks": f(inputs["attn_sinks"][0]),
        "w_out": f(inputs["w_out"][0]),
        "ffn_norm": f(inputs["ffn_norm"][0]),
        "ple_norm": f(inputs["ple_norm"][0]),
        "final_norm": f(inputs["final_norm"]),
        "peer_wq": f(inputs["peer_wq"][0]),
        "peer_keysT": f(np.asarray(inputs["peer_keys"][0]).reshape(16, 128, 128).transpose(0, 2, 1)),
        "peer_uv": f(np.concatenate([np.asarray(inputs["peer_u"][0], dtype=np.float32),
                                     np.asarray(inputs["peer_v"][0], dtype=np.float32)], axis=1)),
        "ple_gate": f(inputs["ple_gate"][0]),
        "ple_proj": f(inputs["ple_proj"][0]),
    }
```

```python
import numpy as np
from contextlib import ExitStack
import concourse.bass as bass
import concourse.mybir as mybir
from concourse.bass_utils import run_bass_kernel_spmd

F32 = mybir.dt.float32
BF16 = mybir.dt.bfloat16
I32 = mybir.dt.int32
U32 = mybir.dt.uint32
ALU = mybir.AluOpType
AF = mybir.ActivationFunctionType
AX = mybir.AxisListType

D = 1024
SEQ = 4096
EPS = 1e-6
UVB_KIND = "Internal"


class _Op:
    __slots__ = ("eng", "fn", "is_dma", "deps", "signal", "tok", "dsem", "dval")

    def __init__(self, eng, fn, is_dma):
        self.eng = eng
        self.fn = fn
        self.is_dma = is_dma
        self.deps = []
        self.signal = False
        self.tok = None
        self.dsem = None
        self.dval = 0


class _Rec:
    def __init__(self):
        self.call = None

    def __getattr__(self, name):
        def f(*a, **k):
            self.call = (name, a, k)
            return self
        return f


class Prog:
    ENGS = ("pe", "act", "dve", "pool", "sp")
    NDMA = {"sp": 24, "act": 8, "pool": 40}

    def __init__(self, nc):
        self.nc = nc
        self.ops = {e: [] for e in self.ENGS}
        self.res = {}
        self.dk = {e: 0 for e in self.ENGS}
        self.last_dma = {}
        self.pending = None
        self.bankmap = list(range(8))

    def _add(self, eng, fn, r, w, is_dma):
        bm = self.bankmap
        r = [(("ps", bm[k[1]]) if (isinstance(k, tuple) and len(k) == 2 and k[0] == "ps") else k) for k in r]
        w = [(("ps", bm[k[1]]) if (isinstance(k, tuple) and len(k) == 2 and k[0] == "ps") else k) for k in w]
        if fn is not None:
            rec = _Rec()
            fn(rec)
            name, a, k = rec.call
            fn = (lambda engine, name=name, a=a, k=k: getattr(engine, name)(*a, **k))
        if self.pending is not None:
            self.pending.append((eng, fn, list(r), list(w), is_dma))
            return None
        return self._commit(eng, fn, r, w, is_dma)

    def begin(self):
        self.pending = []

    def end(self):
        lst, self.pending = self.pending, None
        return lst

    def merge(self, lists, spans=None, starts=None):
        items = []
        for li, lst in enumerate(lists):
            span = 1.0 if spans is None else spans[li]
            st = 0.0 if starts is None else starts[li]
            for j, it in enumerate(lst):
                items.append((st + (j + 0.5) / len(lst) * (span - st), li, j, it))
        items.sort(key=lambda t: t[:3])
        for _, _, _, (eng, fn, r, w, is_dma) in items:
            self._commit(eng, fn, r, w, is_dma)

    def _commit(self, eng, fn, r, w, is_dma):
        op = _Op(eng, fn, is_dma)
        if is_dma:
            op.dval = self.dk[eng] % self.NDMA[eng]
            self.dk[eng] += 1
            self.last_dma[(eng, op.dval)] = op
        deps = []
        for k in r:
            st = self.res.get(k)
            if st is not None and st[0] is not None:
                deps.append(st[0])
        for k in w:
            st = self.res.get(k)
            if st is not None:
                if st[0] is not None:
                    deps.append(st[0])
                deps.extend(st[1])
        for k in r:
            st = self.res.get(k)
            if st is None:
                st = self.res[k] = [None, []]
            st[1].append(op)
        for k in w:
            self.res[k] = [op, []]
        seen = set()
        for d in deps:
            if d is op or id(d) in seen:
                continue
            seen.add(id(d))
            if d.eng == "pe" and eng == "pe" and not d.is_dma and not is_dma:
                continue
            op.deps.append(d)
            d.signal = True
        self.ops[eng].append(op)
        return op

    def op(self, eng, fn, r=(), w=()):
        return self._add(eng, fn, r, w, False)

    def dma(self, q, fn, r=(), w=()):
        return self._add(q, fn, r, w, True)

    def barrier(self):
        lasts = list(self.last_dma.values())
        for e in self.ENGS:
            for op in reversed(self.ops[e]):
                if not op.is_dma and op.fn is not None:
                    lasts.append(op)
                    break
        for d in lasts:
            d.signal = True
        for e in self.ENGS:
            op = _Op(e, None, False)
            op.deps = list(lasts)
            self.ops[e].append(op)

    def emit(self, es):
        nc = self.nc
        csem = {e: es.enter_context(nc.semaphore("c_" + e)) for e in self.ENGS}
        dsem = {e: [es.enter_context(nc.semaphore("d_%s_%d" % (e, i))) for i in range(n)]
                for e, n in self.NDMA.items()}
        for e in self.ENGS:
            cnt = 0
            k = 0
            vals = [0] * self.NDMA.get(e, 0)
            for op in self.ops[e]:
                if op.is_dma:
                    s = op.dval
                    op.dval = vals[s]
                    vals[s] += 16
                    op.dsem = dsem[e][s]
                    op.tok = (op.dsem, vals[s])
                elif op.signal:
                    cnt += 1
                    op.tok = (csem[e], cnt)
        ops = self.ops

        def make(e):
            def body(engine):
                waited = {}
                for op in ops[e]:
                    waits = [d.tok for d in op.deps]
                    if op.is_dma and op.dval > 0:
                        waits.append((op.dsem, op.dval))
                    for (s, v) in waits:
                        if waited.get(s, 0) >= v:
                            continue
                        engine.wait_ge(s, v)
                        waited[s] = v
                    if op.fn is None:
                        continue
                    ins = op.fn(engine)
                    if op.is_dma:
                        ins.then_inc(op.dsem, 16)
                    elif op.signal:
                        ins.then_inc(csem[e], 1)
            return body

        with nc.Block() as block:
            block.tensor(make("pe"))
            block.scalar(make("act"))
            block.vector(make("dve"))
            block.gpsimd(make("pool"))
            block.sync(make("sp"))


C_ID = 0
C_IOTA16 = 128
C_IOTA16x16 = 144
C_TRI = 160
C_NEGI = 288
C_NEGS = 416
C_MCUR = 544
C_MPREV = 672
C_PERM = 800
C_INVF = 928
C_END = 932
NEG = -30000.0

W_Q, W_K, W_V, W_Z, W_B, W_A, W_SQ, W_SK, W_SV, W_COLS = 0, 512, 1024, 1536, 2048, 2052, 2056, 2568, 2696, 2824


def make_consts():
    c = np.zeros((128, C_END), np.float32)
    i = np.arange(128)
    c[:, C_ID:C_ID + 128] = np.eye(128, dtype=np.float32)
    c[:, C_IOTA16:C_IOTA16 + 16] = np.arange(16, dtype=np.float32)[None, :]
    c[:, C_IOTA16x16:C_IOTA16x16 + 16] = (16.0 * np.arange(16, dtype=np.float32))[None, :]
    le = (i[:, None] <= i[None, :])
    lt = (i[:, None] < i[None, :])
    c[:, C_TRI:C_TRI + 128] = le.astype(np.float32)
    c[:, C_NEGI:C_NEGI + 128] = np.where(le, 0.0, NEG)
    c[:, C_NEGS:C_NEGS + 128] = np.where(lt, 0.0, NEG)
    c[:, C_MCUR:C_MCUR + 128] = le.astype(np.float32)
    c[:, C_MPREV:C_MPREV + 128] = (~le).astype(np.float32)
    perm = np.zeros((128, 128), np.float32)
    invf = np.zeros((128,), np.float32)
    for m in range(128):
        d = m % 64
        if d < 8:
            perm[m + 8, m] = -1.0
        elif d < 16:
            perm[m - 8, m] = 1.0
        if d < 16:
            invf[m] = np.float32(500000.0) ** np.float32(-(2.0 * (d % 8)) / 16.0)
    c[:, C_PERM:C_PERM + 128] = perm
    c[:, C_INVF] = invf
    return c


class _Cut(Exception):
    pass


def build(nc, NT, dbg=False, cut=99, phase2=True):
    T = NT * 128
    P = Prog(nc)
    es = ExitStack()
    TWO_PI = 6.283185307179586
    PI = 3.141592653589793

    def dram(name, shape, dt, kind):
        return nc.dram_tensor(name, list(shape), dt, kind=kind).ap()

    x_d = dram("x", [T, D], F32, "ExternalInput")
    p_d = dram("p", [T, 256], F32, "ExternalInput")
    pos_d = dram("positions", [T], I32, "ExternalInput")
    consts_d = dram("consts", [128, C_END], F32, "ExternalInput")
    mix_norm_d = dram("mix_norm", [D], F32, "ExternalInput")
    win_d = dram("w_in", [D, W_COLS], F32, "ExternalInput")
    convT_d = dram("convT", [128, 48], F32, "ExternalInput")
    dtb_d = dram("dn_dt_bias", [4], F32, "ExternalInput")
    alog_d = dram("dn_a_log", [4], F32, "ExternalInput")
    onorm_d = dram("dn_out_norm", [128], F32, "ExternalInput")
    sinks_d = dram("attn_sinks", [8], F32, "ExternalInput")
    wout_d = dram("w_out", [D, D], F32, "ExternalInput")
    ffn_norm_d = dram("ffn_norm", [D], F32, "ExternalInput")
    ple_norm_d = dram("ple_norm", [D], F32, "ExternalInput")
    final_norm_d = dram("final_norm", [D], F32, "ExternalInput")
    wq_d = dram("peer_wq", [D, 2048], F32, "ExternalInput")
    keysT_d = dram("peer_keysT", [16, 128, 128], F32, "ExternalInput")
    uv_d = dram("peer_uv", [16384, 2 * D], F32, "ExternalInput")
    uvb_d = dram("peer_uvb", [16384, 2 * D], BF16, UVB_KIND)
    gate_d = dram("ple_gate", [D, D], F32, "ExternalInput")
    proj_d = dram("ple_proj", [256, D], F32, "ExternalInput")
    out_d = dram("out", [T, D], F32, "ExternalOutput")
    x1_d = dram("x1_scratch", [T, D], F32, "ExternalOutput" if dbg else "Internal")
    dbg_d = {}

    def sb(name, shape, dt):
        return es.enter_context(nc.sbuf_tensor("s_" + name, list(shape), dt))

    ps = [es.enter_context(nc.psum_tensor("ps%d" % i, [128, 512], F32)) for i in range(8)]
    PRIV_WORDS = 36378
    priv = sb("priv", [128, PRIV_WORDS], F32)
    cur = [0]

    def pv(name, shape, dt):
        nel = int(np.prod(shape[1:]))
        esz = 4 if dt in (F32, I32, U32) else 2
        nw = (nel * esz + 3) // 4
        assert cur[0] + nw <= PRIV_WORDS, (name, cur[0], nw)
        ap = priv[:, cur[0]:cur[0] + nw]
        cur[0] += nw
        if dt != F32:
            ap = ap.bitcast(dt)
        ap = ap[:, 0:nel]
        if len(shape) == 3:
            ap = ap.rearrange("p (a b) -> p a b", a=shape[1])
        elif len(shape) == 4:
            ap = ap.rearrange("p (a b c) -> p a b c", a=shape[1], b=shape[2])
        return ap

    consts = sb("consts", [128, C_END], F32)
    ident_bf = sb("ident_bf", [128, 128], BF16)
    ones_bf = sb("ones_bf", [128, 128], BF16)
    perm_bf = sb("perm_bf", [128, 128], BF16)
    mcur_bf = sb("mcur_bf", [128, 128], BF16)
    mprev_bf = sb("mprev_bf", [128, 128], BF16)
    arena = sb("arena", [128, 24576], BF16)
    convT = sb("convT", [128, 12, 4], F32)
    dtb_b = sb("dtb_b", [128, 4], F32)
    nea_b = sb("nea_b", [128, 4], F32)
    onorm_b = sb("onorm_b", [128, 128], F32)
    esink_b = sb("esink_b", [128, 8], F32)

    win = arena[:, 0:8 * W_COLS].rearrange("p (k c) -> p k c", k=8)
    wq = arena[:, 0:16384].rearrange("p (k c) -> p k c", k=8)
    gate_w = arena[:, 16384:24576].rearrange("p (k c) -> p k c", k=8)

    x1 = sb("x1t", [128, D], F32)
    y = sb("y", [128, D], F32)
    ss = sb("ss", [128, 8], F32)
    hbf = sb("hbf", [128, D], BF16)
    hT = sb("hT", [128, 8, 128], BF16)
    wout = pv("wout", [128, 8, D], BF16)
    cvb = [pv("cvb%d" % i, [128, D], BF16) for i in range(2)]
    mix_b = pv("mix_b", [128, D], F32)
    zq = pv("zq", [128, 12, 131], F32)
    cacc = pv("cacc", [128, 12, 128], F32)
    qkvc = pv("qkvc", [128, 12, 128], F32)
    sqbf = pv("sqbf", [128, 8, 128], BF16)
    rs8 = pv("rs8", [128, 8, 128], F32)
    qTn = pv("qTn", [128, 4, 128], BF16)
    kTn = pv("kTn", [128, 4, 128], BF16)
    vTb = pv("vTb", [128, 4, 128], BF16)
    bv = pv("bv", [128, 4, 128], BF16)
    tpb = pv("tpb", [128, 4, 128], BF16)
    tpb2 = pv("tpb2", [128, 4, 128], BF16)
    kd = pv("kd", [128, 4, 128], BF16)
    qgT = pv("qgT", [128, 4, 128], BF16)
    g8 = pv("g8", [128, 24], F32)
    q8 = pv("q8", [128, 8], F32)
    nq8 = pv("nq8", [128, 4], F32)
    qrep = pv("qrep", [128, 8, 128], F32)
    egl = pv("egl", [128, 4], F32)
    kds = pv("kds", [128, 4], F32)
    nbeg = pv("nbeg", [128, 4], F32)
    beta = pv("beta", [128, 4], F32)
    egcrow = pv("egcrow", [128, 4, 128], F32)
    tmpI = pv("tmpI", [128, 4, 128], F32)
    tmpS = pv("tmpS", [128, 4, 128], F32)
    DTI = pv("DTI", [128, 4, 128], F32)
    EB = pv("EB", [128, 4, 128], F32)
    aqkT = pv("aqkT", [128, 4, 128], BF16)
    Wm = [pv("Wm%d" % i, [128, 4, 128], BF16) for i in range(2)]
    Vm = [pv("Vm%d" % i, [128, 4, 128], BF16) for i in range(2)]
    VI = pv("VI", [128, 4, 128], BF16)
    Pm = [pv("Pm%d" % i, [128, 4, 128], BF16) for i in range(2)]
    Sst = [pv("Sst%d" % i, [128, 4, 128], BF16) for i in range(2)]
    rr = pv("rr", [128, 4, 128], BF16)
    vnew = pv("vnew", [128, 4, 128], BF16)
    osq = pv("osq", [128, 4, 128], F32)
    o4 = pv("o4", [128, 8], F32)
    ot = pv("ot", [128, 4, 128], F32)
    zs = pv("zs", [128, 512], F32)
    mix = pv("mix", [128, D], BF16)
    mixT = pv("mixT", [128, 8, 128], BF16)
    qraw = pv("qraw", [128, 5, 128], F32)
    qrb = pv("qrb", [128, 5, 128], BF16)
    qrot = pv("qrot", [128, 5, 128], BF16)
    qtmp = pv("qtmp", [128, 5, 128], F32)
    krot = [pv("krot%d" % i, [128, 128], BF16) for i in range(2)]
    v1 = [pv("v1_%d" % i, [128, 2, 65], BF16) for i in range(2)]
    posi = pv("posi", [128, 128], I32)
    ang = pv("ang", [128, 2, 128], F32)
    angk = pv("angk", [128, 2, 128], I32)
    angf = pv("angf", [128, 2, 128], F32)
    sincos = pv("sincos", [128, 2, 128], F32)
    Ecur = [pv("Ecur%d" % i, [128, 512], BF16) for i in range(2)]
    Eprev = [pv("Eprev%d" % i, [128, 512], BF16) for i in range(2)]
    den = pv("den", [128, 8], F32)
    x1I = [x1, pv("x1I2", [128, D], F32)]
    zsp = [zs, pv("zs2", [128, 512], F32)]
    mixp = [mix, pv("mix2", [128, D], BF16)]
    kTnp = [kTn, pv("kTn2", [128, 4, 128], BF16)]
    qgTp = [qgT, pv("qgT2", [128, 4, 128], BF16)]
    kdp = [kd, pv("kd2", [128, 4, 128], BF16)]
    bvp = [bv, pv("bv2", [128, 4, 128], BF16)]
    aqkTp = [aqkT, pv("aqkT2", [128, 4, 128], BF16)]
    nbegp = [nbeg, pv("nbeg2", [128, 4], F32)]
    eglp = [egl, pv("egl2", [128, 4], F32)]
    PTp = [pv("PTp%d" % i, [128, 4, 128], BF16) for i in range(2)]
    if dbg:
        print("priv words phase I:", cur[0])
    cur[0] = 0
    keysT = pv("keysT", [128, 16, 128], BF16)
    ffn_b = pv("ffn_b", [128, D], F32)
    ple_b = pv("ple_b", [128, D], F32)
    fin_b = pv("fin_b", [128, D], F32)
    proj_w = pv("proj_w", [128, 2, D], BF16)
    qT = pv("qT", [128, 16, 128], BF16)
    sc = pv("sc", [128, 16, 128], F32)
    sc2 = sc
    top = pv("top", [128, 16, 16], F32)
    tidx = pv("tidx", [128, 16, 16], U32)
    tidxf = pv("tidxf", [128, 16, 16], F32)
    cand = sc2.rearrange("p (h two) k -> p h (two k)", two=2)
    cand2 = cand
    best = pv("best", [128, 8, 16], F32)
    bpos = pv("bpos", [128, 8, 16], U32)
    bposf = pv("bposf", [128, 8, 16], F32)
    big4 = cand2.rearrange("p h (a b) -> p h a b", b=16)
    asel = pv("asel", [128, 8, 16], F32)
    bsel = pv("bsel", [128, 8, 16], F32)
    isel = pv("isel", [128, 8, 16], F32)
    jsel = pv("jsel", [128, 8, 16], F32)
    ef = pv("ef", [128, 128], F32)
    eidx = pv("eidx", [128, 128], I32)
    gates = pv("gates", [128, 8, 16], F32)
    gz = pv("gz", [128, 8], F32)
    actpre = pv("actpre", [128, 128], F32)
    gl_a = pv("gl_a", [128, 128], F32)
    gl_b = pv("gl_b", [128, 128], F32)
    wts = pv("wts", [128, 128], F32)
    NG = 20
    GS = 4
    gbw = [pv("gb%d" % i, [128, D], F32) for i in range(NG)]
    gb = [w_.bitcast(BF16) for w_ in gbw]
    prod = [pv("prod%d" % i, [128, D], BF16) for i in range(3)]
    dg = [pv("dg%d" % i, [128, 4, 128], BF16) for i in range(2)]
    pt = pv("pt", [128, 256], F32)
    ptbf = pv("ptbf", [128, 256], BF16)
    pT = pv("pT", [128, 2, 128], BF16)
    if dbg:
        print("priv words phase II:", cur[0])
    x1s = [x1, pv("x1b", [128, D], F32)]
    hbfs = [hbf, pv("hbf2", [128, D], BF16)]
    eidxs = [eidx, pv("eidx2", [128, 128], I32)]
    gatess = [gates, pv("gates2", [128, 8, 16], F32)]
    tbf = pv("tbf", [128, D], BF16)
    sig = gbw[1]
    x3 = gbw[2]
    outt = gbw[3]

    def psm(i):
        return ps[P.bankmap[i]]

    def psbf(i):
        return psm(i)[:].bitcast(BF16)

    def ps4(i):
        return psm(i)[:].rearrange("p (h t) -> p h t", h=4)

    def cst(off, n=128):
        return consts[:, off:off + n]

    P.dma("sp", lambda e: e.dma_start(out=consts[:], in_=consts_d), w=["consts"])
    for (tile_, src, key) in ((mix_b, mix_norm_d, "mix_b"),
                              (dtb_b, dtb_d, "dtb_b"), (nea_b, alog_d, "nea_b"),
                              (onorm_b, onorm_d, "onorm_b"), (esink_b, sinks_d, "esink_b")):
        P.dma("sp", lambda e, tile_=tile_, src=src: e.dma_start(out=tile_[:], in_=src.partition_broadcast(128)),
              w=[key])
    P.dma("sp", lambda e: e.dma_start(out=convT[:].rearrange("p c j -> p (c j)"), in_=convT_d), w=["convT"])
    for kc in range(8):
        P.dma("pool", lambda e, kc=kc: e.dma_start(out=arena[:, kc * W_COLS:(kc + 1) * W_COLS],
                                                   in_=win_d[kc * 128:(kc + 1) * 128, :]), w=["arena"])
    for kc in range(8):
        P.dma("pool", lambda e, kc=kc: e.dma_start(out=wout[:, kc, :], in_=wout_d[kc * 128:(kc + 1) * 128, :]),
              w=["wout"])
    P.op("dve", lambda e: e.tensor_copy(out=ident_bf[:], in_=cst(C_ID)), r=["consts"], w=["ident_bf"])
    P.op("dve", lambda e: e.tensor_copy(out=perm_bf[:], in_=cst(C_PERM)), r=["consts"], w=["perm_bf"])
    P.op("dve", lambda e: e.tensor_copy(out=mcur_bf[:], in_=cst(C_MCUR)), r=["consts"], w=["mcur_bf"])
    P.op("dve", lambda e: e.tensor_copy(out=mprev_bf[:], in_=cst(C_MPREV)), r=["consts"], w=["mprev_bf"])
    P.op("dve", lambda e: e.memset(ones_bf[:], 1.0), w=["ones_bf"])
    P.op("dve", lambda e: e.memset(zq[:], 0.0), w=["zq"])
    P.op("dve", lambda e: e.memset(Sst[0][:], 0.0), w=[("S", 0)])
    for i in range(2):
        P.op("dve", lambda e, i=i: e.memset(v1[i][:], 1.0), w=[("v1", i)])
    P.op("act", lambda e: e.activation(out=nea_b[:], in_=nea_b[:], func=AF.Exp), r=["nea_b"], w=["nea_b"])
    P.op("dve", lambda e: e.tensor_scalar(out=nea_b[:], in0=nea_b[:], scalar1=-1.0, scalar2=None, op0=ALU.mult),
         r=["nea_b"], w=["nea_b"])
    P.op("act", lambda e: e.activation(out=esink_b[:], in_=esink_b[:], func=AF.Exp), r=["esink_b"], w=["esink_b"])

    iota16 = consts[:, C_IOTA16:C_IOTA16 + 16]
    iota16x16 = consts[:, C_IOTA16x16:C_IOTA16x16 + 16]

    def rmsnorm(src, src_key, gain_b, gain_key, dst_f32, dst_key, col, dst_bf=None, dst_bf_key=None):
        P.op("act", lambda e: e.activation(out=dst_f32[:], in_=src[:], func=AF.Square,
                                           accum_out=ss[:, col:col + 1]),
             r=[src_key], w=[dst_key, ("ss", col)])
        P.op("act", lambda e: e.activation(out=ss[:, col + 1:col + 2], in_=ss[:, col:col + 1], func=AF.Sqrt,
                                           scale=1.0 / D, bias=EPS),
             r=[("ss", col)], w=[("ss", col + 1)])
        P.op("dve", lambda e: e.reciprocal(out=ss[:, col + 1:col + 2], in_=ss[:, col + 1:col + 2]),
             r=[("ss", col + 1)], w=[("ss", col + 1)])
        P.op("dve", lambda e: e.scalar_tensor_tensor(out=dst_f32[:], in0=src[:], scalar=ss[:, col + 1:col + 2],
                                                     in1=gain_b[:], op0=ALU.mult, op1=ALU.mult),
             r=[src_key, ("ss", col + 1), gain_key], w=[dst_key])
        if dst_bf is not None:
            P.op("act", lambda e: e.copy(out=dst_bf[:], in_=dst_f32[:]), r=[dst_key], w=[dst_bf_key])

    def transpose8(src_bf, src_key, dst, dst_key, nch, bank):
        pb = psbf(bank)
        for c in range(nch):
            P.op("pe", lambda e, c=c: e.transpose(out=pb[:, c * 128:(c + 1) * 128],
                                                  in_=src_bf[:, c * 128:(c + 1) * 128], identity=ident_bf[:]),
                 r=(list(src_key) if isinstance(src_key, list) else [src_key]) + ["ident_bf"], w=[("ps", bank)])
        P.op("dve", lambda e: e.tensor_copy(out=dst[:].rearrange("p c t -> p (c t)"),
                                            in_=pb[:, 0:nch * 128]),
             r=[("ps", bank)], w=[dst_key])

    def mm(out, lhsT, rhs, start, stop, r, bank):
        P.op("pe", lambda e: e.matmul(out=out, lhsT=lhsT, rhs=rhs, start=start, stop=stop),
             r=r, w=[("ps", bank)])

    def dve(fn, r, w):
        P.op("dve", fn, r=r, w=w)

    def ck(n):
        if cut == n:
            raise _Cut()

    def act(fn, r, w):
        P.op("act", fn, r=r, w=w)

    def proj_fm(col, bank, slot):
        for kc in range(8):
            mm(psm(bank)[:, slot * 128:(slot + 1) * 128], win[:, kc, col:col + 128], hT[:, kc, :],
               kc == 0, kc == 7, ["arena", "hT"], bank)

    def frontA(ti):
        P.bankmap[:] = [0, 1, 2, 0, 1, 0, 1, 2]
        t0 = ti * 128
        par = ti % 2
        x1, zs, mix = x1I[par], zsp[par], mixp[par]
        kTn, qgT, kd, bv, aqkT = kTnp[par], qgTp[par], kdp[par], bvp[par], aqkTp[par]
        nbeg, egl = nbegp[par], eglp[par]
        for cc in range(256 // NT):
            ci = ti * (256 // NT) + cc
            rows = slice((ci // 2) * 128, (ci // 2) * 128 + 128)
            cols = slice((ci % 2) * D, (ci % 2) * D + D)
            cb = ci % 2
            P.dma("pool", lambda e: e.dma_start(out=cvb[cb][:], in_=uv_d[rows, cols]), w=[("cvb", cb)])
            P.dma("pool", lambda e: e.dma_start(out=uvb_d[rows, cols], in_=cvb[cb][:]), r=[("cvb", cb)],
                  w=[("uvb", ci)])
        P.dma("sp", lambda e, t0=t0: e.dma_start(out=x1[:], in_=x_d[t0:t0 + 128, :]), w=[("x1", par)])
        P.dma("sp", lambda e, t0=t0: e.dma_start(out=posi[:], in_=pos_d[t0:t0 + 128].partition_broadcast(128)),
              w=["posi"])
        rmsnorm(x1, ("x1", par), mix_b, "mix_b", hbf, "hbf", 0)
        transpose8(hbf, "hbf", hT, "hT", 8, 0)

        for grp in range(3):
            bank = 1 + grp
            for s4 in range(4):
                proj_fm((grp * 4 + s4) * 128, bank, s4)
            act(lambda e, grp=grp, bank=bank: e.copy(out=zq[:, grp * 4:(grp + 1) * 4, 3:131], in_=ps4(bank)),
                [("ps", bank)], ["zq"])
        for kc in range(8):
            mm(psm(6)[:, 0:512], hT[:, kc, :], win[:, kc, W_Z:W_Z + 512], kc == 0, kc == 7, ["arena", "hT"], 6)
        for kc in range(8):
            mm(psm(7)[:, 0:8], hT[:, kc, :], win[:, kc, W_B:W_B + 8], kc == 0, kc == 7, ["arena", "hT"], 7)
        act(lambda e: e.activation(out=zs[:], in_=psm(6)[:], func=AF.Silu), [("ps", 6)], [("zs", par)])
        act(lambda e: e.activation(out=beta[:], in_=psm(7)[:, 0:4], func=AF.Sigmoid), [("ps", 7)], ["beta"])
        dve(lambda e: e.tensor_tensor(out=g8[:, 4:8], in0=dtb_b[:], in1=psm(7)[:, 4:8], op=ALU.add),
            [("ps", 7), "dtb_b"], ["g8a"])
        act(lambda e: e.activation(out=g8[:, 8:12], in_=g8[:, 4:8], func=AF.Abs), ["g8a"], ["g8b"])
        act(lambda e: e.activation(out=g8[:, 12:16], in_=g8[:, 8:12], func=AF.Exp, scale=-1.0), ["g8b"], ["g8c"])
        act(lambda e: e.activation(out=g8[:, 12:16], in_=g8[:, 12:16], func=AF.Ln, bias=1.0), ["g8c"], ["g8c"])
        act(lambda e: e.activation(out=g8[:, 20:24], in_=beta[:], func=AF.Ln), ["beta"], ["g8e"])
        dve(lambda e: e.scalar_tensor_tensor(out=g8[:, 16:20], in0=g8[:, 4:8], scalar=0.0, in1=g8[:, 12:16],
                                             op0=ALU.max, op1=ALU.add), ["g8a", "g8c"], ["g8d"])
        dve(lambda e: e.tensor_tensor(out=g8[:, 16:20], in0=g8[:, 16:20], in1=nea_b[:], op=ALU.mult),
            ["g8d", "nea_b"], ["g8d"])
        mm(psm(7)[:, 256:260], cst(C_TRI), g8[:, 16:20], True, True, ["consts", "g8d"], 7)
        dve(lambda e: e.tensor_copy(out=q8[:, 0:4], in_=psm(7)[:, 256:260]), [("ps", 7)], ["q8a"])
        dve(lambda e: e.tensor_tensor(out=q8[:, 4:8], in0=q8[:, 0:4], in1=g8[:, 20:24], op=ALU.add),
            ["q8a", "g8e"], ["q8b"])
        dve(lambda e: e.tensor_copy(out=qrep[:], in_=q8[:].unsqueeze(2).to_broadcast([128, 8, 128])),
            ["q8a", "q8b"], ["qrep"])
        for h in range(4):
            mm(psm(6)[:, h * 128:(h + 1) * 128], qrep[:, h, :], cst(C_ID), True, True, ["qrep", "consts", ("zs", par)], 6)
        for h in range(4):
            mm(psm(7)[:, h * 128:(h + 1) * 128], qrep[:, 4 + h, :], cst(C_ID), True, True,
               ["qrep", "consts", "q8a", ("v1", par)], 7)
        gl_last = ps4(6)[:, :, 127]
        act(lambda e: e.activation(out=egl[:], in_=gl_last, func=AF.Exp), [("ps", 6)], [("egl", par)])
        dve(lambda e: e.tensor_tensor(out=kds[:], in0=q8[:, 0:4], in1=gl_last, op=ALU.subtract),
            [("ps", 6), "q8a"], ["kds"])
        act(lambda e: e.activation(out=kds[:], in_=kds[:], func=AF.Exp, scale=-1.0), ["kds"], ["kds"])
        act(lambda e: e.activation(out=nbeg[:], in_=q8[:, 0:4], func=AF.Exp), ["q8a"], [("nbeg", par)])
        dve(lambda e: e.scalar_tensor_tensor(out=nbeg[:], in0=nbeg[:], scalar=-1.0, in1=beta[:],
                                             op0=ALU.mult, op1=ALU.mult), [("nbeg", par), "beta"], [("nbeg", par)])
        act(lambda e: e.activation(out=egcrow[:], in_=ps4(6), func=AF.Exp), [("ps", 6)], ["egcrow"])
        dve(lambda e: e.tensor_scalar(out=nq8[:], in0=q8[:, 0:4], scalar1=-1.0, scalar2=None, op0=ALU.mult),
            ["q8a"], ["nq8"])
        for h in range(4):
            dve(lambda e, h=h: e.scalar_tensor_tensor(out=tmpI[:, h, :], in0=cst(C_NEGI),
                                                      scalar=nq8[:, h:h + 1], in1=psm(6)[:, h * 128:(h + 1) * 128],
                                                      op0=ALU.add, op1=ALU.add),
                [("ps", 6), "nq8", "consts"], [("tmpI", h)])
            dve(lambda e, h=h: e.scalar_tensor_tensor(out=tmpS[:, h, :], in0=cst(C_NEGS),
                                                      scalar=nq8[:, h:h + 1], in1=psm(7)[:, h * 128:(h + 1) * 128],
                                                      op0=ALU.add, op1=ALU.add),
                [("ps", 7), "nq8", "consts"], [("tmpS", h)])
        act(lambda e: e.activation(out=DTI[:], in_=tmpI[:], func=AF.Exp), [("tmpI", h) for h in range(4)], ["DTI"])
        act(lambda e: e.activation(out=EB[:], in_=tmpS[:], func=AF.Exp), [("tmpS", h) for h in range(4)], ["EB"])

        for ch in range(12):
            dve(lambda e, ch=ch: e.tensor_scalar(out=cacc[:, ch, :], in0=zq[:, ch, 3:131],
                                                 scalar1=convT[:, ch, 3:4], scalar2=None, op0=ALU.mult),
                ["zq", "convT"], [("cacc", ch)])
            for j in range(3):
                dve(lambda e, ch=ch, j=j: e.scalar_tensor_tensor(
                    out=cacc[:, ch, :], in0=zq[:, ch, j:j + 128], scalar=convT[:, ch, j:j + 1],
                    in1=cacc[:, ch, :], op0=ALU.mult, op1=ALU.add),
                    ["zq", "convT", ("cacc", ch)], [("cacc", ch)])
        dve(lambda e: e.tensor_copy(out=zq[:, :, 0:3], in_=zq[:, :, 128:131]), ["zq"], ["zq"])
        cacc_all = [("cacc", ch) for ch in range(12)]
        act(lambda e: e.activation(out=qkvc[:], in_=cacc[:], func=AF.Silu), cacc_all, ["qkvc"])
        act(lambda e: e.activation(out=sqbf[:], in_=qkvc[:, 0:8, :], func=AF.Square), ["qkvc"], ["sqbf"])
        for half in range(2):
            bank = 1 + half
            for s4 in range(4):
                mm(psm(bank)[:, s4 * 128:(s4 + 1) * 128], ones_bf[:], sqbf[:, half * 4 + s4, :], True, True,
                   ["ones_bf", "sqbf"], bank)
        act(lambda e: e.activation(out=rs8[:, 0:4, :], in_=ps4(1), func=AF.Sqrt, scale=128.0, bias=128.0 * EPS),
            [("ps", 1)], [("rs8", 0)])
        act(lambda e: e.activation(out=rs8[:, 4:8, :], in_=ps4(2), func=AF.Sqrt, scale=1.0, bias=EPS),
            [("ps", 2)], [("rs8", 1)])
        dve(lambda e: e.reciprocal(out=rs8[:], in_=rs8[:]), [("rs8", 0), ("rs8", 1)], ["rs8"])
        dve(lambda e: e.tensor_tensor(out=qTn[:], in0=qkvc[:, 0:4, :], in1=rs8[:, 0:4, :], op=ALU.mult),
            ["qkvc", "rs8"], ["qTn"])
        dve(lambda e: e.tensor_tensor(out=kTn[:], in0=qkvc[:, 4:8, :], in1=rs8[:, 4:8, :], op=ALU.mult),
            ["qkvc", "rs8"], [("kTn", par)])
        act(lambda e: e.copy(out=vTb[:], in_=qkvc[:, 8:12, :]), ["qkvc"], ["vTb"])
        dve(lambda e: e.tensor_tensor(out=qgT[:], in0=qTn[:], in1=egcrow[:], op=ALU.mult),
            ["qTn", "egcrow"], [("qgT", par)])
        pb1 = psbf(1)
        for h in range(4):
            P.op("pe", lambda e, h=h: e.transpose(out=pb1[:, h * 128:(h + 1) * 128], in_=vTb[:, h, :],
                                                  identity=ident_bf[:]),
                 r=["vTb", "ident_bf", ("rs8", 0)], w=[("ps", 1)])
        dve(lambda e: e.tensor_copy(out=tpb[:].rearrange("p h d -> p (h d)"), in_=pb1[:, 0:512]),
            [("ps", 1)], ["tpb"])
        dve(lambda e: e.tensor_tensor(out=bv[:], in0=tpb[:],
                                      in1=beta[:].unsqueeze(2).to_broadcast([128, 4, 128]), op=ALU.mult),
            ["tpb", "beta"], [("bv", par)])
        pb2 = psbf(2)
        for h in range(4):
            P.op("pe", lambda e, h=h: e.transpose(out=pb2[:, h * 128:(h + 1) * 128], in_=kTn[:, h, :],
                                                  identity=ident_bf[:]),
                 r=[("kTn", par), "ident_bf", ("rs8", 1)], w=[("ps", 2)])
        dve(lambda e: e.tensor_copy(out=tpb2[:].rearrange("p h d -> p (h d)"), in_=pb2[:, 0:512]),
            [("ps", 2)], ["tpb2"])
        dve(lambda e: e.tensor_tensor(out=kd[:], in0=tpb2[:],
                                      in1=kds[:].unsqueeze(2).to_broadcast([128, 4, 128]), op=ALU.mult),
            ["tpb2", "kds"], [("kd", par)])
        for h in range(4):
            mm(psm(3)[:, h * 128:(h + 1) * 128], kTn[:, h, :], kTn[:, h, :], True, True, [("kTn", par), "zq"], 3)
        for h in range(4):
            mm(psm(4)[:, h * 128:(h + 1) * 128], kTn[:, h, :], qTn[:, h, :], True, True, [("kTn", par), "qTn", "qraw"], 4)
        dve(lambda e: e.tensor_tensor(out=Wm[0][:], in0=EB[:], in1=ps4(3), op=ALU.mult),
            [("ps", 3), "EB"], [("Wm", 0)])
        dve(lambda e: e.tensor_tensor(out=aqkT[:], in0=DTI[:], in1=ps4(4), op=ALU.mult),
            [("ps", 4), "DTI"], [("aqkT", par)])
        pb3 = psbf(3)
        for h in range(4):
            P.op("pe", lambda e, h=h: e.transpose(out=pb3[:, h * 128:(h + 1) * 128], in_=Wm[0][:, h, :],
                                                  identity=ident_bf[:]),
                 r=[("Wm", 0), "ident_bf"], w=[("ps", 3)])
        dve(lambda e: e.tensor_copy(out=Vm[0][:].rearrange("p h d -> p (h d)"), in_=pb3[:, 0:512]),
            [("ps", 3)], [("Vm", 0)])
        dve(lambda e: e.scalar_tensor_tensor(out=Pm[0][:], in0=Wm[0][:], scalar=-1.0,
                                             in1=ident_bf[:].unsqueeze(1).to_broadcast([128, 4, 128]),
                                             op0=ALU.mult, op1=ALU.add), [("Wm", 0), "ident_bf"], [("Pm", 0)])
        cw, cp = 0, 0
        for m in range(6):
            nw = 1 - cw
            last = (m == 5)
            if not last:
                for h in range(4):
                    mm(psm(1)[:, h * 128:(h + 1) * 128], Vm[cw][:, h, :], Wm[cw][:, h, :], True, True,
                       [("Vm", cw), ("Wm", cw), ("bv", par)], 1)
            for h in range(4):
                mm(psm(2)[:, h * 128:(h + 1) * 128], Wm[cw][:, h, :], Vm[cw][:, h, :], True, True,
                   [("Vm", cw), ("Wm", cw), ("kd", par)], 2)
            if not last:
                act(lambda e, nw=nw: e.copy(out=Wm[nw][:], in_=ps4(1)), [("ps", 1)], [("Wm", nw)])
            act(lambda e, nw=nw: e.copy(out=Vm[nw][:], in_=ps4(2)), [("ps", 2)], [("Vm", nw)])
            dve(lambda e, nw=nw: e.tensor_tensor(out=VI[:], in0=Vm[nw][:],
                                                 in1=ident_bf[:].unsqueeze(1).to_broadcast([128, 4, 128]),
                                                 op=ALU.add), [("Vm", nw), "ident_bf"], ["VI"])
            for h in range(4):
                mm(psm(3)[:, h * 128:(h + 1) * 128], VI[:, h, :], Pm[cp][:, h, :], True, True,
                   ["VI", ("Pm", cp)], 3)
            dve(lambda e, cp=cp: e.tensor_copy(out=Pm[1 - cp][:], in_=ps4(3)), [("ps", 3)], [("Pm", 1 - cp)])
            cw, cp = nw, 1 - cp
        dve(lambda e: e.tensor_copy(out=PTp[par][:], in_=Pm[cp][:]), [("Pm", cp)], [("PT", par)])

    def swa(ti):
        P.bankmap[:] = [3, 3, 3, 3, 3, 4, 4, 4]
        t0 = ti * 128
        par = ti % 2
        x1, zs, mix = x1I[par], zsp[par], mixp[par]
        kTn, qgT, kd, bv, aqkT = kTnp[par], qgTp[par], kdp[par], bvp[par], aqkTp[par]
        nbeg, egl = nbegp[par], eglp[par]
        for s4 in range(4):
            proj_fm(W_SQ + s4 * 128, 4, s4)
        proj_fm(W_SK, 5, 0)
        act(lambda e: e.copy(out=qraw[:, 0:4, :], in_=ps4(4)), [("ps", 4)], ["qraw"])
        act(lambda e: e.copy(out=qraw[:, 4, :], in_=psm(5)[:, 0:128]), [("ps", 5)], ["qraw"])
        for kc in range(8):
            mm(psm(7)[:, 128:256], hT[:, kc, :], win[:, kc, W_SV:W_SV + 128], kc == 0, kc == 7, ["arena", "hT"], 7)
        dve(lambda e, par=par: e.tensor_copy(out=v1[par][:, :, 0:64],
                                             in_=psm(7)[:, 128:256].rearrange("p (g d) -> p g d", g=2)),
            [("ps", 7)], [("v1", par)])
        dve(lambda e: e.tensor_copy(out=ang[:, 0, :], in_=posi[:]), ["posi"], ["ang0"])
        dve(lambda e: e.tensor_scalar(out=ang[:, 0, :], in0=ang[:, 0, :], scalar1=cst(C_INVF, 1), scalar2=None,
                                      op0=ALU.mult), ["ang0", "consts"], ["ang0"])
        dve(lambda e: e.tensor_scalar(out=ang[:, 1, :], in0=ang[:, 0, :], scalar1=PI / 2, scalar2=None,
                                      op0=ALU.add), ["ang0"], ["ang1"])
        dve(lambda e: e.tensor_scalar(out=angk[:], in0=ang[:], scalar1=1.0 / TWO_PI, scalar2=None, op0=ALU.mult),
            ["ang0", "ang1"], ["angk"])
        dve(lambda e: e.tensor_copy(out=angf[:], in_=angk[:]), ["angk"], ["angf"])
        dve(lambda e: e.scalar_tensor_tensor(out=ang[:], in0=angf[:], scalar=-TWO_PI, in1=ang[:],
                                             op0=ALU.mult, op1=ALU.add), ["angf", "ang0", "ang1"], ["ang"])
        dve(lambda e: e.tensor_single_scalar(out=angf[:], in_=ang[:], scalar=PI, op=ALU.is_gt), ["ang"], ["angf"])
        dve(lambda e: e.scalar_tensor_tensor(out=ang[:], in0=angf[:], scalar=-TWO_PI, in1=ang[:],
                                             op0=ALU.mult, op1=ALU.add), ["angf", "ang"], ["ang"])
        act(lambda e: e.activation(out=sincos[:], in_=ang[:], func=AF.Sin), ["ang"], ["sincos"])
        act(lambda e: e.copy(out=qrb[:], in_=qraw[:]), ["qraw"], ["qrb"])
        for c in range(4):
            mm(psm(5)[:, c * 128:(c + 1) * 128], perm_bf[:], qrb[:, c, :], True, True, ["perm_bf", "qrb"], 5)
        mm(psm(1)[:, 0:128], perm_bf[:], qrb[:, 4, :], True, True, ["perm_bf", "qrb", ("rr", 0), ("rr", 1), ("rr", 2), ("rr", 3)], 1)
        dve(lambda e: e.tensor_tensor(out=qtmp[:, 0:4, :], in0=sincos[:, 0:1, :].to_broadcast([128, 4, 128]),
                                      in1=ps4(5), op=ALU.mult),
            [("ps", 5), "sincos"], ["qtmp"])
        dve(lambda e: e.tensor_tensor(out=qtmp[:, 4, :], in0=sincos[:, 0, :], in1=psm(1)[:, 0:128], op=ALU.mult),
            [("ps", 1), "sincos"], ["qtmp"])
        dve(lambda e: e.tensor_tensor(out=qraw[:], in0=qraw[:],
                                      in1=sincos[:, 1:2, :].to_broadcast([128, 5, 128]), op=ALU.mult),
            ["qraw", "sincos", "qrb"], ["qraw"])
        dve(lambda e: e.tensor_tensor(out=qrot[:, 0:4, :], in0=qraw[:, 0:4, :], in1=qtmp[:, 0:4, :], op=ALU.add),
            ["qraw", "qtmp"], ["qrot"])
        dve(lambda e, par=par: e.tensor_tensor(out=krot[par][:], in0=qraw[:, 4, :], in1=qtmp[:, 4, :], op=ALU.add),
            ["qraw", "qtmp"], [("krot", par)])
        for g in range(2):
            lo, hi = 64 * g, 64 * g + 64
            mm(psm(1 + g)[:], krot[par][lo:hi, :], qrot[lo:hi, 0:4, :].rearrange("p c t -> p (c t)"), True, True,
               [("krot", par), "qrot", "vnew", ("qtmp")], 1 + g)
            act(lambda e, g=g: e.activation(out=Ecur[g][:], in_=psm(1 + g)[:], func=AF.Exp, scale=0.125),
                [("ps", 1 + g)], [("Ecur", g)])
            dve(lambda e, g=g: e.tensor_tensor(out=Ecur[g][:].rearrange("p (c t) -> p c t", c=4),
                                               in0=Ecur[g][:].rearrange("p (c t) -> p c t", c=4),
                                               in1=mcur_bf[:].unsqueeze(1).to_broadcast([128, 4, 128]), op=ALU.mult),
                [("Ecur", g), "mcur_bf"], [("Ecur", g)])
            if ti > 0:
                mm(psm(5 + g)[:], krot[1 - par][lo:hi, :], qrot[lo:hi, 0:4, :].rearrange("p c t -> p (c t)"),
                   True, True, [("krot", 1 - par), "qrot", ("S", 1 - par), "qtmp"], 5 + g)
                act(lambda e, g=g: e.activation(out=Eprev[g][:], in_=psm(5 + g)[:], func=AF.Exp, scale=0.125),
                    [("ps", 5 + g)], [("Eprev", g)])
                dve(lambda e, g=g: e.tensor_tensor(out=Eprev[g][:].rearrange("p (c t) -> p c t", c=4),
                                                   in0=Eprev[g][:].rearrange("p (c t) -> p c t", c=4),
                                                   in1=mprev_bf[:].unsqueeze(1).to_broadcast([128, 4, 128]),
                                                   op=ALU.mult),
                    [("Eprev", g), "mprev_bf"], [("Eprev", g)])
        for g in range(2):
            bank = 3 if g == 0 else 7
            for c in range(4):
                dst = psm(bank)[:, c * 65:(c + 1) * 65]
                deps = [("Ecur", g), ("v1", par), "ot", "tmpS_all"]
                if ti > 0:
                    mm(dst, Eprev[g][:, c * 128:(c + 1) * 128], v1[1 - par][:, g, :], True, False,
                       deps + [("Eprev", g), ("v1", 1 - par)], bank)
                    mm(dst, Ecur[g][:, c * 128:(c + 1) * 128], v1[par][:, g, :], False, True, deps, bank)
                else:
                    mm(dst, Ecur[g][:, c * 128:(c + 1) * 128], v1[par][:, g, :], True, True, deps, bank)
            pv = psm(bank)[:, 0:260].rearrange("p (c d) -> p c d", c=4)
            dve(lambda e, g=g, pv=pv: e.tensor_tensor(out=den[:, g * 4:(g + 1) * 4], in0=esink_b[:, g * 4:(g + 1) * 4],
                                                      in1=pv[:, :, 64], op=ALU.add),
                [("ps", bank), "esink_b"], [("den", g)])
            dve(lambda e, g=g: e.reciprocal(out=den[:, g * 4:(g + 1) * 4], in_=den[:, g * 4:(g + 1) * 4]),
                [("den", g)], [("den", g)])
            dve(lambda e, g=g, pv=pv: e.tensor_tensor(
                out=mix[:, 512 + g * 256:512 + (g + 1) * 256].rearrange("p (c d) -> p c d", c=4),
                in0=den[:, g * 4:(g + 1) * 4].unsqueeze(2).to_broadcast([128, 4, 64]), in1=pv[:, :, 0:64],
                op=ALU.mult), [("ps", bank), ("den", g)], [("mix", par, 1 + g)])

    def back(ti):
        P.bankmap[:] = [6, 5, 6, 3, 7, 5, 6, 7]
        t0 = ti * 128
        par = ti % 2
        x1, zs, mix = x1I[par], zsp[par], mixp[par]
        kTn, qgT, kd, bv, aqkT = kTnp[par], qgTp[par], kdp[par], bvp[par], aqkTp[par]
        nbeg, egl = nbegp[par], eglp[par]
        PT = PTp[par]
        PTk = ("PT", par)
        So, Sn = Sst[par], Sst[1 - par]
        for h in range(4):
            mm(psm(1)[:, h * 128:(h + 1) * 128], kTn[:, h, :], So[:, h, :], True, True, [("kTn", par), ("S", par)], 1)
        dve(lambda e: e.tensor_tensor(out=osq[:], in0=nbeg[:].unsqueeze(2).to_broadcast([128, 4, 128]),
                                      in1=ps4(1), op=ALU.mult),
            [("ps", 1), ("nbeg", par)], ["osq"])
        dve(lambda e: e.tensor_tensor(out=rr[:], in0=osq[:], in1=bv[:], op=ALU.add),
            ["osq", ("bv", par)], [("rr", h) for h in range(4)])
        for h in range(4):
            mm(psm(2)[:, h * 128:(h + 1) * 128], PT[:, h, :], rr[:, h, :], True, True, [PTk, ("rr", h)], 2)
        act(lambda e: e.copy(out=vnew[:], in_=ps4(2)), [("ps", 2)], ["vnew"])
        for h in range(4):
            mm(psm(4)[:, h * 128:(h + 1) * 128], qgT[:, h, :], So[:, h, :], True, False, [("qgT", par), ("S", par), ("aqkT", par)], 4)
            mm(psm(4)[:, h * 128:(h + 1) * 128], aqkT[:, h, :], vnew[:, h, :], False, True, [("aqkT", par), "vnew"], 4)
        for h in range(4):
            mm(psm(5)[:, h * 128:(h + 1) * 128], kd[:, h, :], vnew[:, h, :], True, True, [("kd", par), "vnew", "qraw"], 5)
        for h in range(4):
            dve(lambda e, h=h: e.scalar_tensor_tensor(out=Sn[:, h, :], in0=So[:, h, :], scalar=egl[:, h:h + 1],
                                                      in1=psm(5)[:, h * 128:(h + 1) * 128],
                                                      op0=ALU.mult, op1=ALU.add),
                [("S", par), ("egl", par), ("ps", 5)], [("S", 1 - par)])
        act(lambda e: e.activation(out=osq[:], in_=ps4(4), func=AF.Square), [("ps", 4)], ["osq"])
        dve(lambda e: e.tensor_reduce(out=o4[:, 0:4], in_=osq[:], axis=AX.X, op=ALU.add), ["osq"], ["o4"])
        act(lambda e: e.activation(out=o4[:, 4:8], in_=o4[:, 0:4], func=AF.Sqrt, scale=1.0 / 128.0, bias=EPS),
            ["o4"], ["o4b"])
        dve(lambda e: e.reciprocal(out=o4[:, 4:8], in_=o4[:, 4:8]), ["o4b"], ["o4b"])
        dve(lambda e: e.tensor_tensor(out=ot[:], in0=o4[:, 4:8].unsqueeze(2).to_broadcast([128, 4, 128]),
                                      in1=ps4(4), op=ALU.mult),
            [("ps", 4), "o4b"], ["ot"])
        dve(lambda e: e.tensor_tensor(out=ot[:], in0=ot[:],
                                      in1=onorm_b[:].unsqueeze(1).to_broadcast([128, 4, 128]), op=ALU.mult),
            ["ot", "onorm_b"], ["ot"])
        dve(lambda e: e.tensor_tensor(out=mix[:, 0:512], in0=ot[:].rearrange("p h d -> p (h d)"), in1=zs[:],
                                      op=ALU.mult), ["ot", ("zs", par)], [("mix", par, 0)])

        transpose8(mix, [("mix", par, 0), ("mix", par, 1), ("mix", par, 2)], mixT, "mixT", 8, 0)
        for half in range(2):
            bank = 4 + half
            for kc in range(8):
                mm(psm(bank)[:], mixT[:, kc, :], wout[:, kc, half * 512:(half + 1) * 512], kc == 0, kc == 7,
                   ["mixT", "wout", "osq", "ot", "qtmp"], bank)
            dve(lambda e, half=half, bank=bank: e.tensor_tensor(
                out=y[:, half * 512:(half + 1) * 512], in0=x1[:, half * 512:(half + 1) * 512], in1=psm(bank)[:],
                op=ALU.add), [("ps", bank), ("x1", par)], ["y"])
        P.dma("sp", lambda e, t0=t0: e.dma_start(out=x1_d[t0:t0 + 128, :], in_=y[:]),
              r=["y"], w=[("x1_d", ti)])

    def rec(fn, ti):
        P.begin()
        fn(ti)
        return P.end()

    def swa_start(fa):
        idx = max(j for j, it in enumerate(fa) if ("hT" in it[3] or "posi" in it[3]))
        return (idx + 1.0) / len(fa)

    fa = rec(frontA, 0)
    P.merge([fa, rec(swa, 0)], starts=[0.0, swa_start(fa)])
    for ti in range(NT):
        if ti + 1 < NT:
            fa = rec(frontA, ti + 1)
            P.merge([fa, rec(swa, ti + 1), rec(back, ti)], starts=[0.0, swa_start(fa), 0.0])
        else:
            back(ti)
    P.bankmap[:] = list(range(8))


    if phase2:
        P.barrier()
        for (tile_, src, key) in ((ffn_b, ffn_norm_d, "ffn_b"), (ple_b, ple_norm_d, "ple_b"),
                                  (fin_b, final_norm_d, "fin_b")):
            P.dma("sp", lambda e: e.dma_start(out=tile_[:], in_=src.partition_broadcast(128)), w=[key])
        P.dma("pool", lambda e: e.dma_start(out=keysT[:], in_=keysT_d.rearrange("g d n -> d g n")), w=["keysT"])
        for kc in range(2):
            P.dma("pool", lambda e, kc=kc: e.dma_start(out=proj_w[:, kc, :], in_=proj_d[kc * 128:(kc + 1) * 128, :]),
                  w=["proj_w"])
        for kc in range(8):
            P.dma("pool", lambda e, kc=kc: e.dma_start(out=arena[:, kc * 2048:(kc + 1) * 2048],
                                                       in_=wq_d[kc * 128:(kc + 1) * 128, :]), w=["arena"])
        for kc in range(8):
            P.dma("pool", lambda e, kc=kc: e.dma_start(out=arena[:, 16384 + kc * 1024:16384 + (kc + 1) * 1024],
                                                       in_=gate_d[kc * 128:(kc + 1) * 128, :]), w=["arena"])
        def head(ti):
            t0 = ti * 128
            pq = ti % 2
            x1c, hbfc, eidxc, gatesc = x1s[pq], hbfs[pq], eidxs[pq], gatess[pq]
            kx1, khbf, keidx, kgates = ("x1", pq), ("hbfp", pq), ("eidx", pq), ("gates", pq)
            P.dma("sp", lambda e: e.dma_start(out=x1c[:], in_=x1_d[t0:t0 + 128, :]), r=[("x1_d", ti)], w=[kx1])
            rmsnorm(x1c, kx1, ffn_b, "ffn_b", hbfc, khbf, 0)
            transpose8(hbfc, khbf, hT, "hT", 8, 0)
            for gq in range(4):
                bank = 1 + (gq % 2)
                for gi in range(4):
                    g = gq * 4 + gi
                    for kc in range(8):
                        P.op("pe", lambda e, g=g, gi=gi, kc=kc, bank=bank: e.matmul(
                            out=ps[bank][:, gi * 128:(gi + 1) * 128], lhsT=wq[:, kc, g * 128:(g + 1) * 128],
                            rhs=hT[:, kc, :], start=(kc == 0), stop=(kc == 7)),
                            r=["arena", "hT"], w=[("ps", bank)])
                eng = "act" if gq % 2 == 0 else "dve"
                if eng == "act":
                    P.op("act", lambda e, gq=gq, bank=bank: e.copy(
                        out=qT[:, gq * 4:(gq + 1) * 4, :].rearrange("p g t -> p (g t)"), in_=ps[bank][:]),
                        r=[("ps", bank)], w=[("qT", gq)])
                else:
                    P.op("dve", lambda e, gq=gq, bank=bank: e.tensor_copy(
                        out=qT[:, gq * 4:(gq + 1) * 4, :].rearrange("p g t -> p (g t)"), in_=ps[bank][:]),
                        r=[("ps", bank)], w=[("qT", gq)])
            for gq in range(4):
                bank = 1 + (gq % 2)
                for gi in range(4):
                    g = gq * 4 + gi
                    P.op("pe", lambda e, g=g, gi=gi, bank=bank: e.matmul(
                        out=ps[bank][:, gi * 128:(gi + 1) * 128], lhsT=qT[:, g, :], rhs=keysT[:, g, :],
                        start=True, stop=True),
                        r=[("qT", gq), "keysT"], w=[("ps", bank)])
                P.op("act", lambda e, gq=gq, bank=bank: e.copy(
                    out=sc[:, gq * 4:(gq + 1) * 4, :].rearrange("p g n -> p (g n)"), in_=ps[bank][:]),
                    r=[("ps", bank)], w=[("scg", gq * 4 + q_) for q_ in range(4)])
            if False:
                P.dma("sp", lambda e, t0=t0: e.dma_start(out=dbg_d["sc"][t0:t0 + 128, :],
                                                         in_=sc[:].rearrange("p g n -> p (g n)")),
                      r=[("sc", q) for q in range(4)], w=["dbg_sc"])
            for g in range(16):
                Kg = ("scg", g)
                P.op("dve", lambda e: e.max(out=top[:, g, 0:8], in_=sc[:, g, :]), r=[Kg], w=[("top", g, 0)])
                P.op("dve", lambda e: e.max_index(out=tidx[:, g, 0:8], in_max=top[:, g, 0:8], in_values=sc[:, g, :]),
                     r=[Kg, ("top", g, 0)], w=[("tidx", g, 0)])
                P.op("dve", lambda e: e.match_replace(out=sc[:, g, :], in_to_replace=top[:, g, 0:8],
                                                      in_values=sc[:, g, :], imm_value=-1e30),
                     r=[Kg, ("top", g, 0)], w=[Kg])
                P.op("dve", lambda e: e.max(out=top[:, g, 8:16], in_=sc[:, g, :]), r=[Kg], w=[("top", g, 1)])
                P.op("dve", lambda e: e.max_index(out=tidx[:, g, 8:16], in_max=top[:, g, 8:16],
                                                  in_values=sc[:, g, :]),
                     r=[Kg, ("top", g, 1)], w=[("tidx", g, 1)])
            sc_all = [("scg", g) for g in range(16)]
            topk_all = [("top", g, k) for g in range(16) for k in range(2)]
            tidx_all = [("tidx", g, k) for g in range(16) for k in range(2)]
            top4 = top[:].rearrange("p (h two) k -> p h two k", two=2)
            P.op("dve", lambda e: e.tensor_tensor(
                out=cand[:].rearrange("p h (a b) -> p h a b", b=16),
                in0=top4[:, :, 0, :].unsqueeze(3).to_broadcast([128, 8, 16, 16]),
                in1=top4[:, :, 1, :].unsqueeze(2).to_broadcast([128, 8, 16, 16]), op=ALU.add),
                r=topk_all, w=sc_all)
            P.op("dve", lambda e: e.tensor_copy(out=tidxf[:], in_=tidx[:]), r=tidx_all, w=["tidxf"])
            for h in range(8):
                Kc = [("scg", 2 * h), ("scg", 2 * h + 1)]
                P.op("dve", lambda e: e.max(out=best[:, h, 0:8], in_=cand[:, h, :]), r=Kc, w=[("best", h, 0)])
                P.op("dve", lambda e: e.max_index(out=bpos[:, h, 0:8], in_max=best[:, h, 0:8],
                                                  in_values=cand[:, h, :]),
                     r=Kc + [("best", h, 0)], w=[("bpos", h, 0)])
                P.op("dve", lambda e: e.match_replace(out=cand[:, h, :], in_to_replace=best[:, h, 0:8],
                                                      in_values=cand[:, h, :], imm_value=-1e30),
                     r=Kc + [("best", h, 0)], w=Kc)
                P.op("dve", lambda e: e.max(out=best[:, h, 8:16], in_=cand[:, h, :]), r=Kc, w=[("best", h, 1)])
                P.op("dve", lambda e: e.max_index(out=bpos[:, h, 8:16], in_max=best[:, h, 8:16],
                                                  in_values=cand[:, h, :]),
                     r=Kc + [("best", h, 1)], w=[("bpos", h, 1)])
            best_all = [("best", h, k) for h in range(8) for k in range(2)]
            bpos_all = [("bpos", h, k) for h in range(8) for k in range(2)]
            P.op("dve", lambda e: e.tensor_copy(out=bposf[:], in_=bpos[:]), r=bpos_all, w=["bposf"])
            bc_s = lambda t: t[:].unsqueeze(3).to_broadcast([128, 8, 16, 16])
            bc_c = lambda ap: ap.unsqueeze(1).unsqueeze(1).to_broadcast([128, 8, 16, 16])
            P.op("dve", lambda e: e.tensor_tensor(out=big4[:], in0=bc_s(bposf), in1=bc_c(iota16x16), op=ALU.is_ge),
                 r=["bposf", "consts"], w=sc_all)
            P.op("dve", lambda e: e.tensor_reduce(out=asel[:], in_=big4[:], axis=AX.X, op=ALU.add),
                 r=sc_all, w=["asel"])
            P.op("dve", lambda e: e.tensor_scalar(out=asel[:], in0=asel[:], scalar1=-1.0, scalar2=None, op0=ALU.add),
                 r=["asel"], w=["asel"])
            P.op("dve", lambda e: e.scalar_tensor_tensor(out=bsel[:], in0=asel[:], scalar=-16.0, in1=bposf[:],
                                                         op0=ALU.mult, op1=ALU.add),
                 r=["asel", "bposf"], w=["bsel"])
            tf4 = tidxf[:].rearrange("p (h two) k -> p h two k", two=2)
            for (sel, half, dst, dkey) in ((asel, 0, isel, "isel"), (bsel, 1, jsel, "jsel")):
                skey = "asel" if half == 0 else "bsel"
                P.op("dve", lambda e, sel=sel: e.tensor_tensor(out=big4[:], in0=bc_s(sel), in1=bc_c(iota16),
                                                               op=ALU.is_equal),
                     r=[skey, "consts"], w=sc_all)
                P.op("dve", lambda e, half=half: e.tensor_tensor(
                    out=big4[:], in0=big4[:],
                    in1=tf4[:, :, half, :].unsqueeze(2).to_broadcast([128, 8, 16, 16]), op=ALU.mult),
                    r=sc_all + ["tidxf"], w=sc_all)
                P.op("dve", lambda e, dst=dst: e.tensor_reduce(out=dst[:], in_=big4[:], axis=AX.X, op=ALU.add),
                     r=sc_all, w=[dkey])
            P.op("dve", lambda e: e.scalar_tensor_tensor(
                out=ef[:], in0=isel[:].rearrange("p h s -> p (h s)"), scalar=128.0,
                in1=jsel[:].rearrange("p h s -> p (h s)"), op0=ALU.mult, op1=ALU.add),
                r=["isel", "jsel"], w=["ef"])
            P.op("dve", lambda e: e.tensor_copy(out=eidxc[:], in_=ef[:]), r=["ef"], w=[keidx])
            P.op("dve", lambda e: e.tensor_tensor(out=gatesc[:], in0=best[:],
                                                  in1=best[:, :, 0:1].to_broadcast([128, 8, 16]), op=ALU.subtract),
                 r=best_all, w=[kgates])
            P.op("act", lambda e: e.activation(out=gatesc[:], in_=gatesc[:], func=AF.Exp), r=[kgates], w=[kgates])
            P.op("dve", lambda e: e.tensor_reduce(out=gz[:], in_=gatesc[:], axis=AX.X, op=ALU.add),
                 r=[kgates], w=["gz"])
            P.op("dve", lambda e: e.reciprocal(out=gz[:], in_=gz[:]), r=["gz"], w=["gz"])
            P.op("dve", lambda e: e.tensor_tensor(out=gatesc[:], in0=gatesc[:],
                                                  in1=gz[:].unsqueeze(2).to_broadcast([128, 8, 16]), op=ALU.mult),
                 r=[kgates, "gz"], w=[kgates])
            if False:
                P.dma("sp", lambda e, t0=t0: e.dma_start(out=dbg_d["eidx"][t0:t0 + 128, :], in_=eidx[:]),
                      r=["eidx"], w=["dbg_eidx"])
                P.dma("sp", lambda e, t0=t0: e.dma_start(out=dbg_d["gates"][t0:t0 + 128, :],
                                                         in_=gatesc[:].rearrange("p h s -> p (h s)")),
                      r=["gates"], w=["dbg_gates"])
        def gather(ti):
            pq = ti % 2
            accb = 5 if pq == 0 else 3
            x1c, hbfc, eidxc, gatesc = x1s[pq], hbfs[pq], eidxs[pq], gatess[pq]
            kx1, khbf, keidx, kgates = ("x1", pq), ("hbfp", pq), ("eidx", pq), ("gates", pq)
            def fin(g):
                gs = slice(g * GS, g * GS + GS)
                dk = (ti * (128 // GS) + g) % 2
                apk = [("actpre", q) for q in range(g * GS, g * GS + GS)]
                ga, gbb = gl_a[:, gs], gl_b[:, gs]
                gflat = gatesc[:].rearrange("p h s -> p (h s)")
                P.op("dve", lambda e: e.tensor_tensor(out=ga, in0=actpre[:, gs], in1=actpre[:, gs], op=ALU.mult),
                     r=apk, w=[("gla", g)])
                P.op("dve", lambda e: e.tensor_scalar(out=ga, in0=ga, scalar1=0.044715, scalar2=1.0,
                                                      op0=ALU.mult, op1=ALU.add), r=[("gla", g)], w=[("gla", g)])
                P.op("dve", lambda e: e.tensor_tensor(out=ga, in0=ga, in1=actpre[:, gs], op=ALU.mult),
                     r=[("gla", g)] + apk, w=[("gla", g)])
                P.op("act", lambda e: e.activation(out=gbb, in_=ga, func=AF.Sigmoid, scale=1.5957691216057308),
                     r=[("gla", g)], w=[("glb", g)])
                P.op("dve", lambda e: e.tensor_tensor(out=wts[:, gs], in0=actpre[:, gs], in1=gflat[:, gs],
                                                      op=ALU.mult), r=apk + [kgates], w=[("wts", g)])
                P.op("dve", lambda e: e.tensor_tensor(out=wts[:, gs], in0=wts[:, gs], in1=gbb, op=ALU.mult),
                     r=[("glb", g), ("wts", g)], w=[("wts", g)])
                for j in range(GS):
                    P.op("act", lambda e: e.activation(out=dg[dk][:, j, :], in_=ident_bf[:], func=AF.Copy,
                                                       scale=wts[:, g * GS + j:g * GS + j + 1]),
                         r=[("wts", g), "ident_bf"], w=[("dg", dk)])
                for j in range(GS):
                    sj = g * GS + j
                    bj = (ti * 128 + sj) % NG
                    for half in range(2):
                        mm(ps[accb + half][:], dg[dk][:, j, :], gb[bj][:, D + half * 512:D + (half + 1) * 512],
                           sj == 0, sj == 127, [("dg", dk), ("gb", bj)], accb + half)

            for s in range(128):
                gidx = ti * 128 + s
                b = gidx % NG
                pb_ = gidx % 3
                P.dma("pool", lambda e: e.indirect_dma_start(
                    out=gb[b], out_offset=None, in_=uvb_d,
                    in_offset=bass.IndirectOffsetOnAxis(ap=eidxc[:, s:s + 1], axis=0)),
                    r=[keidx], w=[("gb", b)])
                if s % 4 == 3:
                    P.op("dve", lambda e: e.scalar_tensor_tensor(
                        out=prod[pb_][:], in0=gb[b][:, 0:D], scalar=1.0, in1=hbfc[:], op0=ALU.mult, op1=ALU.mult,
                        accum_out=actpre[:, s:s + 1]),
                        r=[("gb", b), khbf], w=[("prod", pb_), ("actpre", s)])
                else:
                    P.op("dve", lambda e: e.tensor_tensor(out=prod[pb_][:], in0=gb[b][:, 0:D], in1=hbfc[:],
                                                          op=ALU.mult),
                         r=[("gb", b), khbf], w=[("prod", pb_)])
                    P.op("act", lambda e: e.activation(out=prod[pb_][:], in_=prod[pb_][:], func=AF.Copy,
                                                       accum_out=actpre[:, s:s + 1]),
                         r=[("prod", pb_)], w=[("prod", pb_), ("actpre", s)])
                if s % GS == GS - 1:
                    if s // GS >= 1:
                        fin(s // GS - 1)
            fin(128 // GS - 1)
            for half in range(2):
                P.op("dve", lambda e: e.tensor_tensor(out=y[:, half * 512:(half + 1) * 512],
                                                      in0=x1c[:, half * 512:(half + 1) * 512], in1=ps[accb + half][:],
                                                      op=ALU.add), r=[kx1, ("ps", accb + half)], w=["y"])
            if False:
                P.dma("sp", lambda e, t0=t0: e.dma_start(out=dbg_d["x2"][t0:t0 + 128, :], in_=y[:]),
                      r=["y"], w=["dbg_x2"])
        def tail(ti):
            t0 = ti * 128
            rmsnorm(y, "y", ple_b, "ple_b", tbf, "tbf", 2)
            transpose8(tbf, "tbf", hT, "hT", 8, 0)
            for half in range(2):
                bank = 1 + half
                for kc in range(8):
                    P.op("pe", lambda e, half=half, kc=kc, bank=bank: e.matmul(
                        out=ps[bank][:], lhsT=hT[:, kc, :], rhs=gate_w[:, kc, half * 512:(half + 1) * 512],
                        start=(kc == 0), stop=(kc == 7)), r=["hT", "arena"], w=[("ps", bank)])
                P.op("act", lambda e, half=half, bank=bank: e.activation(
                    out=sig[:, half * 512:(half + 1) * 512], in_=ps[bank][:], func=AF.Sigmoid),
                    r=[("ps", bank)], w=[("gb", 1)])
            P.dma("sp", lambda e, t0=t0: e.dma_start(out=pt[:], in_=p_d[t0:t0 + 128, :]), w=["pt"])
            P.op("act", lambda e: e.copy(out=ptbf[:], in_=pt[:]), r=["pt"], w=["ptbf"])
            transpose8(ptbf, "ptbf", pT, "pT", 2, 0)
            for half in range(2):
                bank = 1 + half
                for kc in range(2):
                    P.op("pe", lambda e, half=half, kc=kc, bank=bank: e.matmul(
                        out=ps[bank][:], lhsT=pT[:, kc, :], rhs=proj_w[:, kc, half * 512:(half + 1) * 512],
                        start=(kc == 0), stop=(kc == 1)), r=["pT", "proj_w"], w=[("ps", bank)])
                P.op("dve", lambda e, half=half, bank=bank: e.tensor_tensor(
                    out=x3[:, half * 512:(half + 1) * 512], in0=sig[:, half * 512:(half + 1) * 512],
                    in1=ps[bank][:], op=ALU.mult), r=[("gb", 1), ("ps", bank)], w=[("gb", 2)])
            P.op("dve", lambda e: e.tensor_tensor(out=x3[:], in0=x3[:], in1=y[:], op=ALU.add),
                 r=[("gb", 2), "y"], w=[("gb", 2)])
            rmsnorm(x3, ("gb", 2), fin_b, "fin_b", outt, ("gb", 3), 4)
            P.dma("sp", lambda e, t0=t0: e.dma_start(out=out_d[t0:t0 + 128, :], in_=outt[:]),
                  r=[("gb", 3)], w=["out_d"])
        head(0)
        for ti in range(NT):
            P.begin()
            gather(ti)
            G = P.end()
            H = []
            if ti + 1 < NT:
                P.begin()
                head(ti + 1)
                H = P.end()
            P.merge([G, H], [1.0, 0.95])
            tail(ti)
    P.op("sp", None, r=["out_d"] + [("x1_d", i) for i in range(NT)], w=[])
    P.emit(es)
    es.close()
    return nc


def kernel(**inputs):
    NT = SEQ // 128
    nc = bass.Bass("TRN2", target_bir_lowering=False)
    build(nc, NT)
    shared = core_inputs(inputs, 0)
    in_maps = []
    for b in range(8):
        m = dict(shared)
        m["x"] = np.ascontiguousarray(np.asarray(inputs["x"][b], dtype=np.float32))
        m["p"] = np.ascontiguousarray(np.asarray(inputs["p"][0, b], dtype=np.float32))
        m["positions"] = np.ascontiguousarray(np.asarray(inputs["positions"][b], dtype=np.int32))
        in_maps.append(m)
    res = run_bass_kernel_spmd(nc, in_maps, core_ids=list(range(8)))
    return np.stack([np.asarray(r["out"], dtype=np.float32) for r in res.results], axis=0)


def core_inputs(inputs, b, T=SEQ):
    f = lambda a: np.ascontiguousarray(np.asarray(a, dtype=np.float32))
    w_in = np.asarray(inputs["w_in"][0], dtype=np.float32)
    o_sq = 4 * 512 + 8
    swq = w_in[:, o_sq:o_sq + 512].reshape(D, 8, 64)
    swq_p = np.stack([np.concatenate([swq[:, c], swq[:, 4 + c]], axis=1) for c in range(4)], axis=1).reshape(D, 512)
    w_in_p = np.concatenate([w_in[:, :o_sq], swq_p, w_in[:, o_sq + 512:]], axis=1)
    convT = np.asarray(inputs["conv_w"][0], dtype=np.float32).reshape(4, 12, 128).transpose(2, 1, 0).reshape(128, 48)
    return {
        "x": f(inputs["x"][b, :T]),
        "p": f(inputs["p"][0, b, :T]),
        "positions": np.ascontiguousarray(np.asarray(inputs["positions"][b, :T], dtype=np.int32)),
        "consts": make_consts(),
        "mix_norm": f(inputs["mix_norm"][0]),
        "w_in": f(w_in_p),
        "convT": f(convT),
        "dn_dt_bias": f(inputs["dn_dt_bias"][0]),
        "dn_a_log": f(inputs["dn_a_log"][0]),
        "dn_out_norm": f(inputs["dn_out_norm"][0]),
        "attn_sinks": f(inputs["attn_sinks"][0]),
        "w_out": f(inputs["w_out"][0]),
        "ffn_norm": f(inputs["ffn_norm"][0]),
        "ple_norm": f(inputs["ple_norm"][0]),
        "final_norm": f(inputs["final_norm"]),
        "peer_wq": f(inputs["peer_wq"][0]),
        "peer_keysT": f(np.asarray(inputs["peer_keys"][0]).reshape(16, 128, 128).transpose(0, 2, 1)),
        "peer_uv": f(np.concatenate([np.asarray(inputs["peer_u"][0], dtype=np.float32),
                                     np.asarray(inputs["peer_v"][0], dtype=np.float32)], axis=1)),
        "ple_gate": f(inputs["ple_gate"][0]),
        "ple_proj": f(inputs["ple_proj"][0]),
    }
```

```python
import numpy as np
from contextlib import ExitStack
import concourse.bass as bass
import concourse.mybir as mybir
from concourse.bass_utils import run_bass_kernel_spmd

F32 = mybir.dt.float32
BF16 = mybir.dt.bfloat16
I32 = mybir.dt.int32
U32 = mybir.dt.uint32
ALU = mybir.AluOpType
AF = mybir.ActivationFunctionType
AX = mybir.AxisListType

D = 1024
SEQ = 4096
EPS = 1e-6
UVB_KIND = "Internal"


class _Op:
    __slots__ = ("eng", "fn", "is_dma", "deps", "signal", "tok", "dsem", "dval")

    def __init__(self, eng, fn, is_dma):
        self.eng = eng
        self.fn = fn
        self.is_dma = is_dma
        self.deps = []
        self.signal = False
        self.tok = None
        self.dsem = None
        self.dval = 0


class _Rec:
    def __init__(self):
        self.call = None

    def __getattr__(self, name):
        def f(*a, **k):
            self.call = (name, a, k)
            return self
        return f


class Prog:
    ENGS = ("pe", "act", "dve", "pool", "sp")
    NDMA = {"sp": 24, "act": 8, "pool": 40}

    def __init__(self, nc):
        self.nc = nc
        self.ops = {e: [] for e in self.ENGS}
        self.res = {}
        self.dk = {e: 0 for e in self.ENGS}
        self.last_dma = {}
        self.pending = None
        self.bankmap = list(range(8))

    def _add(self, eng, fn, r, w, is_dma):
        bm = self.bankmap
        r = [(("ps", bm[k[1]]) if (isinstance(k, tuple) and len(k) == 2 and k[0] == "ps") else k) for k in r]
        w = [(("ps", bm[k[1]]) if (isinstance(k, tuple) and len(k) == 2 and k[0] == "ps") else k) for k in w]
        if fn is not None:
            rec = _Rec()
            fn(rec)
            name, a, k = rec.call
            fn = (lambda engine, name=name, a=a, k=k: getattr(engine, name)(*a, **k))
        if self.pending is not None:
            self.pending.append((eng, fn, list(r), list(w), is_dma))
            return None
        return self._commit(eng, fn, r, w, is_dma)

    def begin(self):
        self.pending = []

    def end(self):
        lst, self.pending = self.pending, None
        return lst

    def merge(self, lists, spans=None, starts=None):
        items = []
        for li, lst in enumerate(lists):
            span = 1.0 if spans is None else spans[li]
            st = 0.0 if starts is None else starts[li]
            for j, it in enumerate(lst):
                items.append((st + (j + 0.5) / len(lst) * (span - st), li, j, it))
        items.sort(key=lambda t: t[:3])
        for _, _, _, (eng, fn, r, w, is_dma) in items:
            self._commit(eng, fn, r, w, is_dma)

    def _commit(self, eng, fn, r, w, is_dma):
        op = _Op(eng, fn, is_dma)
        if is_dma:
            op.dval = self.dk[eng] % self.NDMA[eng]
            self.dk[eng] += 1
            self.last_dma[(eng, op.dval)] = op
        deps = []
        for k in r:
            st = self.res.get(k)
            if st is not None and st[0] is not None:
                deps.append(st[0])
        for k in w:
            st = self.res.get(k)
            if st is not None:
                if st[0] is not None:
                    deps.append(st[0])
                deps.extend(st[1])
        for k in r:
            st = self.res.get(k)
            if st is None:
                st = self.res[k] = [None, []]
            st[1].append(op)
        for k in w:
            self.res[k] = [op, []]
        seen = set()
        for d in deps:
            if d is op or id(d) in seen:
                continue
            seen.add(id(d))
            if d.eng == "pe" and eng == "pe" and not d.is_dma and not is_dma:
                continue
            op.deps.append(d)
            d.signal = True
        self.ops[eng].append(op)
        return op

    def op(self, eng, fn, r=(), w=()):
        return self._add(eng, fn, r, w, False)

    def dma(self, q, fn, r=(), w=()):
        return self._add(q, fn, r, w, True)

    def barrier(self):
        lasts = list(self.last_dma.values())
        for e in self.ENGS:
            for op in reversed(self.ops[e]):
                if not op.is_dma and op.fn is not None:
                    lasts.append(op)
                    break
        for d in lasts:
            d.signal = True
        for e in self.ENGS:
            op = _Op(e, None, False)
            op.deps = list(lasts)
            self.ops[e].append(op)

    def emit(self, es):
        nc = self.nc
        csem = {e: es.enter_context(nc.semaphore("c_" + e)) for e in self.ENGS}
        dsem = {e: [es.enter_context(nc.semaphore("d_%s_%d" % (e, i))) for i in range(n)]
                for e, n in self.NDMA.items()}
        for e in self.ENGS:
            cnt = 0
            k = 0
            vals = [0] * self.NDMA.get(e, 0)
            for op in self.ops[e]:
                if op.is_dma:
                    s = op.dval
                    op.dval = vals[s]
                    vals[s] += 16
                    op.dsem = dsem[e][s]
                    op.tok = (op.dsem, vals[s])
                elif op.signal:
                    cnt += 1
                    op.tok = (csem[e], cnt)
        ops = self.ops

        def make(e):
            def body(engine):
                waited = {}
                for op in ops[e]:
                    waits = [d.tok for d in op.deps]
                    if op.is_dma and op.dval > 0:
                        waits.append((op.dsem, op.dval))
                    for (s, v) in waits:
                        if waited.get(s, 0) >= v:
                            continue
                        engine.wait_ge(s, v)
                        waited[s] = v
                    if op.fn is None:
                        continue
                    ins = op.fn(engine)
                    if op.is_dma:
                        ins.then_inc(op.dsem, 16)
                    elif op.signal:
                        ins.then_inc(csem[e], 1)
            return body

        with nc.Block() as block:
            block.tensor(make("pe"))
            block.scalar(make("act"))
            block.vector(make("dve"))
            block.gpsimd(make("pool"))
            block.sync(make("sp"))


C_ID = 0
C_IOTA16 = 128
C_IOTA16x16 = 144
C_TRI = 160
C_NEGI = 288
C_NEGS = 416
C_MCUR = 544
C_MPREV = 672
C_PERM = 800
C_INVF = 928
C_END = 932
NEG = -30000.0

W_Q, W_K, W_V, W_Z, W_B, W_A, W_SQ, W_SK, W_SV, W_COLS = 0, 512, 1024, 1536, 2048, 2052, 2056, 2568, 2696, 2824


def make_consts():
    c = np.zeros((128, C_END), np.float32)
    i = np.arange(128)
    c[:, C_ID:C_ID + 128] = np.eye(128, dtype=np.float32)
    c[:, C_IOTA16:C_IOTA16 + 16] = np.arange(16, dtype=np.float32)[None, :]
    c[:, C_IOTA16x16:C_IOTA16x16 + 16] = (16.0 * np.arange(16, dtype=np.float32))[None, :]
    le = (i[:, None] <= i[None, :])
    lt = (i[:, None] < i[None, :])
    c[:, C_TRI:C_TRI + 128] = le.astype(np.float32)
    c[:, C_NEGI:C_NEGI + 128] = np.where(le, 0.0, NEG)
    c[:, C_NEGS:C_NEGS + 128] = np.where(lt, 0.0, NEG)
    c[:, C_MCUR:C_MCUR + 128] = le.astype(np.float32)
    c[:, C_MPREV:C_MPREV + 128] = (~le).astype(np.float32)
    perm = np.zeros((128, 128), np.float32)
    invf = np.zeros((128,), np.float32)
    for m in range(128):
        d = m % 64
        if d < 8:
            perm[m + 8, m] = -1.0
        elif d < 16:
            perm[m - 8, m] = 1.0
        if d < 16:
            invf[m] = np.float32(500000.0) ** np.float32(-(2.0 * (d % 8)) / 16.0)
    c[:, C_PERM:C_PERM + 128] = perm
    c[:, C_INVF] = invf
    return c


class _Cut(Exception):
    pass


def build(nc, NT, dbg=False, cut=99, phase2=True):
    T = NT * 128
    P = Prog(nc)
    es = ExitStack()
    TWO_PI = 6.283185307179586
    PI = 3.141592653589793

    def dram(name, shape, dt, kind):
        return nc.dram_tensor(name, list(shape), dt, kind=kind).ap()

    x_d = dram("x", [T, D], F32, "ExternalInput")
    p_d = dram("p", [T, 256], F32, "ExternalInput")
    pos_d = dram("positions", [T], I32, "ExternalInput")
    consts_d = dram("consts", [128, C_END], F32, "ExternalInput")
    mix_norm_d = dram("mix_norm", [D], F32, "ExternalInput")
    win_d = dram("w_in", [D, W_COLS], F32, "ExternalInput")
    convT_d = dram("convT", [128, 48], F32, "ExternalInput")
    dtb_d = dram("dn_dt_bias", [4], F32, "ExternalInput")
    alog_d = dram("dn_a_log", [4], F32, "ExternalInput")
    onorm_d = dram("dn_out_norm", [128], F32, "ExternalInput")
    sinks_d = dram("attn_sinks", [8], F32, "ExternalInput")
    wout_d = dram("w_out", [D, D], F32, "ExternalInput")
    ffn_norm_d = dram("ffn_norm", [D], F32, "ExternalInput")
    ple_norm_d = dram("ple_norm", [D], F32, "ExternalInput")
    final_norm_d = dram("final_norm", [D], F32, "ExternalInput")
    wq_d = dram("peer_wq", [D, 2048], F32, "ExternalInput")
    keysT_d = dram("peer_keysT", [16, 128, 128], F32, "ExternalInput")
    uv_d = dram("peer_uv", [16384, 2 * D], F32, "ExternalInput")
    uvb_d = dram("peer_uvb", [16384, 2 * D], BF16, UVB_KIND)
    gate_d = dram("ple_gate", [D, D], F32, "ExternalInput")
    proj_d = dram("ple_proj", [256, D], F32, "ExternalInput")
    out_d = dram("out", [T, D], F32, "ExternalOutput")
    x1_d = dram("x1_scratch", [T, D], F32, "ExternalOutput" if dbg else "Internal")
    dbg_d = {}

    def sb(name, shape, dt):
        return es.enter_context(nc.sbuf_tensor("s_" + name, list(shape), dt))

    ps = [es.enter_context(nc.psum_tensor("ps%d" % i, [128, 512], F32)) for i in range(8)]
    PRIV_WORDS = 36378
    priv = sb("priv", [128, PRIV_WORDS], F32)
    cur = [0]

    def pv(name, shape, dt):
        nel = int(np.prod(shape[1:]))
        esz = 4 if dt in (F32, I32, U32) else 2
        nw = (nel * esz + 3) // 4
        assert cur[0] + nw <= PRIV_WORDS, (name, cur[0], nw)
        ap = priv[:, cur[0]:cur[0] + nw]
        cur[0] += nw
        if dt != F32:
            ap = ap.bitcast(dt)
        ap = ap[:, 0:nel]
        if len(shape) == 3:
            ap = ap.rearrange("p (a b) -> p a b", a=shape[1])
        elif len(shape) == 4:
            ap = ap.rearrange("p (a b c) -> p a b c", a=shape[1], b=shape[2])
        return ap

    consts = sb("consts", [128, C_END], F32)
    ident_bf = sb("ident_bf", [128, 128], BF16)
    ones_bf = sb("ones_bf", [128, 128], BF16)
    perm_bf = sb("perm_bf", [128, 128], BF16)
    mcur_bf = sb("mcur_bf", [128, 128], BF16)
    mprev_bf = sb("mprev_bf", [128, 128], BF16)
    arena = sb("arena", [128, 24576], BF16)
    convT = sb("convT", [128, 12, 4], F32)
    dtb_b = sb("dtb_b", [128, 4], F32)
    nea_b = sb("nea_b", [128, 4], F32)
    onorm_b = sb("onorm_b", [128, 128], F32)
    esink_b = sb("esink_b", [128, 8], F32)

    win = arena[:, 0:8 * W_COLS].rearrange("p (k c) -> p k c", k=8)
    wq = arena[:, 0:16384].rearrange("p (k c) -> p k c", k=8)
    gate_w = arena[:, 16384:24576].rearrange("p (k c) -> p k c", k=8)

    x1 = sb("x1t", [128, D], F32)
    y = sb("y", [128, D], F32)
    ss = sb("ss", [128, 8], F32)
    hbf = sb("hbf", [128, D], BF16)
    hT = sb("hT", [128, 8, 128], BF16)
    wout = pv("wout", [128, 8, D], BF16)
    cvb = [pv("cvb%d" % i, [128, D], BF16) for i in range(2)]
    mix_b = pv("mix_b", [128, D], F32)
    zq = pv("zq", [128, 12, 131], F32)
    cacc = pv("cacc", [128, 12, 128], F32)
    qkvc = pv("qkvc", [128, 12, 128], F32)
    sqbf = pv("sqbf", [128, 8, 128], BF16)
    rs8 = pv("rs8", [128, 8, 128], F32)
    qTn = pv("qTn", [128, 4, 128], BF16)
    kTn = pv("kTn", [128, 4, 128], BF16)
    vTb = pv("vTb", [128, 4, 128], BF16)
    bv = pv("bv", [128, 4, 128], BF16)
    tpb = pv("tpb", [128, 4, 128], BF16)
    tpb2 = pv("tpb2", [128, 4, 128], BF16)
    kd = pv("kd", [128, 4, 128], BF16)
    qgT = pv("qgT", [128, 4, 128], BF16)
    g8 = pv("g8", [128, 24], F32)
    q8 = pv("q8", [128, 8], F32)
    nq8 = pv("nq8", [128, 4], F32)
    qrep = pv("qrep", [128, 8, 128], F32)
    egl = pv("egl", [128, 4], F32)
    kds = pv("kds", [128, 4], F32)
    nbeg = pv("nbeg", [128, 4], F32)
    beta = pv("beta", [128, 4], F32)
    egcrow = pv("egcrow", [128, 4, 128], F32)
    tmpI = pv("tmpI", [128, 4, 128], F32)
    tmpS = pv("tmpS", [128, 4, 128], F32)
    DTI = pv("DTI", [128, 4, 128], F32)
    EB = pv("EB", [128, 4, 128], F32)
    aqkT = pv("aqkT", [128, 4, 128], BF16)
    Wm = [pv("Wm%d" % i, [128, 4, 128], BF16) for i in range(2)]
    Vm = [pv("Vm%d" % i, [128, 4, 128], BF16) for i in range(2)]
    VI = pv("VI", [128, 4, 128], BF16)
    Pm = [pv("Pm%d" % i, [128, 4, 128], BF16) for i in range(2)]
    Sst = [pv("Sst%d" % i, [128, 4, 128], BF16) for i in range(2)]
    rr = pv("rr", [128, 4, 128], BF16)
    vnew = pv("vnew", [128, 4, 128], BF16)
    osq = pv("osq", [128, 4, 128], F32)
    o4 = pv("o4", [128, 8], F32)
    ot = pv("ot", [128, 4, 128], F32)
    zs = pv("zs", [128, 512], F32)
    mix = pv("mix", [128, D], BF16)
    mixT = pv("mixT", [128, 8, 128], BF16)
    qraw = pv("qraw", [128, 5, 128], F32)
    qrb = pv("qrb", [128, 5, 128], BF16)
    qrot = pv("qrot", [128, 5, 128], BF16)
    qtmp = pv("qtmp", [128, 5, 128], F32)
    krot = [pv("krot%d" % i, [128, 128], BF16) for i in range(2)]
    v1 = [pv("v1_%d" % i, [128, 2, 65], BF16) for i in range(2)]
    posi = pv("posi", [128, 128], I32)
    ang = pv("ang", [128, 2, 128], F32)
    angk = pv("angk", [128, 2, 128], I32)
    angf = pv("angf", [128, 2, 128], F32)
    sincos = pv("sincos", [128, 2, 128], F32)
    Ecur = [pv("Ecur%d" % i, [128, 512], BF16) for i in range(2)]
    Eprev = [pv("Eprev%d" % i, [128, 512], BF16) for i in range(2)]
    den = pv("den", [128, 8], F32)
    x1I = [x1, pv("x1I2", [128, D], F32)]
    zsp = [zs, pv("zs2", [128, 512], F32)]
    mixp = [mix, pv("mix2", [128, D], BF16)]
    kTnp = [kTn, pv("kTn2", [128, 4, 128], BF16)]
    qgTp = [qgT, pv("qgT2", [128, 4, 128], BF16)]
    kdp = [kd, pv("kd2", [128, 4, 128], BF16)]
    bvp = [bv, pv("bv2", [128, 4, 128], BF16)]
    aqkTp = [aqkT, pv("aqkT2", [128, 4, 128], BF16)]
    nbegp = [nbeg, pv("nbeg2", [128, 4], F32)]
    eglp = [egl, pv("egl2", [128, 4], F32)]
    PTp = [pv("PTp%d" % i, [128, 4, 128], BF16) for i in range(2)]
    if dbg:
        print("priv words phase I:", cur[0])
    cur[0] = 0
    keysT = pv("keysT", [128, 16, 128], BF16)
    ffn_b = pv("ffn_b", [128, D], F32)
    ple_b = pv("ple_b", [128, D], F32)
    fin_b = pv("fin_b", [128, D], F32)
    proj_w = pv("proj_w", [128, 2, D], BF16)
    qT = pv("qT", [128, 16, 128], BF16)
    sc = pv("sc", [128, 16, 128], F32)
    sc2 = sc
    top = pv("top", [128, 16, 16], F32)
    tidx = pv("tidx", [128, 16, 16], U32)
    tidxf = pv("tidxf", [128, 16, 16], F32)
    cand = sc2.rearrange("p (h two) k -> p h (two k)", two=2)
    cand2 = cand
    best = pv("best", [128, 8, 16], F32)
    bpos = pv("bpos", [128, 8, 16], U32)
    bposf = pv("bposf", [128, 8, 16], F32)
    big4 = cand2.rearrange("p h (a b) -> p h a b", b=16)
    asel = pv("asel", [128, 8, 16], F32)
    bsel = pv("bsel", [128, 8, 16], F32)
    isel = pv("isel", [128, 8, 16], F32)
    jsel = pv("jsel", [128, 8, 16], F32)
    ef = pv("ef", [128, 128], F32)
    eidx = pv("eidx", [128, 128], I32)
    gates = pv("gates", [128, 8, 16], F32)
    gz = pv("gz", [128, 8], F32)
    actpre = pv("actpre", [128, 128], F32)
    gl_a = pv("gl_a", [128, 128], F32)
    gl_b = pv("gl_b", [128, 128], F32)
    wts = pv("wts", [128, 128], F32)
    NG = 20
    GS = 4
    gbw = [pv("gb%d" % i, [128, D], F32) for i in range(NG)]
    gb = [w_.bitcast(BF16) for w_ in gbw]
    prod = [pv("prod%d" % i, [128, D], BF16) for i in range(3)]
    dg = [pv("dg%d" % i, [128, 4, 128], BF16) for i in range(2)]
    pt = pv("pt", [128, 256], F32)
    ptbf = pv("ptbf", [128, 256], BF16)
    pT = pv("pT", [128, 2, 128], BF16)
    if dbg:
        print("priv words phase II:", cur[0])
    x1s = [x1, pv("x1b", [128, D], F32)]
    hbfs = [hbf, pv("hbf2", [128, D], BF16)]
    eidxs = [eidx, pv("eidx2", [128, 128], I32)]
    gatess = [gates, pv("gates2", [128, 8, 16], F32)]
    tbf = pv("tbf", [128, D], BF16)
    sig = gbw[1]
    x3 = gbw[2]
    outt = gbw[3]

    def psm(i):
        return ps[P.bankmap[i]]

    def psbf(i):
        return psm(i)[:].bitcast(BF16)

    def ps4(i):
        return psm(i)[:].rearrange("p (h t) -> p h t", h=4)

    def cst(off, n=128):
        return consts[:, off:off + n]

    P.dma("sp", lambda e: e.dma_start(out=consts[:], in_=consts_d), w=["consts"])
    for (tile_, src, key) in ((mix_b, mix_norm_d, "mix_b"),
                              (dtb_b, dtb_d, "dtb_b"), (nea_b, alog_d, "nea_b"),
                              (onorm_b, onorm_d, "onorm_b"), (esink_b, sinks_d, "esink_b")):
        P.dma("sp", lambda e, tile_=tile_, src=src: e.dma_start(out=tile_[:], in_=src.partition_broadcast(128)),
              w=[key])
    P.dma("sp", lambda e: e.dma_start(out=convT[:].rearrange("p c j -> p (c j)"), in_=convT_d), w=["convT"])
    for kc in range(8):
        P.dma("pool", lambda e, kc=kc: e.dma_start(out=arena[:, kc * W_COLS:(kc + 1) * W_COLS],
                                                   in_=win_d[kc * 128:(kc + 1) * 128, :]), w=["arena"])
    for kc in range(8):
        P.dma("pool", lambda e, kc=kc: e.dma_start(out=wout[:, kc, :], in_=wout_d[kc * 128:(kc + 1) * 128, :]),
              w=["wout"])
    P.op("dve", lambda e: e.tensor_copy(out=ident_bf[:], in_=cst(C_ID)), r=["consts"], w=["ident_bf"])
    P.op("dve", lambda e: e.tensor_copy(out=perm_bf[:], in_=cst(C_PERM)), r=["consts"], w=["perm_bf"])
    P.op("dve", lambda e: e.tensor_copy(out=mcur_bf[:], in_=cst(C_MCUR)), r=["consts"], w=["mcur_bf"])
    P.op("dve", lambda e: e.tensor_copy(out=mprev_bf[:], in_=cst(C_MPREV)), r=["consts"], w=["mprev_bf"])
    P.op("dve", lambda e: e.memset(ones_bf[:], 1.0), w=["ones_bf"])
    P.op("dve", lambda e: e.memset(zq[:], 0.0), w=["zq"])
    P.op("dve", lambda e: e.memset(Sst[0][:], 0.0), w=[("S", 0)])
    for i in range(2):
        P.op("dve", lambda e, i=i: e.memset(v1[i][:], 1.0), w=[("v1", i)])
    P.op("act", lambda e: e.activation(out=nea_b[:], in_=nea_b[:], func=AF.Exp), r=["nea_b"], w=["nea_b"])
    P.op("dve", lambda e: e.tensor_scalar(out=nea_b[:], in0=nea_b[:], scalar1=-1.0, scalar2=None, op0=ALU.mult),
         r=["nea_b"], w=["nea_b"])
    P.op("act", lambda e: e.activation(out=esink_b[:], in_=esink_b[:], func=AF.Exp), r=["esink_b"], w=["esink_b"])

    iota16 = consts[:, C_IOTA16:C_IOTA16 + 16]
    iota16x16 = consts[:, C_IOTA16x16:C_IOTA16x16 + 16]

    def rmsnorm(src, src_key, gain_b, gain_key, dst_f32, dst_key, col, dst_bf=None, dst_bf_key=None):
        P.op("act", lambda e: e.activation(out=dst_f32[:], in_=src[:], func=AF.Square,
                                           accum_out=ss[:, col:col + 1]),
             r=[src_key], w=[dst_key, ("ss", col)])
        P.op("act", lambda e: e.activation(out=ss[:, col + 1:col + 2], in_=ss[:, col:col + 1], func=AF.Sqrt,
                                           scale=1.0 / D, bias=EPS),
             r=[("ss", col)], w=[("ss", col + 1)])
        P.op("dve", lambda e: e.reciprocal(out=ss[:, col + 1:col + 2], in_=ss[:, col + 1:col + 2]),
             r=[("ss", col + 1)], w=[("ss", col + 1)])
        P.op("dve", lambda e: e.scalar_tensor_tensor(out=dst_f32[:], in0=src[:], scalar=ss[:, col + 1:col + 2],
                                                     in1=gain_b[:], op0=ALU.mult, op1=ALU.mult),
             r=[src_key, ("ss", col + 1), gain_key], w=[dst_key])
        if dst_bf is not None:
            P.op("act", lambda e: e.copy(out=dst_bf[:], in_=dst_f32[:]), r=[dst_key], w=[dst_bf_key])

    def transpose8(src_bf, src_key, dst, dst_key, nch, bank):
        pb = psbf(bank)
        for c in range(nch):
            P.op("pe", lambda e, c=c: e.transpose(out=pb[:, c * 128:(c + 1) * 128],
                                                  in_=src_bf[:, c * 128:(c + 1) * 128], identity=ident_bf[:]),
                 r=(list(src_key) if isinstance(src_key, list) else [src_key]) + ["ident_bf"], w=[("ps", bank)])
        P.op("dve", lambda e: e.tensor_copy(out=dst[:].rearrange("p c t -> p (c t)"),
                                            in_=pb[:, 0:nch * 128]),
             r=[("ps", bank)], w=[dst_key])

    def mm(out, lhsT, rhs, start, stop, r, bank):
        P.op("pe", lambda e: e.matmul(out=out, lhsT=lhsT, rhs=rhs, start=start, stop=stop),
             r=r, w=[("ps", bank)])

    def dve(fn, r, w):
        P.op("dve", fn, r=r, w=w)

    def ck(n):
        if cut == n:
            raise _Cut()

    def act(fn, r, w):
        P.op("act", fn, r=r, w=w)

    def proj_fm(col, bank, slot):
        for kc in range(8):
            mm(psm(bank)[:, slot * 128:(slot + 1) * 128], win[:, kc, col:col + 128], hT[:, kc, :],
               kc == 0, kc == 7, ["arena", "hT"], bank)

    def frontA(ti):
        P.bankmap[:] = [0, 1, 2, 0, 1, 0, 1, 2]
        t0 = ti * 128
        par = ti % 2
        x1, zs, mix = x1I[par], zsp[par], mixp[par]
        kTn, qgT, kd, bv, aqkT = kTnp[par], qgTp[par], kdp[par], bvp[par], aqkTp[par]
        nbeg, egl = nbegp[par], eglp[par]
        for cc in range(256 // NT):
            ci = ti * (256 // NT) + cc
            rows = slice((ci // 2) * 128, (ci // 2) * 128 + 128)
            cols = slice((ci % 2) * D, (ci % 2) * D + D)
            cb = ci % 2
            P.dma("pool", lambda e: e.dma_start(out=cvb[cb][:], in_=uv_d[rows, cols]), w=[("cvb", cb)])
            P.dma("pool", lambda e: e.dma_start(out=uvb_d[rows, cols], in_=cvb[cb][:]), r=[("cvb", cb)],
                  w=[("uvb", ci)])
        P.dma("sp", lambda e, t0=t0: e.dma_start(out=x1[:], in_=x_d[t0:t0 + 128, :]), w=[("x1", par)])
        P.dma("sp", lambda e, t0=t0: e.dma_start(out=posi[:], in_=pos_d[t0:t0 + 128].partition_broadcast(128)),
              w=["posi"])
        rmsnorm(x1, ("x1", par), mix_b, "mix_b", hbf, "hbf", 0)
        transpose8(hbf, "hbf", hT, "hT", 8, 0)

        for grp in range(3):
            bank = 1 + grp
            for s4 in range(4):
                proj_fm((grp * 4 + s4) * 128, bank, s4)
            act(lambda e, grp=grp, bank=bank: e.copy(out=zq[:, grp * 4:(grp + 1) * 4, 3:131], in_=ps4(bank)),
                [("ps", bank)], ["zq"])
        for kc in range(8):
            mm(psm(6)[:, 0:512], hT[:, kc, :], win[:, kc, W_Z:W_Z + 512], kc == 0, kc == 7, ["arena", "hT"], 6)
        for kc in range(8):
            mm(psm(7)[:, 0:8], hT[:, kc, :], win[:, kc, W_B:W_B + 8], kc == 0, kc == 7, ["arena", "hT"], 7)
        act(lambda e: e.activation(out=zs[:], in_=psm(6)[:], func=AF.Silu), [("ps", 6)], [("zs", par)])
        act(lambda e: e.activation(out=beta[:], in_=psm(7)[:, 0:4], func=AF.Sigmoid), [("ps", 7)], ["beta"])
        dve(lambda e: e.tensor_tensor(out=g8[:, 4:8], in0=dtb_b[:], in1=psm(7)[:, 4:8], op=ALU.add),
            [("ps", 7), "dtb_b"], ["g8a"])
        act(lambda e: e.activation(out=g8[:, 8:12], in_=g8[:, 4:8], func=AF.Abs), ["g8a"], ["g8b"])
        act(lambda e: e.activation(out=g8[:, 12:16], in_=g8[:, 8:12], func=AF.Exp, scale=-1.0), ["g8b"], ["g8c"])
        act(lambda e: e.activation(out=g8[:, 12:16], in_=g8[:, 12:16], func=AF.Ln, bias=1.0), ["g8c"], ["g8c"])
        act(lambda e: e.activation(out=g8[:, 20:24], in_=beta[:], func=AF.Ln), ["beta"], ["g8e"])
        dve(lambda e: e.scalar_tensor_tensor(out=g8[:, 16:20], in0=g8[:, 4:8], scalar=0.0, in1=g8[:, 12:16],
                                             op0=ALU.max, op1=ALU.add), ["g8a", "g8c"], ["g8d"])
        dve(lambda e: e.tensor_tensor(out=g8[:, 16:20], in0=g8[:, 16:20], in1=nea_b[:], op=ALU.mult),
            ["g8d", "nea_b"], ["g8d"])
        mm(psm(7)[:, 256:260], cst(C_TRI), g8[:, 16:20], True, True, ["consts", "g8d"], 7)
        dve(lambda e: e.tensor_copy(out=q8[:, 0:4], in_=psm(7)[:, 256:260]), [("ps", 7)], ["q8a"])
        dve(lambda e: e.tensor_tensor(out=q8[:, 4:8], in0=q8[:, 0:4], in1=g8[:, 20:24], op=ALU.add),
            ["q8a", "g8e"], ["q8b"])
        dve(lambda e: e.tensor_copy(out=qrep[:], in_=q8[:].unsqueeze(2).to_broadcast([128, 8, 128])),
            ["q8a", "q8b"], ["qrep"])
        for h in range(4):
            mm(psm(6)[:, h * 128:(h + 1) * 128], qrep[:, h, :], cst(C_ID), True, True, ["qrep", "consts", ("zs", par)], 6)
        for h in range(4):
            mm(psm(7)[:, h * 128:(h + 1) * 128], qrep[:, 4 + h, :], cst(C_ID), True, True,
               ["qrep", "consts", "q8a", ("v1", par)], 7)
        gl_last = ps4(6)[:, :, 127]
        act(lambda e: e.activation(out=egl[:], in_=gl_last, func=AF.Exp), [("ps", 6)], [("egl", par)])
        dve(lambda e: e.tensor_tensor(out=kds[:], in0=q8[:, 0:4], in1=gl_last, op=ALU.subtract),
            [("ps", 6), "q8a"], ["kds"])
        act(lambda e: e.activation(out=kds[:], in_=kds[:], func=AF.Exp, scale=-1.0), ["kds"], ["kds"])
        act(lambda e: e.activation(out=nbeg[:], in_=q8[:, 0:4], func=AF.Exp), ["q8a"], [("nbeg", par)])
        dve(lambda e: e.scalar_tensor_tensor(out=nbeg[:], in0=nbeg[:], scalar=-1.0, in1=beta[:],
                                             op0=ALU.mult, op1=ALU.mult), [("nbeg", par), "beta"], [("nbeg", par)])
        act(lambda e: e.activation(out=egcrow[:], in_=ps4(6), func=AF.Exp), [("ps", 6)], ["egcrow"])
        dve(lambda e: e.tensor_scalar(out=nq8[:], in0=q8[:, 0:4], scalar1=-1.0, scalar2=None, op0=ALU.mult),
            ["q8a"], ["nq8"])
        for h in range(4):
            dve(lambda e, h=h: e.scalar_tensor_tensor(out=tmpI[:, h, :], in0=cst(C_NEGI),
                                                      scalar=nq8[:, h:h + 1], in1=psm(6)[:, h * 128:(h + 1) * 128],
                                                      op0=ALU.add, op1=ALU.add),
                [("ps", 6), "nq8", "consts"], [("tmpI", h)])
            dve(lambda e, h=h: e.scalar_tensor_tensor(out=tmpS[:, h, :], in0=cst(C_NEGS),
                                                      scalar=nq8[:, h:h + 1], in1=psm(7)[:, h * 128:(h + 1) * 128],
                                                      op0=ALU.add, op1=ALU.add),
                [("ps", 7), "nq8", "consts"], [("tmpS", h)])
        act(lambda e: e.activation(out=DTI[:], in_=tmpI[:], func=AF.Exp), [("tmpI", h) for h in range(4)], ["DTI"])
        act(lambda e: e.activation(out=EB[:], in_=tmpS[:], func=AF.Exp), [("tmpS", h) for h in range(4)], ["EB"])

        for ch in range(12):
            dve(lambda e, ch=ch: e.tensor_scalar(out=cacc[:, ch, :], in0=zq[:, ch, 3:131],
                                                 scalar1=convT[:, ch, 3:4], scalar2=None, op0=ALU.mult),
                ["zq", "convT"], [("cacc", ch)])
            for j in range(3):
                dve(lambda e, ch=ch, j=j: e.scalar_tensor_tensor(
                    out=cacc[:, ch, :], in0=zq[:, ch, j:j + 128], scalar=convT[:, ch, j:j + 1],
                    in1=cacc[:, ch, :], op0=ALU.mult, op1=ALU.add),
                    ["zq", "convT", ("cacc", ch)], [("cacc", ch)])
        dve(lambda e: e.tensor_copy(out=zq[:, :, 0:3], in_=zq[:, :, 128:131]), ["zq"], ["zq"])
        cacc_all = [("cacc", ch) for ch in range(12)]
        act(lambda e: e.activation(out=qkvc[:], in_=cacc[:], func=AF.Silu), cacc_all, ["qkvc"])
        act(lambda e: e.activation(out=sqbf[:], in_=qkvc[:, 0:8, :], func=AF.Square), ["qkvc"], ["sqbf"])
        for half in range(2):
            bank = 1 + half
            for s4 in range(4):
                mm(psm(bank)[:, s4 * 128:(s4 + 1) * 128], ones_bf[:], sqbf[:, half * 4 + s4, :], True, True,
                   ["ones_bf", "sqbf"], bank)
        act(lambda e: e.activation(out=rs8[:, 0:4, :], in_=ps4(1), func=AF.Sqrt, scale=128.0, bias=128.0 * EPS),
            [("ps", 1)], [("rs8", 0)])
        act(lambda e: e.activation(out=rs8[:, 4:8, :], in_=ps4(2), func=AF.Sqrt, scale=1.0, bias=EPS),
            [("ps", 2)], [("rs8", 1)])
        dve(lambda e: e.reciprocal(out=rs8[:], in_=rs8[:]), [("rs8", 0), ("rs8", 1)], ["rs8"])
        dve(lambda e: e.tensor_tensor(out=qTn[:], in0=qkvc[:, 0:4, :], in1=rs8[:, 0:4, :], op=ALU.mult),
            ["qkvc", "rs8"], ["qTn"])
        dve(lambda e: e.tensor_tensor(out=kTn[:], in0=qkvc[:, 4:8, :], in1=rs8[:, 4:8, :], op=ALU.mult),
            ["qkvc", "rs8"], [("kTn", par)])
        act(lambda e: e.copy(out=vTb[:], in_=qkvc[:, 8:12, :]), ["qkvc"], ["vTb"])
        dve(lambda e: e.tensor_tensor(out=qgT[:], in0=qTn[:], in1=egcrow[:], op=ALU.mult),
            ["qTn", "egcrow"], [("qgT", par)])
        pb1 = psbf(1)
        for h in range(4):
            P.op("pe", lambda e, h=h: e.transpose(out=pb1[:, h * 128:(h + 1) * 128], in_=vTb[:, h, :],
                                                  identity=ident_bf[:]),
                 r=["vTb", "ident_bf", ("rs8", 0)], w=[("ps", 1)])
        dve(lambda e: e.tensor_copy(out=tpb[:].rearrange("p h d -> p (h d)"), in_=pb1[:, 0:512]),
            [("ps", 1)], ["tpb"])
        dve(lambda e: e.tensor_tensor(out=bv[:], in0=tpb[:],
                                      in1=beta[:].unsqueeze(2).to_broadcast([128, 4, 128]), op=ALU.mult),
            ["tpb", "beta"], [("bv", par)])
        pb2 = psbf(2)
        for h in range(4):
            P.op("pe", lambda e, h=h: e.transpose(out=pb2[:, h * 128:(h + 1) * 128], in_=kTn[:, h, :],
                                                  identity=ident_bf[:]),
                 r=[("kTn", par), "ident_bf", ("rs8", 1)], w=[("ps", 2)])
        dve(lambda e: e.tensor_copy(out=tpb2[:].rearrange("p h d -> p (h d)"), in_=pb2[:, 0:512]),
            [("ps", 2)], ["tpb2"])
        dve(lambda e: e.tensor_tensor(out=kd[:], in0=tpb2[:],
                                      in1=kds[:].unsqueeze(2).to_broadcast([128, 4, 128]), op=ALU.mult),
            ["tpb2", "kds"], [("kd", par)])
        for h in range(4):
            mm(psm(3)[:, h * 128:(h + 1) * 128], kTn[:, h, :], kTn[:, h, :], True, True, [("kTn", par), "zq"], 3)
        for h in range(4):
            mm(psm(4)[:, h * 128:(h + 1) * 128], kTn[:, h, :], qTn[:, h, :], True, True, [("kTn", par), "qTn", "qraw"], 4)
        dve(lambda e: e.tensor_tensor(out=Wm[0][:], in0=EB[:], in1=ps4(3), op=ALU.mult),
            [("ps", 3), "EB"], [("Wm", 0)])
        dve(lambda e: e.tensor_tensor(out=aqkT[:], in0=DTI[:], in1=ps4(4), op=ALU.mult),
            [("ps", 4), "DTI"], [("aqkT", par)])
        pb3 = psbf(3)
        for h in range(4):
            P.op("pe", lambda e, h=h: e.transpose(out=pb3[:, h * 128:(h + 1) * 128], in_=Wm[0][:, h, :],
                                                  identity=ident_bf[:]),
                 r=[("Wm", 0), "ident_bf"], w=[("ps", 3)])
        dve(lambda e: e.tensor_copy(out=Vm[0][:].rearrange("p h d -> p (h d)"), in_=pb3[:, 0:512]),
            [("ps", 3)], [("Vm", 0)])
        dve(lambda e: e.scalar_tensor_tensor(out=Pm[0][:], in0=Wm[0][:], scalar=-1.0,
                                             in1=ident_bf[:].unsqueeze(1).to_broadcast([128, 4, 128]),
                                             op0=ALU.mult, op1=ALU.add), [("Wm", 0), "ident_bf"], [("Pm", 0)])
        cw, cp = 0, 0
        for m in range(6):
            nw = 1 - cw
            last = (m == 5)
            if not last:
                for h in range(4):
                    mm(psm(1)[:, h * 128:(h + 1) * 128], Vm[cw][:, h, :], Wm[cw][:, h, :], True, True,
                       [("Vm", cw), ("Wm", cw), ("bv", par)], 1)
            for h in range(4):
                mm(psm(2)[:, h * 128:(h + 1) * 128], Wm[cw][:, h, :], Vm[cw][:, h, :], True, True,
                   [("Vm", cw), ("Wm", cw), ("kd", par)], 2)
            if not last:
                act(lambda e, nw=nw: e.copy(out=Wm[nw][:], in_=ps4(1)), [("ps", 1)], [("Wm", nw)])
            act(lambda e, nw=nw: e.copy(out=Vm[nw][:], in_=ps4(2)), [("ps", 2)], [("Vm", nw)])
            dve(lambda e, nw=nw: e.tensor_tensor(out=VI[:], in0=Vm[nw][:],
                                                 in1=ident_bf[:].unsqueeze(1).to_broadcast([128, 4, 128]),
                                                 op=ALU.add), [("Vm", nw), "ident_bf"], ["VI"])
            for h in range(4):
                mm(psm(3)[:, h * 128:(h + 1) * 128], VI[:, h, :], Pm[cp][:, h, :], True, True,
                   ["VI", ("Pm", cp)], 3)
            dve(lambda e, cp=cp: e.tensor_copy(out=Pm[1 - cp][:], in_=ps4(3)), [("ps", 3)], [("Pm", 1 - cp)])
            cw, cp = nw, 1 - cp
        dve(lambda e: e.tensor_copy(out=PTp[par][:], in_=Pm[cp][:]), [("Pm", cp)], [("PT", par)])

    def swa(ti):
        P.bankmap[:] = [3, 3, 3, 3, 3, 4, 4, 4]
        t0 = ti * 128
        par = ti % 2
        x1, zs, mix = x1I[par], zsp[par], mixp[par]
        kTn, qgT, kd, bv, aqkT = kTnp[par], qgTp[par], kdp[par], bvp[par], aqkTp[par]
        nbeg, egl = nbegp[par], eglp[par]
        for s4 in range(4):
            proj_fm(W_SQ + s4 * 128, 4, s4)
        proj_fm(W_SK, 5, 0)
        act(lambda e: e.copy(out=qraw[:, 0:4, :], in_=ps4(4)), [("ps", 4)], ["qraw"])
        act(lambda e: e.copy(out=qraw[:, 4, :], in_=psm(5)[:, 0:128]), [("ps", 5)], ["qraw"])
        for kc in range(8):
            mm(psm(7)[:, 128:256], hT[:, kc, :], win[:, kc, W_SV:W_SV + 128], kc == 0, kc == 7, ["arena", "hT"], 7)
        dve(lambda e, par=par: e.tensor_copy(out=v1[par][:, :, 0:64],
                                             in_=psm(7)[:, 128:256].rearrange("p (g d) -> p g d", g=2)),
            [("ps", 7)], [("v1", par)])
        dve(lambda e: e.tensor_copy(out=ang[:, 0, :], in_=posi[:]), ["posi"], ["ang0"])
        dve(lambda e: e.tensor_scalar(out=ang[:, 0, :], in0=ang[:, 0, :], scalar1=cst(C_INVF, 1), scalar2=None,
                                      op0=ALU.mult), ["ang0", "consts"], ["ang0"])
        dve(lambda e: e.tensor_scalar(out=ang[:, 1, :], in0=ang[:, 0, :], scalar1=PI / 2, scalar2=None,
                                      op0=ALU.add), ["ang0"], ["ang1"])
        dve(lambda e: e.tensor_scalar(out=angk[:], in0=ang[:], scalar1=1.0 / TWO_PI, scalar2=None, op0=ALU.mult),
            ["ang0", "ang1"], ["angk"])
        dve(lambda e: e.tensor_copy(out=angf[:], in_=angk[:]), ["angk"], ["angf"])
        dve(lambda e: e.scalar_tensor_tensor(out=ang[:], in0=angf[:], scalar=-TWO_PI, in1=ang[:],
                                             op0=ALU.mult, op1=ALU.add), ["angf", "ang0", "ang1"], ["ang"])
        dve(lambda e: e.tensor_single_scalar(out=angf[:], in_=ang[:], scalar=PI, op=ALU.is_gt), ["ang"], ["angf"])
        dve(lambda e: e.scalar_tensor_tensor(out=ang[:], in0=angf[:], scalar=-TWO_PI, in1=ang[:],
                                             op0=ALU.mult, op1=ALU.add), ["angf", "ang"], ["ang"])
        act(lambda e: e.activation(out=sincos[:], in_=ang[:], func=AF.Sin), ["ang"], ["sincos"])
        act(lambda e: e.copy(out=qrb[:], in_=qraw[:]), ["qraw"], ["qrb"])
        for c in range(4):
            mm(psm(5)[:, c * 128:(c + 1) * 128], perm_bf[:], qrb[:, c, :], True, True, ["perm_bf", "qrb"], 5)
        mm(psm(1)[:, 0:128], perm_bf[:], qrb[:, 4, :], True, True, ["perm_bf", "qrb", ("rr", 0), ("rr", 1), ("rr", 2), ("rr", 3)], 1)
        dve(lambda e: e.tensor_tensor(out=qtmp[:, 0:4, :], in0=sincos[:, 0:1, :].to_broadcast([128, 4, 128]),
                                      in1=ps4(5), op=ALU.mult),
            [("ps", 5), "sincos"], ["qtmp"])
        dve(lambda e: e.tensor_tensor(out=qtmp[:, 4, :], in0=sincos[:, 0, :], in1=psm(1)[:, 0:128], op=ALU.mult),
            [("ps", 1), "sincos"], ["qtmp"])
        dve(lambda e: e.tensor_tensor(out=qraw[:], in0=qraw[:],
                                      in1=sincos[:, 1:2, :].to_broadcast([128, 5, 128]), op=ALU.mult),
            ["qraw", "sincos", "qrb"], ["qraw"])
        dve(lambda e: e.tensor_tensor(out=qrot[:, 0:4, :], in0=qraw[:, 0:4, :], in1=qtmp[:, 0:4, :], op=ALU.add),
            ["qraw", "qtmp"], ["qrot"])
        dve(lambda e, par=par: e.tensor_tensor(out=krot[par][:], in0=qraw[:, 4, :], in1=qtmp[:, 4, :], op=ALU.add),
            ["qraw", "qtmp"], [("krot", par)])
        for g in range(2):
            lo, hi = 64 * g, 64 * g + 64
            mm(psm(1 + g)[:], krot[par][lo:hi, :], qrot[lo:hi, 0:4, :].rearrange("p c t -> p (c t)"), True, True,
               [("krot", par), "qrot", "vnew", ("qtmp")], 1 + g)
            act(lambda e, g=g: e.activation(out=Ecur[g][:], in_=psm(1 + g)[:], func=AF.Exp, scale=0.125),
                [("ps", 1 + g)], [("Ecur", g)])
            dve(lambda e, g=g: e.tensor_tensor(out=Ecur[g][:].rearrange("p (c t) -> p c t", c=4),
                                               in0=Ecur[g][:].rearrange("p (c t) -> p c t", c=4),
                                               in1=mcur_bf[:].unsqueeze(1).to_broadcast([128, 4, 128]), op=ALU.mult),
                [("Ecur", g), "mcur_bf"], [("Ecur", g)])
            if ti > 0:
                mm(psm(5 + g)[:], krot[1 - par][lo:hi, :], qrot[lo:hi, 0:4, :].rearrange("p c t -> p (c t)"),
                   True, True, [("krot", 1 - par), "qrot", ("S", 1 - par), "qtmp"], 5 + g)
                act(lambda e, g=g: e.activation(out=Eprev[g][:], in_=psm(5 + g)[:], func=AF.Exp, scale=0.125),
                    [("ps", 5 + g)], [("Eprev", g)])
                dve(lambda e, g=g: e.tensor_tensor(out=Eprev[g][:].rearrange("p (c t) -> p c t", c=4),
                                                   in0=Eprev[g][:].rearrange("p (c t) -> p c t", c=4),
                                                   in1=mprev_bf[:].unsqueeze(1).to_broadcast([128, 4, 128]),
                                                   op=ALU.mult),
                    [("Eprev", g), "mprev_bf"], [("Eprev", g)])
        for g in range(2):
            bank = 3 if g == 0 else 7
            for c in range(4):
                dst = psm(bank)[:, c * 65:(c + 1) * 65]
                deps = [("Ecur", g), ("v1", par), "ot", "tmpS_all"]
                if ti > 0:
                    mm(dst, Eprev[g][:, c * 128:(c + 1) * 128], v1[1 - par][:, g, :], True, False,
                       deps + [("Eprev", g), ("v1", 1 - par)], bank)
                    mm(dst, Ecur[g][:, c * 128:(c + 1) * 128], v1[par][:, g, :], False, True, deps, bank)
                else:
                    mm(dst, Ecur[g][:, c * 128:(c + 1) * 128], v1[par][:, g, :], True, True, deps, bank)
            pv = psm(bank)[:, 0:260].rearrange("p (c d) -> p c d", c=4)
            dve(lambda e, g=g, pv=pv: e.tensor_tensor(out=den[:, g * 4:(g + 1) * 4], in0=esink_b[:, g * 4:(g + 1) * 4],
                                                      in1=pv[:, :, 64], op=ALU.add),
                [("ps", bank), "esink_b"], [("den", g)])
            dve(lambda e, g=g: e.reciprocal(out=den[:, g * 4:(g + 1) * 4], in_=den[:, g * 4:(g + 1) * 4]),
                [("den", g)], [("den", g)])
            dve(lambda e, g=g, pv=pv: e.tensor_tensor(
                out=mix[:, 512 + g * 256:512 + (g + 1) * 256].rearrange("p (c d) -> p c d", c=4),
                in0=den[:, g * 4:(g + 1) * 4].unsqueeze(2).to_broadcast([128, 4, 64]), in1=pv[:, :, 0:64],
                op=ALU.mult), [("ps", bank), ("den", g)], [("mix", par, 1 + g)])

    def back(ti):
        P.bankmap[:] = [6, 5, 6, 3, 7, 5, 6, 7]
        t0 = ti * 128
        par = ti % 2
        x1, zs, mix = x1I[par], zsp[par], mixp[par]
        kTn, qgT, kd, bv, aqkT = kTnp[par], qgTp[par], kdp[par], bvp[par], aqkTp[par]
        nbeg, egl = nbegp[par], eglp[par]
        PT = PTp[par]
        PTk = ("PT", par)
        So, Sn = Sst[par], Sst[1 - par]
        for h in range(4):
            mm(psm(1)[:, h * 128:(h + 1) * 128], kTn[:, h, :], So[:, h, :], True, True, [("kTn", par), ("S", par)], 1)
        dve(lambda e: e.tensor_tensor(out=osq[:], in0=nbeg[:].unsqueeze(2).to_broadcast([128, 4, 128]),
                                      in1=ps4(1), op=ALU.mult),
            [("ps", 1), ("nbeg", par)], ["osq"])
        dve(lambda e: e.tensor_tensor(out=rr[:], in0=osq[:], in1=bv[:], op=ALU.add),
            ["osq", ("bv", par)], [("rr", h) for h in range(4)])
        for h in range(4):
            mm(psm(2)[:, h * 128:(h + 1) * 128], PT[:, h, :], rr[:, h, :], True, True, [PTk, ("rr", h)], 2)
        act(lambda e: e.copy(out=vnew[:], in_=ps4(2)), [("ps", 2)], ["vnew"])
        for h in range(4):
            mm(psm(4)[:, h * 128:(h + 1) * 128], qgT[:, h, :], So[:, h, :], True, False, [("qgT", par), ("S", par), ("aqkT", par)], 4)
            mm(psm(4)[:, h * 128:(h + 1) * 128], aqkT[:, h, :], vnew[:, h, :], False, True, [("aqkT", par), "vnew"], 4)
        for h in range(4):
            mm(psm(5)[:, h * 128:(h + 1) * 128], kd[:, h, :], vnew[:, h, :], True, True, [("kd", par), "vnew", "qraw"], 5)
        for h in range(4):
            dve(lambda e, h=h: e.scalar_tensor_tensor(out=Sn[:, h, :], in0=So[:, h, :], scalar=egl[:, h:h + 1],
                                                      in1=psm(5)[:, h * 128:(h + 1) * 128],
                                                      op0=ALU.mult, op1=ALU.add),
                [("S", par), ("egl", par), ("ps", 5)], [("S", 1 - par)])
        act(lambda e: e.activation(out=osq[:], in_=ps4(4), func=AF.Square), [("ps", 4)], ["osq"])
        dve(lambda e: e.tensor_reduce(out=o4[:, 0:4], in_=osq[:], axis=AX.X, op=ALU.add), ["osq"], ["o4"])
        act(lambda e: e.activation(out=o4[:, 4:8], in_=o4[:, 0:4], func=AF.Sqrt, scale=1.0 / 128.0, bias=EPS),
            ["o4"], ["o4b"])
        dve(lambda e: e.reciprocal(out=o4[:, 4:8], in_=o4[:, 4:8]), ["o4b"], ["o4b"])
        dve(lambda e: e.tensor_tensor(out=ot[:], in0=o4[:, 4:8].unsqueeze(2).to_broadcast([128, 4, 128]),
                                      in1=ps4(4), op=ALU.mult),
            [("ps", 4), "o4b"], ["ot"])
        dve(lambda e: e.tensor_tensor(out=ot[:], in0=ot[:],
                                      in1=onorm_b[:].unsqueeze(1).to_broadcast([128, 4, 128]), op=ALU.mult),
            ["ot", "onorm_b"], ["ot"])
        dve(lambda e: e.tensor_tensor(out=mix[:, 0:512], in0=ot[:].rearrange("p h d -> p (h d)"), in1=zs[:],
                                      op=ALU.mult), ["ot", ("zs", par)], [("mix", par, 0)])

        transpose8(mix, [("mix", par, 0), ("mix", par, 1), ("mix", par, 2)], mixT, "mixT", 8, 0)
        for half in range(2):
            bank = 4 + half
            for kc in range(8):
                mm(psm(bank)[:], mixT[:, kc, :], wout[:, kc, half * 512:(half + 1) * 512], kc == 0, kc == 7,
                   ["mixT", "wout", "osq", "ot", "qtmp"], bank)
            dve(lambda e, half=half, bank=bank: e.tensor_tensor(
                out=y[:, half * 512:(half + 1) * 512], in0=x1[:, half * 512:(half + 1) * 512], in1=psm(bank)[:],
                op=ALU.add), [("ps", bank), ("x1", par)], ["y"])
        P.dma("sp", lambda e, t0=t0: e.dma_start(out=x1_d[t0:t0 + 128, :], in_=y[:]),
              r=["y"], w=[("x1_d", ti)])

    def rec(fn, ti):
        P.begin()
        fn(ti)
        return P.end()

    def swa_start(fa):
        idx = max(j for j, it in enumerate(fa) if ("hT" in it[3] or "posi" in it[3]))
        return (idx + 1.0) / len(fa)

    fa = rec(frontA, 0)
    P.merge([fa, rec(swa, 0)], starts=[0.0, swa_start(fa)])
    for ti in range(NT):
        if ti + 1 < NT:
            fa = rec(frontA, ti + 1)
            P.merge([fa, rec(swa, ti + 1), rec(back, ti)], starts=[0.0, swa_start(fa), 0.0])
        else:
            back(ti)
    P.bankmap[:] = list(range(8))


    if phase2:
        for kc in range(8):
            P.dma("pool", lambda e, kc=kc: e.dma_start(out=arena[:, kc * 2048:(kc + 1) * 2048],
                                                       in_=wq_d[kc * 128:(kc + 1) * 128, :]), w=["arena"])
        for kc in range(8):
            P.dma("pool", lambda e, kc=kc: e.dma_start(out=arena[:, 16384 + kc * 1024:16384 + (kc + 1) * 1024],
                                                       in_=gate_d[kc * 128:(kc + 1) * 128, :]), w=["arena"])
        P.barrier()
        for (tile_, src, key) in ((ffn_b, ffn_norm_d, "ffn_b"), (ple_b, ple_norm_d, "ple_b"),
                                  (fin_b, final_norm_d, "fin_b")):
            P.dma("sp", lambda e: e.dma_start(out=tile_[:], in_=src.partition_broadcast(128)), w=[key])
        P.dma("pool", lambda e: e.dma_start(out=keysT[:], in_=keysT_d.rearrange("g d n -> d g n")), w=["keysT"])
        for kc in range(2):
            P.dma("pool", lambda e, kc=kc: e.dma_start(out=proj_w[:, kc, :], in_=proj_d[kc * 128:(kc + 1) * 128, :]),
                  w=["proj_w"])
        def head(ti):
            t0 = ti * 128
            pq = ti % 2
            x1c, hbfc, eidxc, gatesc = x1s[pq], hbfs[pq], eidxs[pq], gatess[pq]
            kx1, khbf, keidx, kgates = ("x1", pq), ("hbfp", pq), ("eidx", pq), ("gates", pq)
            P.dma("sp", lambda e: e.dma_start(out=x1c[:], in_=x1_d[t0:t0 + 128, :]), r=[("x1_d", ti)], w=[kx1])
            rmsnorm(x1c, kx1, ffn_b, "ffn_b", hbfc, khbf, 0)
            transpose8(hbfc, khbf, hT, "hT", 8, 0)
            for gq in range(4):
                bank = 1 + (gq % 2)
                for gi in range(4):
                    g = gq * 4 + gi
                    for kc in range(8):
                        P.op("pe", lambda e, g=g, gi=gi, kc=kc, bank=bank: e.matmul(
                            out=ps[bank][:, gi * 128:(gi + 1) * 128], lhsT=wq[:, kc, g * 128:(g + 1) * 128],
                            rhs=hT[:, kc, :], start=(kc == 0), stop=(kc == 7)),
                            r=["arena", "hT"], w=[("ps", bank)])
                eng = "act" if gq % 2 == 0 else "dve"
                if eng == "act":
                    P.op("act", lambda e, gq=gq, bank=bank: e.copy(
                        out=qT[:, gq * 4:(gq + 1) * 4, :].rearrange("p g t -> p (g t)"), in_=ps[bank][:]),
                        r=[("ps", bank)], w=[("qT", gq)])
                else:
                    P.op("dve", lambda e, gq=gq, bank=bank: e.tensor_copy(
                        out=qT[:, gq * 4:(gq + 1) * 4, :].rearrange("p g t -> p (g t)"), in_=ps[bank][:]),
                        r=[("ps", bank)], w=[("qT", gq)])
            for gq in range(4):
                bank = 1 + (gq % 2)
                for gi in range(4):
                    g = gq * 4 + gi
                    P.op("pe", lambda e, g=g, gi=gi, bank=bank: e.matmul(
                        out=ps[bank][:, gi * 128:(gi + 1) * 128], lhsT=qT[:, g, :], rhs=keysT[:, g, :],
                        start=True, stop=True),
                        r=[("qT", gq), "keysT"], w=[("ps", bank)])
                P.op("act", lambda e, gq=gq, bank=bank: e.copy(
                    out=sc[:, gq * 4:(gq + 1) * 4, :].rearrange("p g n -> p (g n)"), in_=ps[bank][:]),
                    r=[("ps", bank)], w=[("scg", gq * 4 + q_) for q_ in range(4)])
            if False:
                P.dma("sp", lambda e, t0=t0: e.dma_start(out=dbg_d["sc"][t0:t0 + 128, :],
                                                         in_=sc[:].rearrange("p g n -> p (g n)")),
                      r=[("sc", q) for q in range(4)], w=["dbg_sc"])
            for g in range(16):
                Kg = ("scg", g)
                P.op("dve", lambda e: e.max(out=top[:, g, 0:8], in_=sc[:, g, :]), r=[Kg], w=[("top", g, 0)])
                P.op("dve", lambda e: e.max_index(out=tidx[:, g, 0:8], in_max=top[:, g, 0:8], in_values=sc[:, g, :]),
                     r=[Kg, ("top", g, 0)], w=[("tidx", g, 0)])
                P.op("dve", lambda e: e.match_replace(out=sc[:, g, :], in_to_replace=top[:, g, 0:8],
                                                      in_values=sc[:, g, :], imm_value=-1e30),
                     r=[Kg, ("top", g, 0)], w=[Kg])
                P.op("dve", lambda e: e.max(out=top[:, g, 8:16], in_=sc[:, g, :]), r=[Kg], w=[("top", g, 1)])
                P.op("dve", lambda e: e.max_index(out=tidx[:, g, 8:16], in_max=top[:, g, 8:16],
                                                  in_values=sc[:, g, :]),
                     r=[Kg, ("top", g, 1)], w=[("tidx", g, 1)])
            sc_all = [("scg", g) for g in range(16)]
            topk_all = [("top", g, k) for g in range(16) for k in range(2)]
            tidx_all = [("tidx", g, k) for g in range(16) for k in range(2)]
            top4 = top[:].rearrange("p (h two) k -> p h two k", two=2)
            P.op("dve", lambda e: e.tensor_tensor(
                out=cand[:].rearrange("p h (a b) -> p h a b", b=16),
                in0=top4[:, :, 0, :].unsqueeze(3).to_broadcast([128, 8, 16, 16]),
                in1=top4[:, :, 1, :].unsqueeze(2).to_broadcast([128, 8, 16, 16]), op=ALU.add),
                r=topk_all, w=sc_all)
            P.op("dve", lambda e: e.tensor_copy(out=tidxf[:], in_=tidx[:]), r=tidx_all, w=["tidxf"])
            for h in range(8):
                Kc = [("scg", 2 * h), ("scg", 2 * h + 1)]
                P.op("dve", lambda e: e.max(out=best[:, h, 0:8], in_=cand[:, h, :]), r=Kc, w=[("best", h, 0)])
                P.op("dve", lambda e: e.max_index(out=bpos[:, h, 0:8], in_max=best[:, h, 0:8],
                                                  in_values=cand[:, h, :]),
                     r=Kc + [("best", h, 0)], w=[("bpos", h, 0)])
                P.op("dve", lambda e: e.match_replace(out=cand[:, h, :], in_to_replace=best[:, h, 0:8],
                                                      in_values=cand[:, h, :], imm_value=-1e30),
                     r=Kc + [("best", h, 0)], w=Kc)
                P.op("dve", lambda e: e.max(out=best[:, h, 8:16], in_=cand[:, h, :]), r=Kc, w=[("best", h, 1)])
                P.op("dve", lambda e: e.max_index(out=bpos[:, h, 8:16], in_max=best[:, h, 8:16],
                                                  in_values=cand[:, h, :]),
                     r=Kc + [("best", h, 1)], w=[("bpos", h, 1)])
            best_all = [("best", h, k) for h in range(8) for k in range(2)]
            bpos_all = [("bpos", h, k) for h in range(8) for k in range(2)]
            P.op("dve", lambda e: e.tensor_copy(out=bposf[:], in_=bpos[:]), r=bpos_all, w=["bposf"])
            bc_s = lambda t: t[:].unsqueeze(3).to_broadcast([128, 8, 16, 16])
            bc_c = lambda ap: ap.unsqueeze(1).unsqueeze(1).to_broadcast([128, 8, 16, 16])
            P.op("dve", lambda e: e.tensor_tensor(out=big4[:], in0=bc_s(bposf), in1=bc_c(iota16x16), op=ALU.is_ge),
                 r=["bposf", "consts"], w=sc_all)
            P.op("dve", lambda e: e.tensor_reduce(out=asel[:], in_=big4[:], axis=AX.X, op=ALU.add),
                 r=sc_all, w=["asel"])
            P.op("dve", lambda e: e.tensor_scalar(out=asel[:], in0=asel[:], scalar1=-1.0, scalar2=None, op0=ALU.add),
                 r=["asel"], w=["asel"])
            P.op("dve", lambda e: e.scalar_tensor_tensor(out=bsel[:], in0=asel[:], scalar=-16.0, in1=bposf[:],
                                                         op0=ALU.mult, op1=ALU.add),
                 r=["asel", "bposf"], w=["bsel"])
            tf4 = tidxf[:].rearrange("p (h two) k -> p h two k", two=2)
            for (sel, half, dst, dkey) in ((asel, 0, isel, "isel"), (bsel, 1, jsel, "jsel")):
                skey = "asel" if half == 0 else "bsel"
                P.op("dve", lambda e, sel=sel: e.tensor_tensor(out=big4[:], in0=bc_s(sel), in1=bc_c(iota16),
                                                               op=ALU.is_equal),
                     r=[skey, "consts"], w=sc_all)
                P.op("dve", lambda e, half=half: e.tensor_tensor(
                    out=big4[:], in0=big4[:],
                    in1=tf4[:, :, half, :].unsqueeze(2).to_broadcast([128, 8, 16, 16]), op=ALU.mult),
                    r=sc_all + ["tidxf"], w=sc_all)
                P.op("dve", lambda e, dst=dst: e.tensor_reduce(out=dst[:], in_=big4[:], axis=AX.X, op=ALU.add),
                     r=sc_all, w=[dkey])
            P.op("dve", lambda e: e.scalar_tensor_tensor(
                out=ef[:], in0=isel[:].rearrange("p h s -> p (h s)"), scalar=128.0,
                in1=jsel[:].rearrange("p h s -> p (h s)"), op0=ALU.mult, op1=ALU.add),
                r=["isel", "jsel"], w=["ef"])
            P.op("dve", lambda e: e.tensor_copy(out=eidxc[:], in_=ef[:]), r=["ef"], w=[keidx])
            P.op("dve", lambda e: e.tensor_tensor(out=gatesc[:], in0=best[:],
                                                  in1=best[:, :, 0:1].to_broadcast([128, 8, 16]), op=ALU.subtract),
                 r=best_all, w=[kgates])
            P.op("act", lambda e: e.activation(out=gatesc[:], in_=gatesc[:], func=AF.Exp), r=[kgates], w=[kgates])
            P.op("dve", lambda e: e.tensor_reduce(out=gz[:], in_=gatesc[:], axis=AX.X, op=ALU.add),
                 r=[kgates], w=["gz"])
            P.op("dve", lambda e: e.reciprocal(out=gz[:], in_=gz[:]), r=["gz"], w=["gz"])
            P.op("dve", lambda e: e.tensor_tensor(out=gatesc[:], in0=gatesc[:],
                                                  in1=gz[:].unsqueeze(2).to_broadcast([128, 8, 16]), op=ALU.mult),
                 r=[kgates, "gz"], w=[kgates])
            if False:
                P.dma("sp", lambda e, t0=t0: e.dma_start(out=dbg_d["eidx"][t0:t0 + 128, :], in_=eidx[:]),
                      r=["eidx"], w=["dbg_eidx"])
                P.dma("sp", lambda e, t0=t0: e.dma_start(out=dbg_d["gates"][t0:t0 + 128, :],
                                                         in_=gatesc[:].rearrange("p h s -> p (h s)")),
                      r=["gates"], w=["dbg_gates"])
        def gather(ti):
            pq = ti % 2
            accb = 5 if pq == 0 else 3
            x1c, hbfc, eidxc, gatesc = x1s[pq], hbfs[pq], eidxs[pq], gatess[pq]
            kx1, khbf, keidx, kgates = ("x1", pq), ("hbfp", pq), ("eidx", pq), ("gates", pq)
            def fin(g):
                gs = slice(g * GS, g * GS + GS)
                dk = (ti * (128 // GS) + g) % 2
                apk = [("actpre", q) for q in range(g * GS, g * GS + GS)]
                ga, gbb = gl_a[:, gs], gl_b[:, gs]
                gflat = gatesc[:].rearrange("p h s -> p (h s)")
                P.op("dve", lambda e: e.tensor_tensor(out=ga, in0=actpre[:, gs], in1=actpre[:, gs], op=ALU.mult),
                     r=apk, w=[("gla", g)])
                P.op("dve", lambda e: e.tensor_scalar(out=ga, in0=ga, scalar1=0.044715, scalar2=1.0,
                                                      op0=ALU.mult, op1=ALU.add), r=[("gla", g)], w=[("gla", g)])
                P.op("dve", lambda e: e.tensor_tensor(out=ga, in0=ga, in1=actpre[:, gs], op=ALU.mult),
                     r=[("gla", g)] + apk, w=[("gla", g)])
                P.op("act", lambda e: e.activation(out=gbb, in_=ga, func=AF.Sigmoid, scale=1.5957691216057308),
                     r=[("gla", g)], w=[("glb", g)])
                P.op("dve", lambda e: e.tensor_tensor(out=wts[:, gs], in0=actpre[:, gs], in1=gflat[:, gs],
                                                      op=ALU.mult), r=apk + [kgates], w=[("wts", g)])
                P.op("dve", lambda e: e.tensor_tensor(out=wts[:, gs], in0=wts[:, gs], in1=gbb, op=ALU.mult),
                     r=[("glb", g), ("wts", g)], w=[("wts", g)])
                for j in range(GS):
                    P.op("act", lambda e: e.activation(out=dg[dk][:, j, :], in_=ident_bf[:], func=AF.Copy,
                                                       scale=wts[:, g * GS + j:g * GS + j + 1]),
                         r=[("wts", g), "ident_bf"], w=[("dg", dk)])
                for j in range(GS):
                    sj = g * GS + j
                    bj = (ti * 128 + sj) % NG
                    for half in range(2):
                        mm(ps[accb + half][:], dg[dk][:, j, :], gb[bj][:, D + half * 512:D + (half + 1) * 512],
                           sj == 0, sj == 127, [("dg", dk), ("gb", bj)], accb + half)

            for s in range(128):
                gidx = ti * 128 + s
                b = gidx % NG
                pb_ = gidx % 3
                P.dma("pool", lambda e: e.indirect_dma_start(
                    out=gb[b], out_offset=None, in_=uvb_d,
                    in_offset=bass.IndirectOffsetOnAxis(ap=eidxc[:, s:s + 1], axis=0)),
                    r=[keidx], w=[("gb", b)])
                if s % 4 == 3:
                    P.op("dve", lambda e: e.scalar_tensor_tensor(
                        out=prod[pb_][:], in0=gb[b][:, 0:D], scalar=1.0, in1=hbfc[:], op0=ALU.mult, op1=ALU.mult,
                        accum_out=actpre[:, s:s + 1]),
                        r=[("gb", b), khbf], w=[("prod", pb_), ("actpre", s)])
                else:
                    P.op("dve", lambda e: e.tensor_tensor(out=prod[pb_][:], in0=gb[b][:, 0:D], in1=hbfc[:],
                                                          op=ALU.mult),
                         r=[("gb", b), khbf], w=[("prod", pb_)])
                    P.op("act", lambda e: e.activation(out=prod[pb_][:], in_=prod[pb_][:], func=AF.Copy,
                                                       accum_out=actpre[:, s:s + 1]),
                         r=[("prod", pb_)], w=[("prod", pb_), ("actpre", s)])
                if s % GS == GS - 1:
                    if s // GS >= 1:
                        fin(s // GS - 1)
            fin(128 // GS - 1)
            for half in range(2):
                P.op("dve", lambda e: e.tensor_tensor(out=y[:, half * 512:(half + 1) * 512],
                                                      in0=x1c[:, half * 512:(half + 1) * 512], in1=ps[accb + half][:],
                                                      op=ALU.add), r=[kx1, ("ps", accb + half)], w=["y"])
            if False:
                P.dma("sp", lambda e, t0=t0: e.dma_start(out=dbg_d["x2"][t0:t0 + 128, :], in_=y[:]),
                      r=["y"], w=["dbg_x2"])
        def tail(ti):
            t0 = ti * 128
            rmsnorm(y, "y", ple_b, "ple_b", tbf, "tbf", 2)
            transpose8(tbf, "tbf", hT, "hT", 8, 0)
            for half in range(2):
                bank = 1 + half
                for kc in range(8):
                    P.op("pe", lambda e, half=half, kc=kc, bank=bank: e.matmul(
                        out=ps[bank][:], lhsT=hT[:, kc, :], rhs=gate_w[:, kc, half * 512:(half + 1) * 512],
                        start=(kc == 0), stop=(kc == 7)), r=["hT", "arena"], w=[("ps", bank)])
                P.op("act", lambda e, half=half, bank=bank: e.activation(
                    out=sig[:, half * 512:(half + 1) * 512], in_=ps[bank][:], func=AF.Sigmoid),
                    r=[("ps", bank)], w=[("gb", 1)])
            P.dma("sp", lambda e, t0=t0: e.dma_start(out=pt[:], in_=p_d[t0:t0 + 128, :]), w=["pt"])
            P.op("act", lambda e: e.copy(out=ptbf[:], in_=pt[:]), r=["pt"], w=["ptbf"])
            transpose8(ptbf, "ptbf", pT, "pT", 2, 0)
            for half in range(2):
                bank = 1 + half
                for kc in range(2):
                    P.op("pe", lambda e, half=half, kc=kc, bank=bank: e.matmul(
                        out=ps[bank][:], lhsT=pT[:, kc, :], rhs=proj_w[:, kc, half * 512:(half + 1) * 512],
                        start=(kc == 0), stop=(kc == 1)), r=["pT", "proj_w"], w=[("ps", bank)])
                P.op("dve", lambda e, half=half, bank=bank: e.tensor_tensor(
                    out=x3[:, half * 512:(half + 1) * 512], in0=sig[:, half * 512:(half + 1) * 512],
                    in1=ps[bank][:], op=ALU.mult), r=[("gb", 1), ("ps", bank)], w=[("gb", 2)])
            P.op("dve", lambda e: e.tensor_tensor(out=x3[:], in0=x3[:], in1=y[:], op=ALU.add),
                 r=[("gb", 2), "y"], w=[("gb", 2)])
            rmsnorm(x3, ("gb", 2), fin_b, "fin_b", outt, ("gb", 3), 4)
            P.dma("sp", lambda e, t0=t0: e.dma_start(out=out_d[t0:t0 + 128, :], in_=outt[:]),
                  r=[("gb", 3)], w=["out_d"])
        head(0)
        for ti in range(NT):
            P.begin()
            gather(ti)
            G = P.end()
            H = []
            if ti + 1 < NT:
                P.begin()
                head(ti + 1)
                H = P.end()
            P.merge([G, H], [1.0, 0.95])
            tail(ti)
    P.op("sp", None, r=["out_d"] + [("x1_d", i) for i in range(NT)], w=[])
    P.emit(es)
    es.close()
    return nc


def kernel(**inputs):
    NT = SEQ // 128
    nc = bass.Bass("TRN2", target_bir_lowering=False)
    build(nc, NT)
    shared = core_inputs(inputs, 0)
    in_maps = []
    for b in range(8):
        m = dict(shared)
        m["x"] = np.ascontiguousarray(np.asarray(inputs["x"][b], dtype=np.float32))
        m["p"] = np.ascontiguousarray(np.asarray(inputs["p"][0, b], dtype=np.float32))
        m["positions"] = np.ascontiguousarray(np.asarray(inputs["positions"][b], dtype=np.int32))
        in_maps.append(m)
    res = run_bass_kernel_spmd(nc, in_maps, core_ids=list(range(8)))
    return np.stack([np.asarray(r["out"], dtype=np.float32) for r in res.results], axis=0)


def core_inputs(inputs, b, T=SEQ):
    f = lambda a: np.ascontiguousarray(np.asarray(a, dtype=np.float32))
    w_in = np.asarray(inputs["w_in"][0], dtype=np.float32)
    o_sq = 4 * 512 + 8
    swq = w_in[:, o_sq:o_sq + 512].reshape(D, 8, 64)
    swq_p = np.stack([np.concatenate([swq[:, c], swq[:, 4 + c]], axis=1) for c in range(4)], axis=1).reshape(D, 512)
    w_in_p = np.concatenate([w_in[:, :o_sq], swq_p, w_in[:, o_sq + 512:]], axis=1)
    convT = np.asarray(inputs["conv_w"][0], dtype=np.float32).reshape(4, 12, 128).transpose(2, 1, 0).reshape(128, 48)
    return {
        "x": f(inputs["x"][b, :T]),
        "p": f(inputs["p"][0, b, :T]),
        "positions": np.ascontiguousarray(np.asarray(inputs["positions"][b, :T], dtype=np.int32)),
        "consts": make_consts(),
        "mix_norm": f(inputs["mix_norm"][0]),
        "w_in": f(w_in_p),
        "convT": f(convT),
        "dn_dt_bias": f(inputs["dn_dt_bias"][0]),
        "dn_a_log": f(inputs["dn_a_log"][0]),
        "dn_out_norm": f(inputs["dn_out_norm"][0]),
        "attn_sinks": f(inputs["attn_sinks"][0]),
        "w_out": f(inputs["w_out"][0]),
        "ffn_norm": f(inputs["ffn_norm"][0]),
        "ple_norm": f(inputs["ple_norm"][0]),
        "final_norm": f(inputs["final_norm"]),
        "peer_wq": f(inputs["peer_wq"][0]),
        "peer_keysT": f(np.asarray(inputs["peer_keys"][0]).reshape(16, 128, 128).transpose(0, 2, 1)),
        "peer_uv": f(np.concatenate([np.asarray(inputs["peer_u"][0], dtype=np.float32),
                                     np.asarray(inputs["peer_v"][0], dtype=np.float32)], axis=1)),
        "ple_gate": f(inputs["ple_gate"][0]),
        "ple_proj": f(inputs["ple_proj"][0]),
    }
```

```python
import numpy as np
from contextlib import ExitStack
import concourse.bass as bass
import concourse.mybir as mybir
from concourse.bass_utils import run_bass_kernel_spmd

F32 = mybir.dt.float32
BF16 = mybir.dt.bfloat16
I32 = mybir.dt.int32
U32 = mybir.dt.uint32
ALU = mybir.AluOpType
AF = mybir.ActivationFunctionType
AX = mybir.AxisListType

D = 1024
SEQ = 4096
EPS = 1e-6
UVB_KIND = "Internal"


class _Op:
    __slots__ = ("eng", "fn", "is_dma", "deps", "signal", "tok", "dsem", "dval")

    def __init__(self, eng, fn, is_dma):
        self.eng = eng
        self.fn = fn
        self.is_dma = is_dma
        self.deps = []
        self.signal = False
        self.tok = None
        self.dsem = None
        self.dval = 0


class _Rec:
    def __init__(self):
        self.call = None

    def __getattr__(self, name):
        def f(*a, **k):
            self.call = (name, a, k)
            return self
        return f


class Prog:
    ENGS = ("pe", "act", "dve", "pool", "sp")
    NDMA = {"sp": 24, "act": 8, "pool": 40}

    def __init__(self, nc):
        self.nc = nc
        self.ops = {e: [] for e in self.ENGS}
        self.res = {}
        self.dk = {e: 0 for e in self.ENGS}
        self.last_dma = {}
        self.pending = None
        self.bankmap = list(range(8))

    def _add(self, eng, fn, r, w, is_dma):
        bm = self.bankmap
        r = [(("ps", bm[k[1]]) if (isinstance(k, tuple) and len(k) == 2 and k[0] == "ps") else k) for k in r]
        w = [(("ps", bm[k[1]]) if (isinstance(k, tuple) and len(k) == 2 and k[0] == "ps") else k) for k in w]
        if fn is not None:
            rec = _Rec()
            fn(rec)
            name, a, k = rec.call
            fn = (lambda engine, name=name, a=a, k=k: getattr(engine, name)(*a, **k))
        if self.pending is not None:
            self.pending.append((eng, fn, list(r), list(w), is_dma))
            return None
        return self._commit(eng, fn, r, w, is_dma)

    def begin(self):
        self.pending = []

    def end(self):
        lst, self.pending = self.pending, None
        return lst

    def merge(self, lists, spans=None, starts=None):
        items = []
        for li, lst in enumerate(lists):
            span = 1.0 if spans is None else spans[li]
            st = 0.0 if starts is None else starts[li]
            for j, it in enumerate(lst):
                items.append((st + (j + 0.5) / len(lst) * (span - st), li, j, it))
        items.sort(key=lambda t: t[:3])
        for _, _, _, (eng, fn, r, w, is_dma) in items:
            self._commit(eng, fn, r, w, is_dma)

    def _commit(self, eng, fn, r, w, is_dma):
        op = _Op(eng, fn, is_dma)
        if is_dma:
            op.dval = self.dk[eng] % self.NDMA[eng]
            self.dk[eng] += 1
            self.last_dma[(eng, op.dval)] = op
        deps = []
        for k in r:
            st = self.res.get(k)
            if st is not None and st[0] is not None:
                deps.append(st[0])
        for k in w:
            st = self.res.get(k)
            if st is not None:
                if st[0] is not None:
                    deps.append(st[0])
                deps.extend(st[1])
        for k in r:
            st = self.res.get(k)
            if st is None:
                st = self.res[k] = [None, []]
            st[1].append(op)
        for k in w:
            self.res[k] = [op, []]
        seen = set()
        for d in deps:
            if d is op or id(d) in seen:
                continue
            seen.add(id(d))
            if d.eng == "pe" and eng == "pe" and not d.is_dma and not is_dma:
                continue
            op.deps.append(d)
            d.signal = True
        self.ops[eng].append(op)
        return op

    def op(self, eng, fn, r=(), w=()):
        return self._add(eng, fn, r, w, False)

    def dma(self, q, fn, r=(), w=()):
        return self._add(q, fn, r, w, True)

    def barrier(self):
        lasts = list(self.last_dma.values())
        for e in self.ENGS:
            for op in reversed(self.ops[e]):
                if not op.is_dma and op.fn is not None:
                    lasts.append(op)
                    break
        for d in lasts:
            d.signal = True
        for e in self.ENGS:
            op = _Op(e, None, False)
            op.deps = list(lasts)
            self.ops[e].append(op)

    def emit(self, es):
        nc = self.nc
        csem = {e: es.enter_context(nc.semaphore("c_" + e)) for e in self.ENGS}
        dsem = {e: [es.enter_context(nc.semaphore("d_%s_%d" % (e, i))) for i in range(n)]
                for e, n in self.NDMA.items()}
        for e in self.ENGS:
            cnt = 0
            k = 0
            vals = [0] * self.NDMA.get(e, 0)
            for op in self.ops[e]:
                if op.is_dma:
                    s = op.dval
                    op.dval = vals[s]
                    vals[s] += 16
                    op.dsem = dsem[e][s]
                    op.tok = (op.dsem, vals[s])
                elif op.signal:
                    cnt += 1
                    op.tok = (csem[e], cnt)
        ops = self.ops

        def make(e):
            def body(engine):
                waited = {}
                for op in ops[e]:
                    waits = [d.tok for d in op.deps]
                    if op.is_dma and op.dval > 0:
                        waits.append((op.dsem, op.dval))
                    for (s, v) in waits:
                        if waited.get(s, 0) >= v:
                            continue
                        engine.wait_ge(s, v)
                        waited[s] = v
                    if op.fn is None:
                        continue
                    ins = op.fn(engine)
                    if op.is_dma:
                        ins.then_inc(op.dsem, 16)
                    elif op.signal:
                        ins.then_inc(csem[e], 1)
            return body

        with nc.Block() as block:
            block.tensor(make("pe"))
            block.scalar(make("act"))
            block.vector(make("dve"))
            block.gpsimd(make("pool"))
            block.sync(make("sp"))


C_ID = 0
C_IOTA16 = 128
C_IOTA16x16 = 144
C_TRI = 160
C_NEGI = 288
C_NEGS = 416
C_MCUR = 544
C_MPREV = 672
C_PERM = 800
C_INVF = 928
C_END = 932
NEG = -30000.0

W_Q, W_K, W_V, W_Z, W_B, W_A, W_SQ, W_SK, W_SV, W_COLS = 0, 512, 1024, 1536, 2048, 2052, 2056, 2568, 2696, 2824


def make_consts():
    c = np.zeros((128, C_END), np.float32)
    i = np.arange(128)
    c[:, C_ID:C_ID + 128] = np.eye(128, dtype=np.float32)
    c[:, C_IOTA16:C_IOTA16 + 16] = np.arange(16, dtype=np.float32)[None, :]
    c[:, C_IOTA16x16:C_IOTA16x16 + 16] = (16.0 * np.arange(16, dtype=np.float32))[None, :]
    le = (i[:, None] <= i[None, :])
    lt = (i[:, None] < i[None, :])
    c[:, C_TRI:C_TRI + 128] = le.astype(np.float32)
    c[:, C_NEGI:C_NEGI + 128] = np.where(le, 0.0, NEG)
    c[:, C_NEGS:C_NEGS + 128] = np.where(lt, 0.0, NEG)
    c[:, C_MCUR:C_MCUR + 128] = le.astype(np.float32)
    c[:, C_MPREV:C_MPREV + 128] = (~le).astype(np.float32)
    perm = np.zeros((128, 128), np.float32)
    invf = np.zeros((128,), np.float32)
    for m in range(128):
        d = m % 64
        if d < 8:
            perm[m + 8, m] = -1.0
        elif d < 16:
            perm[m - 8, m] = 1.0
        if d < 16:
            invf[m] = np.float32(500000.0) ** np.float32(-(2.0 * (d % 8)) / 16.0)
    c[:, C_PERM:C_PERM + 128] = perm
    c[:, C_INVF] = invf
    return c


class _Cut(Exception):
    pass


def build(nc, NT, dbg=False, cut=99, phase2=True):
    T = NT * 128
    P = Prog(nc)
    es = ExitStack()
    TWO_PI = 6.283185307179586
    PI = 3.141592653589793

    def dram(name, shape, dt, kind):
        return nc.dram_tensor(name, list(shape), dt, kind=kind).ap()

    x_d = dram("x", [T, D], F32, "ExternalInput")
    p_d = dram("p", [T, 256], F32, "ExternalInput")
    pos_d = dram("positions", [T], I32, "ExternalInput")
    consts_d = dram("consts", [128, C_END], F32, "ExternalInput")
    mix_norm_d = dram("mix_norm", [D], F32, "ExternalInput")
    win_d = dram("w_in", [D, W_COLS], F32, "ExternalInput")
    convT_d = dram("convT", [128, 48], F32, "ExternalInput")
    dtb_d = dram("dn_dt_bias", [4], F32, "ExternalInput")
    alog_d = dram("dn_a_log", [4], F32, "ExternalInput")
    onorm_d = dram("dn_out_norm", [128], F32, "ExternalInput")
    sinks_d = dram("attn_sinks", [8], F32, "ExternalInput")
    wout_d = dram("w_out", [D, D], F32, "ExternalInput")
    ffn_norm_d = dram("ffn_norm", [D], F32, "ExternalInput")
    ple_norm_d = dram("ple_norm", [D], F32, "ExternalInput")
    final_norm_d = dram("final_norm", [D], F32, "ExternalInput")
    wq_d = dram("peer_wq", [D, 2048], F32, "ExternalInput")
    keysT_d = dram("peer_keysT", [16, 128, 128], F32, "ExternalInput")
    uv_d = dram("peer_uv", [16384, 2 * D], F32, "ExternalInput")
    uvb_d = dram("peer_uvb", [16384, 2 * D], BF16, UVB_KIND)
    gate_d = dram("ple_gate", [D, D], F32, "ExternalInput")
    proj_d = dram("ple_proj", [256, D], F32, "ExternalInput")
    out_d = dram("out", [T, D], F32, "ExternalOutput")
    x1_d = dram("x1_scratch", [T, D], F32, "ExternalOutput" if dbg else "Internal")
    dbg_d = {}

    def sb(name, shape, dt):
        return es.enter_context(nc.sbuf_tensor("s_" + name, list(shape), dt))

    ps = [es.enter_context(nc.psum_tensor("ps%d" % i, [128, 512], F32)) for i in range(8)]
    PRIV_WORDS = 36378
    priv = sb("priv", [128, PRIV_WORDS], F32)
    cur = [0]

    def pv(name, shape, dt):
        nel = int(np.prod(shape[1:]))
        esz = 4 if dt in (F32, I32, U32) else 2
        nw = (nel * esz + 3) // 4
        assert cur[0] + nw <= PRIV_WORDS, (name, cur[0], nw)
        ap = priv[:, cur[0]:cur[0] + nw]
        cur[0] += nw
        if dt != F32:
            ap = ap.bitcast(dt)
        ap = ap[:, 0:nel]
        if len(shape) == 3:
            ap = ap.rearrange("p (a b) -> p a b", a=shape[1])
        elif len(shape) == 4:
            ap = ap.rearrange("p (a b c) -> p a b c", a=shape[1], b=shape[2])
        return ap

    consts = sb("consts", [128, C_END], F32)
    ident_bf = sb("ident_bf", [128, 128], BF16)
    ones_bf = sb("ones_bf", [128, 128], BF16)
    perm_bf = sb("perm_bf", [128, 128], BF16)
    mcur_bf = sb("mcur_bf", [128, 128], BF16)
    mprev_bf = sb("mprev_bf", [128, 128], BF16)
    arena = sb("arena", [128, 24576], BF16)
    convT = sb("convT", [128, 12, 4], F32)
    dtb_b = sb("dtb_b", [128, 4], F32)
    nea_b = sb("nea_b", [128, 4], F32)
    onorm_b = sb("onorm_b", [128, 128], F32)
    esink_b = sb("esink_b", [128, 8], F32)

    win = arena[:, 0:8 * W_COLS].rearrange("p (k c) -> p k c", k=8)
    wq = arena[:, 0:16384].rearrange("p (k c) -> p k c", k=8)
    gate_w = arena[:, 16384:24576].rearrange("p (k c) -> p k c", k=8)

    x1 = sb("x1t", [128, D], F32)
    y = sb("y", [128, D], F32)
    ss = sb("ss", [128, 8], F32)
    hbf = sb("hbf", [128, D], BF16)
    hT = sb("hT", [128, 8, 128], BF16)
    wout = pv("wout", [128, 8, D], BF16)
    cvb = [pv("cvb%d" % i, [128, D], BF16) for i in range(2)]
    mix_b = pv("mix_b", [128, D], F32)
    zq = pv("zq", [128, 12, 131], F32)
    cacc = pv("cacc", [128, 12, 128], F32)
    qkvc = pv("qkvc", [128, 12, 128], F32)
    sqbf = pv("sqbf", [128, 8, 128], BF16)
    rs8 = pv("rs8", [128, 8, 128], F32)
    qTn = pv("qTn", [128, 4, 128], BF16)
    kTn = pv("kTn", [128, 4, 128], BF16)
    vTb = pv("vTb", [128, 4, 128], BF16)
    bv = pv("bv", [128, 4, 128], BF16)
    tpb = pv("tpb", [128, 4, 128], BF16)
    tpb2 = pv("tpb2", [128, 4, 128], BF16)
    kd = pv("kd", [128, 4, 128], BF16)
    qgT = pv("qgT", [128, 4, 128], BF16)
    g8 = pv("g8", [128, 24], F32)
    q8 = pv("q8", [128, 8], F32)
    nq8 = pv("nq8", [128, 4], F32)
    qrep = pv("qrep", [128, 8, 128], F32)
    egl = pv("egl", [128, 4], F32)
    kds = pv("kds", [128, 4], F32)
    nbeg = pv("nbeg", [128, 4], F32)
    beta = pv("beta", [128, 4], F32)
    egcrow = pv("egcrow", [128, 4, 128], F32)
    tmpI = pv("tmpI", [128, 4, 128], F32)
    tmpS = pv("tmpS", [128, 4, 128], F32)
    DTI = pv("DTI", [128, 4, 128], F32)
    EB = pv("EB", [128, 4, 128], F32)
    aqkT = pv("aqkT", [128, 4, 128], BF16)
    Wm = [pv("Wm%d" % i, [128, 4, 128], BF16) for i in range(2)]
    Vm = [pv("Vm%d" % i, [128, 4, 128], BF16) for i in range(2)]
    VI = pv("VI", [128, 4, 128], BF16)
    Pm = [pv("Pm%d" % i, [128, 4, 128], BF16) for i in range(2)]
    Sst = [pv("Sst%d" % i, [128, 4, 128], BF16) for i in range(2)]
    rr = pv("rr", [128, 4, 128], BF16)
    vnew = pv("vnew", [128, 4, 128], BF16)
    osq = pv("osq", [128, 4, 128], F32)
    o4 = pv("o4", [128, 8], F32)
    ot = pv("ot", [128, 4, 128], F32)
    zs = pv("zs", [128, 512], F32)
    mix = pv("mix", [128, D], BF16)
    mixT = pv("mixT", [128, 8, 128], BF16)
    qraw = pv("qraw", [128, 5, 128], F32)
    qrb = pv("qrb", [128, 5, 128], BF16)
    qrot = pv("qrot", [128, 5, 128], BF16)
    qtmp = pv("qtmp", [128, 5, 128], F32)
    krot = [pv("krot%d" % i, [128, 128], BF16) for i in range(2)]
    v1 = [pv("v1_%d" % i, [128, 2, 65], BF16) for i in range(2)]
    posi = pv("posi", [128, 128], I32)
    ang = pv("ang", [128, 2, 128], F32)
    angk = pv("angk", [128, 2, 128], I32)
    angf = pv("angf", [128, 2, 128], F32)
    sincos = pv("sincos", [128, 2, 128], F32)
    Ecur = [pv("Ecur%d" % i, [128, 512], BF16) for i in range(2)]
    Eprev = [pv("Eprev%d" % i, [128, 512], BF16) for i in range(2)]
    den = pv("den", [128, 8], F32)
    x1I = [x1, pv("x1I2", [128, D], F32)]
    zsp = [zs, pv("zs2", [128, 512], F32)]
    mixp = [mix, pv("mix2", [128, D], BF16)]
    kTnp = [kTn, pv("kTn2", [128, 4, 128], BF16)]
    qgTp = [qgT, pv("qgT2", [128, 4, 128], BF16)]
    kdp = [kd, pv("kd2", [128, 4, 128], BF16)]
    bvp = [bv, pv("bv2", [128, 4, 128], BF16)]
    aqkTp = [aqkT, pv("aqkT2", [128, 4, 128], BF16)]
    nbegp = [nbeg, pv("nbeg2", [128, 4], F32)]
    eglp = [egl, pv("egl2", [128, 4], F32)]
    PTp = [pv("PTp%d" % i, [128, 4, 128], BF16) for i in range(2)]
    if dbg:
        print("priv words phase I:", cur[0])
    cur[0] = 0
    keysT = pv("keysT", [128, 16, 128], BF16)
    ffn_b = pv("ffn_b", [128, D], F32)
    ple_b = pv("ple_b", [128, D], F32)
    fin_b = pv("fin_b", [128, D], F32)
    proj_w = pv("proj_w", [128, 2, D], BF16)
    qT = pv("qT", [128, 16, 128], BF16)
    sc = pv("sc", [128, 16, 128], F32)
    sc2 = sc
    top = pv("top", [128, 16, 16], F32)
    tidx = pv("tidx", [128, 16, 16], U32)
    tidxf = pv("tidxf", [128, 16, 16], F32)
    cand = sc2.rearrange("p (h two) k -> p h (two k)", two=2)
    cand2 = cand
    best = pv("best", [128, 8, 16], F32)
    bpos = pv("bpos", [128, 8, 16], U32)
    bposf = pv("bposf", [128, 8, 16], F32)
    big4 = cand2.rearrange("p h (a b) -> p h a b", b=16)
    asel = pv("asel", [128, 8, 16], F32)
    bsel = pv("bsel", [128, 8, 16], F32)
    isel = pv("isel", [128, 8, 16], F32)
    jsel = pv("jsel", [128, 8, 16], F32)
    ef = pv("ef", [128, 128], F32)
    eidx = pv("eidx", [128, 128], I32)
    gates = pv("gates", [128, 8, 16], F32)
    gz = pv("gz", [128, 8], F32)
    actpre = pv("actpre", [128, 128], F32)
    gl_a = pv("gl_a", [128, 128], F32)
    gl_b = pv("gl_b", [128, 128], F32)
    wts = pv("wts", [128, 128], F32)
    NG = 20
    GS = 4
    gbw = [pv("gb%d" % i, [128, D], F32) for i in range(NG)]
    gb = [w_.bitcast(BF16) for w_ in gbw]
    prod = [pv("prod%d" % i, [128, D], BF16) for i in range(3)]
    dg = [pv("dg%d" % i, [128, 4, 128], BF16) for i in range(2)]
    pt = pv("pt", [128, 256], F32)
    ptbf = pv("ptbf", [128, 256], BF16)
    pT = pv("pT", [128, 2, 128], BF16)
    if dbg:
        print("priv words phase II:", cur[0])
    x1s = [x1, pv("x1b", [128, D], F32)]
    hbfs = [hbf, pv("hbf2", [128, D], BF16)]
    eidxs = [eidx, pv("eidx2", [128, 128], I32)]
    gatess = [gates, pv("gates2", [128, 8, 16], F32)]
    tbf = pv("tbf", [128, D], BF16)
    sig = gbw[1]
    x3 = gbw[2]
    outt = gbw[3]

    def psm(i):
        return ps[P.bankmap[i]]

    def psbf(i):
        return psm(i)[:].bitcast(BF16)

    def ps4(i):
        return psm(i)[:].rearrange("p (h t) -> p h t", h=4)

    def cst(off, n=128):
        return consts[:, off:off + n]

    P.dma("sp", lambda e: e.dma_start(out=consts[:], in_=consts_d), w=["consts"])
    for (tile_, src, key) in ((mix_b, mix_norm_d, "mix_b"),
                              (dtb_b, dtb_d, "dtb_b"), (nea_b, alog_d, "nea_b"),
                              (onorm_b, onorm_d, "onorm_b"), (esink_b, sinks_d, "esink_b")):
        P.dma("sp", lambda e, tile_=tile_, src=src: e.dma_start(out=tile_[:], in_=src.partition_broadcast(128)),
              w=[key])
    P.dma("sp", lambda e: e.dma_start(out=convT[:].rearrange("p c j -> p (c j)"), in_=convT_d), w=["convT"])
    for kc in range(8):
        P.dma("pool", lambda e, kc=kc: e.dma_start(out=arena[:, kc * W_COLS:(kc + 1) * W_COLS],
                                                   in_=win_d[kc * 128:(kc + 1) * 128, :]), w=["arena"])
    for kc in range(8):
        P.dma("pool", lambda e, kc=kc: e.dma_start(out=wout[:, kc, :], in_=wout_d[kc * 128:(kc + 1) * 128, :]),
              w=["wout"])
    P.op("dve", lambda e: e.tensor_copy(out=ident_bf[:], in_=cst(C_ID)), r=["consts"], w=["ident_bf"])
    P.op("dve", lambda e: e.tensor_copy(out=perm_bf[:], in_=cst(C_PERM)), r=["consts"], w=["perm_bf"])
    P.op("dve", lambda e: e.tensor_copy(out=mcur_bf[:], in_=cst(C_MCUR)), r=["consts"], w=["mcur_bf"])
    P.op("dve", lambda e: e.tensor_copy(out=mprev_bf[:], in_=cst(C_MPREV)), r=["consts"], w=["mprev_bf"])
    P.op("dve", lambda e: e.memset(ones_bf[:], 1.0), w=["ones_bf"])
    P.op("dve", lambda e: e.memset(zq[:], 0.0), w=["zq"])
    P.op("dve", lambda e: e.memset(Sst[0][:], 0.0), w=[("S", 0)])
    for i in range(2):
        P.op("dve", lambda e, i=i: e.memset(v1[i][:], 1.0), w=[("v1", i)])
    P.op("act", lambda e: e.activation(out=nea_b[:], in_=nea_b[:], func=AF.Exp), r=["nea_b"], w=["nea_b"])
    P.op("dve", lambda e: e.tensor_scalar(out=nea_b[:], in0=nea_b[:], scalar1=-1.0, scalar2=None, op0=ALU.mult),
         r=["nea_b"], w=["nea_b"])
    P.op("act", lambda e: e.activation(out=esink_b[:], in_=esink_b[:], func=AF.Exp), r=["esink_b"], w=["esink_b"])

    iota16 = consts[:, C_IOTA16:C_IOTA16 + 16]
    iota16x16 = consts[:, C_IOTA16x16:C_IOTA16x16 + 16]

    def rmsnorm(src, src_key, gain_b, gain_key, dst_f32, dst_key, col, dst_bf=None, dst_bf_key=None):
        P.op("act", lambda e: e.activation(out=dst_f32[:], in_=src[:], func=AF.Square,
                                           accum_out=ss[:, col:col + 1]),
             r=[src_key], w=[dst_key, ("ss", col)])
        P.op("act", lambda e: e.activation(out=ss[:, col + 1:col + 2], in_=ss[:, col:col + 1], func=AF.Sqrt,
                                           scale=1.0 / D, bias=EPS),
             r=[("ss", col)], w=[("ss", col + 1)])
        P.op("dve", lambda e: e.reciprocal(out=ss[:, col + 1:col + 2], in_=ss[:, col + 1:col + 2]),
             r=[("ss", col + 1)], w=[("ss", col + 1)])
        P.op("dve", lambda e: e.scalar_tensor_tensor(out=dst_f32[:], in0=src[:], scalar=ss[:, col + 1:col + 2],
                                                     in1=gain_b[:], op0=ALU.mult, op1=ALU.mult),
             r=[src_key, ("ss", col + 1), gain_key], w=[dst_key])
        if dst_bf is not None:
            P.op("act", lambda e: e.copy(out=dst_bf[:], in_=dst_f32[:]), r=[dst_key], w=[dst_bf_key])

    def transpose8(src_bf, src_key, dst, dst_key, nch, bank):
        pb = psbf(bank)
        for c in range(nch):
            P.op("pe", lambda e, c=c: e.transpose(out=pb[:, c * 128:(c + 1) * 128],
                                                  in_=src_bf[:, c * 128:(c + 1) * 128], identity=ident_bf[:]),
                 r=(list(src_key) if isinstance(src_key, list) else [src_key]) + ["ident_bf"], w=[("ps", bank)])
        P.op("dve", lambda e: e.tensor_copy(out=dst[:].rearrange("p c t -> p (c t)"),
                                            in_=pb[:, 0:nch * 128]),
             r=[("ps", bank)], w=[dst_key])

    def mm(out, lhsT, rhs, start, stop, r, bank):
        P.op("pe", lambda e: e.matmul(out=out, lhsT=lhsT, rhs=rhs, start=start, stop=stop),
             r=r, w=[("ps", bank)])

    def dve(fn, r, w):
        P.op("dve", fn, r=r, w=w)

    def ck(n):
        if cut == n:
            raise _Cut()

    def act(fn, r, w):
        P.op("act", fn, r=r, w=w)

    def proj_fm(col, bank, slot):
        for kc in range(8):
            mm(psm(bank)[:, slot * 128:(slot + 1) * 128], win[:, kc, col:col + 128], hT[:, kc, :],
               kc == 0, kc == 7, ["arena", "hT"], bank)

    def frontA(ti):
        P.bankmap[:] = [0, 1, 2, 0, 1, 0, 1, 2]
        t0 = ti * 128
        par = ti % 2
        x1, zs, mix = x1I[par], zsp[par], mixp[par]
        kTn, qgT, kd, bv, aqkT = kTnp[par], qgTp[par], kdp[par], bvp[par], aqkTp[par]
        nbeg, egl = nbegp[par], eglp[par]
        for cc in range(256 // NT):
            ci = ti * (256 // NT) + cc
            rows = slice((ci // 2) * 128, (ci // 2) * 128 + 128)
            cols = slice((ci % 2) * D, (ci % 2) * D + D)
            cb = ci % 2
            P.dma("pool", lambda e: e.dma_start(out=cvb[cb][:], in_=uv_d[rows, cols]), w=[("cvb", cb)])
            P.dma("pool", lambda e: e.dma_start(out=uvb_d[rows, cols], in_=cvb[cb][:]), r=[("cvb", cb)],
                  w=[("uvb", ci)])
        P.dma("sp", lambda e, t0=t0: e.dma_start(out=x1[:], in_=x_d[t0:t0 + 128, :]), w=[("x1", par)])
        P.dma("sp", lambda e, t0=t0: e.dma_start(out=posi[:], in_=pos_d[t0:t0 + 128].partition_broadcast(128)),
              w=["posi"])
        rmsnorm(x1, ("x1", par), mix_b, "mix_b", hbf, "hbf", 0)
        transpose8(hbf, "hbf", hT, "hT", 8, 0)

        for grp in range(3):
            bank = 1 + grp
            for s4 in range(4):
                proj_fm((grp * 4 + s4) * 128, bank, s4)
            act(lambda e, grp=grp, bank=bank: e.copy(out=zq[:, grp * 4:(grp + 1) * 4, 3:131], in_=ps4(bank)),
                [("ps", bank)], ["zq"])
        for kc in range(8):
            mm(psm(6)[:, 0:512], hT[:, kc, :], win[:, kc, W_Z:W_Z + 512], kc == 0, kc == 7, ["arena", "hT"], 6)
        for kc in range(8):
            mm(psm(7)[:, 0:8], hT[:, kc, :], win[:, kc, W_B:W_B + 8], kc == 0, kc == 7, ["arena", "hT"], 7)
        act(lambda e: e.activation(out=zs[:], in_=psm(6)[:], func=AF.Silu), [("ps", 6)], [("zs", par)])
        act(lambda e: e.activation(out=beta[:], in_=psm(7)[:, 0:4], func=AF.Sigmoid), [("ps", 7)], ["beta"])
        dve(lambda e: e.tensor_tensor(out=g8[:, 4:8], in0=dtb_b[:], in1=psm(7)[:, 4:8], op=ALU.add),
            [("ps", 7), "dtb_b"], ["g8a"])
        act(lambda e: e.activation(out=g8[:, 8:12], in_=g8[:, 4:8], func=AF.Abs), ["g8a"], ["g8b"])
        act(lambda e: e.activation(out=g8[:, 12:16], in_=g8[:, 8:12], func=AF.Exp, scale=-1.0), ["g8b"], ["g8c"])
        act(lambda e: e.activation(out=g8[:, 12:16], in_=g8[:, 12:16], func=AF.Ln, bias=1.0), ["g8c"], ["g8c"])
        act(lambda e: e.activation(out=g8[:, 20:24], in_=beta[:], func=AF.Ln), ["beta"], ["g8e"])
        dve(lambda e: e.scalar_tensor_tensor(out=g8[:, 16:20], in0=g8[:, 4:8], scalar=0.0, in1=g8[:, 12:16],
                                             op0=ALU.max, op1=ALU.add), ["g8a", "g8c"], ["g8d"])
        dve(lambda e: e.tensor_tensor(out=g8[:, 16:20], in0=g8[:, 16:20], in1=nea_b[:], op=ALU.mult),
            ["g8d", "nea_b"], ["g8d"])
        mm(psm(7)[:, 256:260], cst(C_TRI), g8[:, 16:20], True, True, ["consts", "g8d"], 7)
        dve(lambda e: e.tensor_copy(out=q8[:, 0:4], in_=psm(7)[:, 256:260]), [("ps", 7)], ["q8a"])
        dve(lambda e: e.tensor_tensor(out=q8[:, 4:8], in0=q8[:, 0:4], in1=g8[:, 20:24], op=ALU.add),
            ["q8a", "g8e"], ["q8b"])
        dve(lambda e: e.tensor_copy(out=qrep[:], in_=q8[:].unsqueeze(2).to_broadcast([128, 8, 128])),
            ["q8a", "q8b"], ["qrep"])
        for h in range(4):
            mm(psm(6)[:, h * 128:(h + 1) * 128], qrep[:, h, :], cst(C_ID), True, True, ["qrep", "consts", ("zs", par)], 6)
        for h in range(4):
            mm(psm(7)[:, h * 128:(h + 1) * 128], qrep[:, 4 + h, :], cst(C_ID), True, True,
               ["qrep", "consts", "q8a", ("v1", par)], 7)
        gl_last = ps4(6)[:, :, 127]
        act(lambda e: e.activation(out=egl[:], in_=gl_last, func=AF.Exp), [("ps", 6)], [("egl", par)])
        dve(lambda e: e.tensor_tensor(out=kds[:], in0=q8[:, 0:4], in1=gl_last, op=ALU.subtract),
            [("ps", 6), "q8a"], ["kds"])
        act(lambda e: e.activation(out=kds[:], in_=kds[:], func=AF.Exp, scale=-1.0), ["kds"], ["kds"])
        act(lambda e: e.activation(out=nbeg[:], in_=q8[:, 0:4], func=AF.Exp), ["q8a"], [("nbeg", par)])
        dve(lambda e: e.scalar_tensor_tensor(out=nbeg[:], in0=nbeg[:], scalar=-1.0, in1=beta[:],
                                             op0=ALU.mult, op1=ALU.mult), [("nbeg", par), "beta"], [("nbeg", par)])
        act(lambda e: e.activation(out=egcrow[:], in_=ps4(6), func=AF.Exp), [("ps", 6)], ["egcrow"])
        dve(lambda e: e.tensor_scalar(out=nq8[:], in0=q8[:, 0:4], scalar1=-1.0, scalar2=None, op0=ALU.mult),
            ["q8a"], ["nq8"])
        for h in range(4):
            dve(lambda e, h=h: e.scalar_tensor_tensor(out=tmpI[:, h, :], in0=cst(C_NEGI),
                                                      scalar=nq8[:, h:h + 1], in1=psm(6)[:, h * 128:(h + 1) * 128],
                                                      op0=ALU.add, op1=ALU.add),
                [("ps", 6), "nq8", "consts"], [("tmpI", h)])
            dve(lambda e, h=h: e.scalar_tensor_tensor(out=tmpS[:, h, :], in0=cst(C_NEGS),
                                                      scalar=nq8[:, h:h + 1], in1=psm(7)[:, h * 128:(h + 1) * 128],
                                                      op0=ALU.add, op1=ALU.add),
                [("ps", 7), "nq8", "consts"], [("tmpS", h)])
        act(lambda e: e.activation(out=DTI[:], in_=tmpI[:], func=AF.Exp), [("tmpI", h) for h in range(4)], ["DTI"])
        act(lambda e: e.activation(out=EB[:], in_=tmpS[:], func=AF.Exp), [("tmpS", h) for h in range(4)], ["EB"])

        for ch in range(12):
            dve(lambda e, ch=ch: e.tensor_scalar(out=cacc[:, ch, :], in0=zq[:, ch, 3:131],
                                                 scalar1=convT[:, ch, 3:4], scalar2=None, op0=ALU.mult),
                ["zq", "convT"], [("cacc", ch)])
            for j in range(3):
                dve(lambda e, ch=ch, j=j: e.scalar_tensor_tensor(
                    out=cacc[:, ch, :], in0=zq[:, ch, j:j + 128], scalar=convT[:, ch, j:j + 1],
                    in1=cacc[:, ch, :], op0=ALU.mult, op1=ALU.add),
                    ["zq", "convT", ("cacc", ch)], [("cacc", ch)])
        dve(lambda e: e.tensor_copy(out=zq[:, :, 0:3], in_=zq[:, :, 128:131]), ["zq"], ["zq"])
        cacc_all = [("cacc", ch) for ch in range(12)]
        act(lambda e: e.activation(out=qkvc[:], in_=cacc[:], func=AF.Silu), cacc_all, ["qkvc"])
        act(lambda e: e.activation(out=sqbf[:], in_=qkvc[:, 0:8, :], func=AF.Square), ["qkvc"], ["sqbf"])
        for half in range(2):
            bank = 1 + half
            for s4 in range(4):
                mm(psm(bank)[:, s4 * 128:(s4 + 1) * 128], ones_bf[:], sqbf[:, half * 4 + s4, :], True, True,
                   ["ones_bf", "sqbf"], bank)
        act(lambda e: e.activation(out=rs8[:, 0:4, :], in_=ps4(1), func=AF.Sqrt, scale=128.0, bias=128.0 * EPS),
            [("ps", 1)], [("rs8", 0)])
        act(lambda e: e.activation(out=rs8[:, 4:8, :], in_=ps4(2), func=AF.Sqrt, scale=1.0, bias=EPS),
            [("ps", 2)], [("rs8", 1)])
        dve(lambda e: e.reciprocal(out=rs8[:], in_=rs8[:]), [("rs8", 0), ("rs8", 1)], ["rs8"])
        dve(lambda e: e.tensor_tensor(out=qTn[:], in0=qkvc[:, 0:4, :], in1=rs8[:, 0:4, :], op=ALU.mult),
            ["qkvc", "rs8"], ["qTn"])
        dve(lambda e: e.tensor_tensor(out=kTn[:], in0=qkvc[:, 4:8, :], in1=rs8[:, 4:8, :], op=ALU.mult),
            ["qkvc", "rs8"], [("kTn", par)])
        act(lambda e: e.copy(out=vTb[:], in_=qkvc[:, 8:12, :]), ["qkvc"], ["vTb"])
        dve(lambda e: e.tensor_tensor(out=qgT[:], in0=qTn[:], in1=egcrow[:], op=ALU.mult),
            ["qTn", "egcrow"], [("qgT", par)])
        pb1 = psbf(1)
        for h in range(4):
            P.op("pe", lambda e, h=h: e.transpose(out=pb1[:, h * 128:(h + 1) * 128], in_=vTb[:, h, :],
                                                  identity=ident_bf[:]),
                 r=["vTb", "ident_bf", ("rs8", 0)], w=[("ps", 1)])
        dve(lambda e: e.tensor_copy(out=tpb[:].rearrange("p h d -> p (h d)"), in_=pb1[:, 0:512]),
            [("ps", 1)], ["tpb"])
        dve(lambda e: e.tensor_tensor(out=bv[:], in0=tpb[:],
                                      in1=beta[:].unsqueeze(2).to_broadcast([128, 4, 128]), op=ALU.mult),
            ["tpb", "beta"], [("bv", par)])
        pb2 = psbf(2)
        for h in range(4):
            P.op("pe", lambda e, h=h: e.transpose(out=pb2[:, h * 128:(h + 1) * 128], in_=kTn[:, h, :],
                                                  identity=ident_bf[:]),
                 r=[("kTn", par), "ident_bf", ("rs8", 1)], w=[("ps", 2)])
        dve(lambda e: e.tensor_copy(out=tpb2[:].rearrange("p h d -> p (h d)"), in_=pb2[:, 0:512]),
            [("ps", 2)], ["tpb2"])
        dve(lambda e: e.tensor_tensor(out=kd[:], in0=tpb2[:],
                                      in1=kds[:].unsqueeze(2).to_broadcast([128, 4, 128]), op=ALU.mult),
            ["tpb2", "kds"], [("kd", par)])
        for h in range(4):
            mm(psm(3)[:, h * 128:(h + 1) * 128], kTn[:, h, :], kTn[:, h, :], True, True, [("kTn", par), "zq"], 3)
        for h in range(4):
            mm(psm(4)[:, h * 128:(h + 1) * 128], kTn[:, h, :], qTn[:, h, :], True, True, [("kTn", par), "qTn", "qraw"], 4)
        dve(lambda e: e.tensor_tensor(out=Wm[0][:], in0=EB[:], in1=ps4(3), op=ALU.mult),
            [("ps", 3), "EB"], [("Wm", 0)])
        dve(lambda e: e.tensor_tensor(out=aqkT[:], in0=DTI[:], in1=ps4(4), op=ALU.mult),
            [("ps", 4), "DTI"], [("aqkT", par)])
        pb3 = psbf(3)
        for h in range(4):
            P.op("pe", lambda e, h=h: e.transpose(out=pb3[:, h * 128:(h + 1) * 128], in_=Wm[0][:, h, :],
                                                  identity=ident_bf[:]),
                 r=[("Wm", 0), "ident_bf"], w=[("ps", 3)])
        dve(lambda e: e.tensor_copy(out=Vm[0][:].rearrange("p h d -> p (h d)"), in_=pb3[:, 0:512]),
            [("ps", 3)], [("Vm", 0)])
        dve(lambda e: e.scalar_tensor_tensor(out=Pm[0][:], in0=Wm[0][:], scalar=-1.0,
                                             in1=ident_bf[:].unsqueeze(1).to_broadcast([128, 4, 128]),
                                             op0=ALU.mult, op1=ALU.add), [("Wm", 0), "ident_bf"], [("Pm", 0)])
        cw, cp = 0, 0
        for m in range(6):
            nw = 1 - cw
            last = (m == 5)
            if not last:
                for h in range(4):
                    mm(psm(1)[:, h * 128:(h + 1) * 128], Vm[cw][:, h, :], Wm[cw][:, h, :], True, True,
                       [("Vm", cw), ("Wm", cw), ("bv", par)], 1)
            for h in range(4):
                mm(psm(2)[:, h * 128:(h + 1) * 128], Wm[cw][:, h, :], Vm[cw][:, h, :], True, True,
                   [("Vm", cw), ("Wm", cw), ("kd", par)], 2)
            if not last:
                act(lambda e, nw=nw: e.copy(out=Wm[nw][:], in_=ps4(1)), [("ps", 1)], [("Wm", nw)])
            act(lambda e, nw=nw: e.copy(out=Vm[nw][:], in_=ps4(2)), [("ps", 2)], [("Vm", nw)])
            dve(lambda e, nw=nw: e.tensor_tensor(out=VI[:], in0=Vm[nw][:],
                                                 in1=ident_bf[:].unsqueeze(1).to_broadcast([128, 4, 128]),
                                                 op=ALU.add), [("Vm", nw), "ident_bf"], ["VI"])
            for h in range(4):
                mm(psm(3)[:, h * 128:(h + 1) * 128], VI[:, h, :], Pm[cp][:, h, :], True, True,
                   ["VI", ("Pm", cp)], 3)
            dve(lambda e, cp=cp: e.tensor_copy(out=Pm[1 - cp][:], in_=ps4(3)), [("ps", 3)], [("Pm", 1 - cp)])
            cw, cp = nw, 1 - cp
        dve(lambda e: e.tensor_copy(out=PTp[par][:], in_=Pm[cp][:]), [("Pm", cp)], [("PT", par)])

    def swa(ti):
        P.bankmap[:] = [3, 3, 3, 3, 3, 4, 4, 4]
        t0 = ti * 128
        par = ti % 2
        x1, zs, mix = x1I[par], zsp[par], mixp[par]
        kTn, qgT, kd, bv, aqkT = kTnp[par], qgTp[par], kdp[par], bvp[par], aqkTp[par]
        nbeg, egl = nbegp[par], eglp[par]
        for s4 in range(4):
            proj_fm(W_SQ + s4 * 128, 4, s4)
        proj_fm(W_SK, 5, 0)
        act(lambda e: e.copy(out=qraw[:, 0:4, :], in_=ps4(4)), [("ps", 4)], ["qraw"])
        act(lambda e: e.copy(out=qraw[:, 4, :], in_=psm(5)[:, 0:128]), [("ps", 5)], ["qraw"])
        for kc in range(8):
            mm(psm(7)[:, 128:256], hT[:, kc, :], win[:, kc, W_SV:W_SV + 128], kc == 0, kc == 7, ["arena", "hT"], 7)
        dve(lambda e, par=par: e.tensor_copy(out=v1[par][:, :, 0:64],
                                             in_=psm(7)[:, 128:256].rearrange("p (g d) -> p g d", g=2)),
            [("ps", 7)], [("v1", par)])
        dve(lambda e: e.tensor_copy(out=ang[:, 0, :], in_=posi[:]), ["posi"], ["ang0"])
        dve(lambda e: e.tensor_scalar(out=ang[:, 0, :], in0=ang[:, 0, :], scalar1=cst(C_INVF, 1), scalar2=None,
                                      op0=ALU.mult), ["ang0", "consts"], ["ang0"])
        dve(lambda e: e.tensor_scalar(out=ang[:, 1, :], in0=ang[:, 0, :], scalar1=PI / 2, scalar2=None,
                                      op0=ALU.add), ["ang0"], ["ang1"])
        dve(lambda e: e.tensor_scalar(out=angk[:], in0=ang[:], scalar1=1.0 / TWO_PI, scalar2=None, op0=ALU.mult),
            ["ang0", "ang1"], ["angk"])
        dve(lambda e: e.tensor_copy(out=angf[:], in_=angk[:]), ["angk"], ["angf"])
        dve(lambda e: e.scalar_tensor_tensor(out=ang[:], in0=angf[:], scalar=-TWO_PI, in1=ang[:],
                                             op0=ALU.mult, op1=ALU.add), ["angf", "ang0", "ang1"], ["ang"])
        dve(lambda e: e.tensor_single_scalar(out=angf[:], in_=ang[:], scalar=PI, op=ALU.is_gt), ["ang"], ["angf"])
        dve(lambda e: e.scalar_tensor_tensor(out=ang[:], in0=angf[:], scalar=-TWO_PI, in1=ang[:],
                                             op0=ALU.mult, op1=ALU.add), ["angf", "ang"], ["ang"])
        act(lambda e: e.activation(out=sincos[:], in_=ang[:], func=AF.Sin), ["ang"], ["sincos"])
        act(lambda e: e.copy(out=qrb[:], in_=qraw[:]), ["qraw"], ["qrb"])
        for c in range(4):
            mm(psm(5)[:, c * 128:(c + 1) * 128], perm_bf[:], qrb[:, c, :], True, True, ["perm_bf", "qrb"], 5)
        mm(psm(1)[:, 0:128], perm_bf[:], qrb[:, 4, :], True, True, ["perm_bf", "qrb", ("rr", 0), ("rr", 1), ("rr", 2), ("rr", 3)], 1)
        dve(lambda e: e.tensor_tensor(out=qtmp[:, 0:4, :], in0=sincos[:, 0:1, :].to_broadcast([128, 4, 128]),
                                      in1=ps4(5), op=ALU.mult),
            [("ps", 5), "sincos"], ["qtmp"])
        dve(lambda e: e.tensor_tensor(out=qtmp[:, 4, :], in0=sincos[:, 0, :], in1=psm(1)[:, 0:128], op=ALU.mult),
            [("ps", 1), "sincos"], ["qtmp"])
        dve(lambda e: e.tensor_tensor(out=qraw[:], in0=qraw[:],
                                      in1=sincos[:, 1:2, :].to_broadcast([128, 5, 128]), op=ALU.mult),
            ["qraw", "sincos", "qrb"], ["qraw"])
        dve(lambda e: e.tensor_tensor(out=qrot[:, 0:4, :], in0=qraw[:, 0:4, :], in1=qtmp[:, 0:4, :], op=ALU.add),
            ["qraw", "qtmp"], ["qrot"])
        dve(lambda e, par=par: e.tensor_tensor(out=krot[par][:], in0=qraw[:, 4, :], in1=qtmp[:, 4, :], op=ALU.add),
            ["qraw", "qtmp"], [("krot", par)])
        for g in range(2):
            lo, hi = 64 * g, 64 * g + 64
            mm(psm(1 + g)[:], krot[par][lo:hi, :], qrot[lo:hi, 0:4, :].rearrange("p c t -> p (c t)"), True, True,
               [("krot", par), "qrot", "vnew", ("qtmp")], 1 + g)
            act(lambda e, g=g: e.activation(out=Ecur[g][:], in_=psm(1 + g)[:], func=AF.Exp, scale=0.125),
                [("ps", 1 + g)], [("Ecur", g)])
            dve(lambda e, g=g: e.tensor_tensor(out=Ecur[g][:].rearrange("p (c t) -> p c t", c=4),
                                               in0=Ecur[g][:].rearrange("p (c t) -> p c t", c=4),
                                               in1=mcur_bf[:].unsqueeze(1).to_broadcast([128, 4, 128]), op=ALU.mult),
                [("Ecur", g), "mcur_bf"], [("Ecur", g)])
            if ti > 0:
                mm(psm(5 + g)[:], krot[1 - par][lo:hi, :], qrot[lo:hi, 0:4, :].rearrange("p c t -> p (c t)"),
                   True, True, [("krot", 1 - par), "qrot", ("S", 1 - par), "qtmp"], 5 + g)
                act(lambda e, g=g: e.activation(out=Eprev[g][:], in_=psm(5 + g)[:], func=AF.Exp, scale=0.125),
                    [("ps", 5 + g)], [("Eprev", g)])
                dve(lambda e, g=g: e.tensor_tensor(out=Eprev[g][:].rearrange("p (c t) -> p c t", c=4),
                                                   in0=Eprev[g][:].rearrange("p (c t) -> p c t", c=4),
                                                   in1=mprev_bf[:].unsqueeze(1).to_broadcast([128, 4, 128]),
                                                   op=ALU.mult),
                    [("Eprev", g), "mprev_bf"], [("Eprev", g)])
        for g in range(2):
            bank = 3 if g == 0 else 7
            for c in range(4):
                dst = psm(bank)[:, c * 65:(c + 1) * 65]
                deps = [("Ecur", g), ("v1", par), "ot", "tmpS_all"]
                if ti > 0:
                    mm(dst, Eprev[g][:, c * 128:(c + 1) * 128], v1[1 - par][:, g, :], True, False,
                       deps + [("Eprev", g), ("v1", 1 - par)], bank)
                    mm(dst, Ecur[g][:, c * 128:(c + 1) * 128], v1[par][:, g, :], False, True, deps, bank)
                else:
                    mm(dst, Ecur[g][:, c * 128:(c + 1) * 128], v1[par][:, g, :], True, True, deps, bank)
            pv = psm(bank)[:, 0:260].rearrange("p (c d) -> p c d", c=4)
            dve(lambda e, g=g, pv=pv: e.tensor_tensor(out=den[:, g * 4:(g + 1) * 4], in0=esink_b[:, g * 4:(g + 1) * 4],
                                                      in1=pv[:, :, 64], op=ALU.add),
                [("ps", bank), "esink_b"], [("den", g)])
            dve(lambda e, g=g: e.reciprocal(out=den[:, g * 4:(g + 1) * 4], in_=den[:, g * 4:(g + 1) * 4]),
                [("den", g)], [("den", g)])
            dve(lambda e, g=g, pv=pv: e.tensor_tensor(
                out=mix[:, 512 + g * 256:512 + (g + 1) * 256].rearrange("p (c d) -> p c d", c=4),
                in0=den[:, g * 4:(g + 1) * 4].unsqueeze(2).to_broadcast([128, 4, 64]), in1=pv[:, :, 0:64],
                op=ALU.mult), [("ps", bank), ("den", g)], [("mix", par, 1 + g)])

    def back(ti):
        P.bankmap[:] = [6, 5, 6, 3, 7, 5, 6, 7]
        t0 = ti * 128
        par = ti % 2
        x1, zs, mix = x1I[par], zsp[par], mixp[par]
        kTn, qgT, kd, bv, aqkT = kTnp[par], qgTp[par], kdp[par], bvp[par], aqkTp[par]
        nbeg, egl = nbegp[par], eglp[par]
        PT = PTp[par]
        PTk = ("PT", par)
        So, Sn = Sst[par], Sst[1 - par]
        for h in range(4):
            mm(psm(1)[:, h * 128:(h + 1) * 128], kTn[:, h, :], So[:, h, :], True, True, [("kTn", par), ("S", par)], 1)
        dve(lambda e: e.tensor_tensor(out=osq[:], in0=nbeg[:].unsqueeze(2).to_broadcast([128, 4, 128]),
                                      in1=ps4(1), op=ALU.mult),
            [("ps", 1), ("nbeg", par)], ["osq"])
        dve(lambda e: e.tensor_tensor(out=rr[:], in0=osq[:], in1=bv[:], op=ALU.add),
            ["osq", ("bv", par)], [("rr", h) for h in range(4)])
        for h in range(4):
            mm(psm(2)[:, h * 128:(h + 1) * 128], PT[:, h, :], rr[:, h, :], True, True, [PTk, ("rr", h)], 2)
        act(lambda e: e.copy(out=vnew[:], in_=ps4(2)), [("ps", 2)], ["vnew"])
        for h in range(4):
            mm(psm(4)[:, h * 128:(h + 1) * 128], qgT[:, h, :], So[:, h, :], True, False, [("qgT", par), ("S", par), ("aqkT", par)], 4)
            mm(psm(4)[:, h * 128:(h + 1) * 128], aqkT[:, h, :], vnew[:, h, :], False, True, [("aqkT", par), "vnew"], 4)
        for h in range(4):
            mm(psm(5)[:, h * 128:(h + 1) * 128], kd[:, h, :], vnew[:, h, :], True, True, [("kd", par), "vnew", "qraw"], 5)
        for h in range(4):
            dve(lambda e, h=h: e.scalar_tensor_tensor(out=Sn[:, h, :], in0=So[:, h, :], scalar=egl[:, h:h + 1],
                                                      in1=psm(5)[:, h * 128:(h + 1) * 128],
                                                      op0=ALU.mult, op1=ALU.add),
                [("S", par), ("egl", par), ("ps", 5)], [("S", 1 - par)])
        act(lambda e: e.activation(out=osq[:], in_=ps4(4), func=AF.Square), [("ps", 4)], ["osq"])
        dve(lambda e: e.tensor_reduce(out=o4[:, 0:4], in_=osq[:], axis=AX.X, op=ALU.add), ["osq"], ["o4"])
        act(lambda e: e.activation(out=o4[:, 4:8], in_=o4[:, 0:4], func=AF.Sqrt, scale=1.0 / 128.0, bias=EPS),
            ["o4"], ["o4b"])
        dve(lambda e: e.reciprocal(out=o4[:, 4:8], in_=o4[:, 4:8]), ["o4b"], ["o4b"])
        dve(lambda e: e.tensor_tensor(out=ot[:], in0=o4[:, 4:8].unsqueeze(2).to_broadcast([128, 4, 128]),
                                      in1=ps4(4), op=ALU.mult),
            [("ps", 4), "o4b"], ["ot"])
        dve(lambda e: e.tensor_tensor(out=ot[:], in0=ot[:],
                                      in1=onorm_b[:].unsqueeze(1).to_broadcast([128, 4, 128]), op=ALU.mult),
            ["ot", "onorm_b"], ["ot"])
        dve(lambda e: e.tensor_tensor(out=mix[:, 0:512], in0=ot[:].rearrange("p h d -> p (h d)"), in1=zs[:],
                                      op=ALU.mult), ["ot", ("zs", par)], [("mix", par, 0)])

        transpose8(mix, [("mix", par, 0), ("mix", par, 1), ("mix", par, 2)], mixT, "mixT", 8, 0)
        for half in range(2):
            bank = 4 + half
            for kc in range(8):
                mm(psm(bank)[:], mixT[:, kc, :], wout[:, kc, half * 512:(half + 1) * 512], kc == 0, kc == 7,
                   ["mixT", "wout", "osq", "ot", "qtmp"], bank)
            dve(lambda e, half=half, bank=bank: e.tensor_tensor(
                out=y[:, half * 512:(half + 1) * 512], in0=x1[:, half * 512:(half + 1) * 512], in1=psm(bank)[:],
                op=ALU.add), [("ps", bank), ("x1", par)], ["y"])
        P.dma("sp", lambda e, t0=t0: e.dma_start(out=x1_d[t0:t0 + 128, :], in_=y[:]),
              r=["y"], w=[("x1_d", ti)])

    def rec(fn, ti):
        P.begin()
        fn(ti)
        return P.end()

    def swa_start(fa):
        idx = max(j for j, it in enumerate(fa) if ("hT" in it[3] or "posi" in it[3]))
        return (idx + 1.0) / len(fa)

    fa = rec(frontA, 0)
    P.merge([fa, rec(swa, 0)], starts=[0.0, swa_start(fa)])
    for ti in range(NT):
        if ti + 1 < NT:
            fa = rec(frontA, ti + 1)
            P.merge([fa, rec(swa, ti + 1), rec(back, ti)], starts=[0.0, swa_start(fa), 0.0])
        else:
            back(ti)
    P.bankmap[:] = list(range(8))


    if phase2:
        for kc in range(8):
            P.dma("pool", lambda e, kc=kc: e.dma_start(out=arena[:, kc * 2048:(kc + 1) * 2048],
                                                       in_=wq_d[kc * 128:(kc + 1) * 128, :]), w=["arena"])
        for kc in range(8):
            P.dma("pool", lambda e, kc=kc: e.dma_start(out=arena[:, 16384 + kc * 1024:16384 + (kc + 1) * 1024],
                                                       in_=gate_d[kc * 128:(kc + 1) * 128, :]), w=["arena"])
        P.barrier()
        for (tile_, src, key) in ((ffn_b, ffn_norm_d, "ffn_b"), (ple_b, ple_norm_d, "ple_b"),
                                  (fin_b, final_norm_d, "fin_b")):
            P.dma("sp", lambda e: e.dma_start(out=tile_[:], in_=src.partition_broadcast(128)), w=[key])
        P.dma("pool", lambda e: e.dma_start(out=keysT[:], in_=keysT_d.rearrange("g d n -> d g n")), w=["keysT"])
        for kc in range(2):
            P.dma("pool", lambda e, kc=kc: e.dma_start(out=proj_w[:, kc, :], in_=proj_d[kc * 128:(kc + 1) * 128, :]),
                  w=["proj_w"])
        def head(ti):
            t0 = ti * 128
            pq = ti % 2
            x1c, hbfc, eidxc, gatesc = x1s[pq], hbfs[pq], eidxs[pq], gatess[pq]
            kx1, khbf, keidx, kgates = ("x1", pq), ("hbfp", pq), ("eidx", pq), ("gates", pq)
            P.dma("sp", lambda e: e.dma_start(out=x1c[:], in_=x1_d[t0:t0 + 128, :]), r=[("x1_d", ti)], w=[kx1])
            rmsnorm(x1c, kx1, ffn_b, "ffn_b", hbfc, khbf, 0)
            transpose8(hbfc, khbf, hT, "hT", 8, 0)
            for gq in range(4):
                bank = 1 + (gq % 2)
                for gi in range(4):
                    g = gq * 4 + gi
                    for kc in range(8):
                        P.op("pe", lambda e, g=g, gi=gi, kc=kc, bank=bank: e.matmul(
                            out=ps[bank][:, gi * 128:(gi + 1) * 128], lhsT=wq[:, kc, g * 128:(g + 1) * 128],
                            rhs=hT[:, kc, :], start=(kc == 0), stop=(kc == 7)),
                            r=["arena", "hT"], w=[("ps", bank)])
                eng = "act" if gq % 2 == 0 else "dve"
                if eng == "act":
                    P.op("act", lambda e, gq=gq, bank=bank: e.copy(
                        out=qT[:, gq * 4:(gq + 1) * 4, :].rearrange("p g t -> p (g t)"), in_=ps[bank][:]),
                        r=[("ps", bank)], w=[("qT", gq)])
                else:
                    P.op("dve", lambda e, gq=gq, bank=bank: e.tensor_copy(
                        out=qT[:, gq * 4:(gq + 1) * 4, :].rearrange("p g t -> p (g t)"), in_=ps[bank][:]),
                        r=[("ps", bank)], w=[("qT", gq)])
            for gq in range(4):
                bank = 1 + (gq % 2)
                for gi in range(4):
                    g = gq * 4 + gi
                    P.op("pe", lambda e, g=g, gi=gi, bank=bank: e.matmul(
                        out=ps[bank][:, gi * 128:(gi + 1) * 128], lhsT=qT[:, g, :], rhs=keysT[:, g, :],
                        start=True, stop=True),
                        r=[("qT", gq), "keysT"], w=[("ps", bank)])
                P.op("act", lambda e, gq=gq, bank=bank: e.copy(
                    out=sc[:, gq * 4:(gq + 1) * 4, :].rearrange("p g n -> p (g n)"), in_=ps[bank][:]),
                    r=[("ps", bank)], w=[("scg", gq * 4 + q_) for q_ in range(4)])
            if False:
                P.dma("sp", lambda e, t0=t0: e.dma_start(out=dbg_d["sc"][t0:t0 + 128, :],
                                                         in_=sc[:].rearrange("p g n -> p (g n)")),
                      r=[("sc", q) for q in range(4)], w=["dbg_sc"])
            for g in range(16):
                Kg = ("scg", g)
                P.op("dve", lambda e: e.max(out=top[:, g, 0:8], in_=sc[:, g, :]), r=[Kg], w=[("top", g, 0)])
                P.op("dve", lambda e: e.max_index(out=tidx[:, g, 0:8], in_max=top[:, g, 0:8], in_values=sc[:, g, :]),
                     r=[Kg, ("top", g, 0)], w=[("tidx", g, 0)])
                P.op("dve", lambda e: e.match_replace(out=sc[:, g, :], in_to_replace=top[:, g, 0:8],
                                                      in_values=sc[:, g, :], imm_value=-1e30),
                     r=[Kg, ("top", g, 0)], w=[Kg])
                P.op("dve", lambda e: e.max(out=top[:, g, 8:16], in_=sc[:, g, :]), r=[Kg], w=[("top", g, 1)])
                P.op("dve", lambda e: e.max_index(out=tidx[:, g, 8:16], in_max=top[:, g, 8:16],
                                                  in_values=sc[:, g, :]),
                     r=[Kg, ("top", g, 1)], w=[("tidx", g, 1)])
            sc_all = [("scg", g) for g in range(16)]
            topk_all = [("top", g, k) for g in range(16) for k in range(2)]
            tidx_all = [("tidx", g, k) for g in range(16) for k in range(2)]
            top4 = top[:].rearrange("p (h two) k -> p h two k", two=2)
            P.op("dve", lambda e: e.tensor_tensor(
                out=cand[:].rearrange("p h (a b) -> p h a b", b=16),
                in0=top4[:, :, 0, :].unsqueeze(3).to_broadcast([128, 8, 16, 16]),
                in1=top4[:, :, 1, :].unsqueeze(2).to_broadcast([128, 8, 16, 16]), op=ALU.add),
                r=topk_all, w=sc_all)
            P.op("dve", lambda e: e.tensor_copy(out=tidxf[:], in_=tidx[:]), r=tidx_all, w=["tidxf"])
            for h in range(8):
                Kc = [("scg", 2 * h), ("scg", 2 * h + 1)]
                P.op("dve", lambda e: e.max(out=best[:, h, 0:8], in_=cand[:, h, :]), r=Kc, w=[("best", h, 0)])
                P.op("dve", lambda e: e.max_index(out=bpos[:, h, 0:8], in_max=best[:, h, 0:8],
                                                  in_values=cand[:, h, :]),
                     r=Kc + [("best", h, 0)], w=[("bpos", h, 0)])
                P.op("dve", lambda e: e.match_replace(out=cand[:, h, :], in_to_replace=best[:, h, 0:8],
                                                      in_values=cand[:, h, :], imm_value=-1e30),
                     r=Kc + [("best", h, 0)], w=Kc)
                P.op("dve", lambda e: e.max(out=best[:, h, 8:16], in_=cand[:, h, :]), r=Kc, w=[("best", h, 1)])
                P.op("dve", lambda e: e.max_index(out=bpos[:, h, 8:16], in_max=best[:, h, 8:16],
                                                  in_values=cand[:, h, :]),
                     r=Kc + [("best", h, 1)], w=[("bpos", h, 1)])
            best_all = [("best", h, k) for h in range(8) for k in range(2)]
            bpos_all = [("bpos", h, k) for h in range(8) for k in range(2)]
            P.op("dve", lambda e: e.tensor_copy(out=bposf[:], in_=bpos[:]), r=bpos_all, w=["bposf"])
            bc_s = lambda t: t[:].unsqueeze(3).to_broadcast([128, 8, 16, 16])
            bc_c = lambda ap: ap.unsqueeze(1).unsqueeze(1).to_broadcast([128, 8, 16, 16])
            P.op("dve", lambda e: e.tensor_tensor(out=big4[:], in0=bc_s(bposf), in1=bc_c(iota16x16), op=ALU.is_ge),
                 r=["bposf", "consts"], w=sc_all)
            P.op("dve", lambda e: e.tensor_reduce(out=asel[:], in_=big4[:], axis=AX.X, op=ALU.add),
                 r=sc_all, w=["asel"])
            P.op("dve", lambda e: e.tensor_scalar(out=asel[:], in0=asel[:], scalar1=-1.0, scalar2=None, op0=ALU.add),
                 r=["asel"], w=["asel"])
            P.op("dve", lambda e: e.scalar_tensor_tensor(out=bsel[:], in0=asel[:], scalar=-16.0, in1=bposf[:],
                                                         op0=ALU.mult, op1=ALU.add),
                 r=["asel", "bposf"], w=["bsel"])
            tf4 = tidxf[:].rearrange("p (h two) k -> p h two k", two=2)
            for (sel, half, dst, dkey) in ((asel, 0, isel, "isel"), (bsel, 1, jsel, "jsel")):
                skey = "asel" if half == 0 else "bsel"
                P.op("dve", lambda e, sel=sel: e.tensor_tensor(out=big4[:], in0=bc_s(sel), in1=bc_c(iota16),
                                                               op=ALU.is_equal),
                     r=[skey, "consts"], w=sc_all)
                P.op("dve", lambda e, half=half: e.tensor_tensor(
                    out=big4[:], in0=big4[:],
                    in1=tf4[:, :, half, :].unsqueeze(2).to_broadcast([128, 8, 16, 16]), op=ALU.mult),
                    r=sc_all + ["tidxf"], w=sc_all)
                P.op("dve", lambda e, dst=dst: e.tensor_reduce(out=dst[:], in_=big4[:], axis=AX.X, op=ALU.add),
                     r=sc_all, w=[dkey])
            P.op("dve", lambda e: e.scalar_tensor_tensor(
                out=ef[:], in0=isel[:].rearrange("p h s -> p (h s)"), scalar=128.0,
                in1=jsel[:].rearrange("p h s -> p (h s)"), op0=ALU.mult, op1=ALU.add),
                r=["isel", "jsel"], w=["ef"])
            P.op("dve", lambda e: e.tensor_copy(out=eidxc[:], in_=ef[:]), r=["ef"], w=[keidx])
            P.op("dve", lambda e: e.tensor_tensor(out=gatesc[:], in0=best[:],
                                                  in1=best[:, :, 0:1].to_broadcast([128, 8, 16]), op=ALU.subtract),
                 r=best_all, w=[kgates])
            P.op("act", lambda e: e.activation(out=gatesc[:], in_=gatesc[:], func=AF.Exp), r=[kgates], w=[kgates])
            P.op("dve", lambda e: e.tensor_reduce(out=gz[:], in_=gatesc[:], axis=AX.X, op=ALU.add),
                 r=[kgates], w=["gz"])
            P.op("dve", lambda e: e.reciprocal(out=gz[:], in_=gz[:]), r=["gz"], w=["gz"])
            P.op("dve", lambda e: e.tensor_tensor(out=gatesc[:], in0=gatesc[:],
                                                  in1=gz[:].unsqueeze(2).to_broadcast([128, 8, 16]), op=ALU.mult),
                 r=[kgates, "gz"], w=[kgates])
            if False:
                P.dma("sp", lambda e, t0=t0: e.dma_start(out=dbg_d["eidx"][t0:t0 + 128, :], in_=eidx[:]),
                      r=["eidx"], w=["dbg_eidx"])
                P.dma("sp", lambda e, t0=t0: e.dma_start(out=dbg_d["gates"][t0:t0 + 128, :],
                                                         in_=gatesc[:].rearrange("p h s -> p (h s)")),
                      r=["gates"], w=["dbg_gates"])
        def gather(ti):
            pq = ti % 2
            accb = 5 if pq == 0 else 3
            x1c, hbfc, eidxc, gatesc = x1s[pq], hbfs[pq], eidxs[pq], gatess[pq]
            kx1, khbf, keidx, kgates = ("x1", pq), ("hbfp", pq), ("eidx", pq), ("gates", pq)
            def fin(g):
                gs = slice(g * GS, g * GS + GS)
                dk = (ti * (128 // GS) + g) % 2
                apk = [("actpre", q) for q in range(g * GS, g * GS + GS)]
                ga, gbb = gl_a[:, gs], gl_b[:, gs]
                gflat = gatesc[:].rearrange("p h s -> p (h s)")
                P.op("dve", lambda e: e.tensor_tensor(out=ga, in0=actpre[:, gs], in1=actpre[:, gs], op=ALU.mult),
                     r=apk, w=[("gla", g)])
                P.op("dve", lambda e: e.tensor_scalar(out=ga, in0=ga, scalar1=0.044715, scalar2=1.0,
                                                      op0=ALU.mult, op1=ALU.add), r=[("gla", g)], w=[("gla", g)])
                P.op("dve", lambda e: e.tensor_tensor(out=ga, in0=ga, in1=actpre[:, gs], op=ALU.mult),
                     r=[("gla", g)] + apk, w=[("gla", g)])
                P.op("act", lambda e: e.activation(out=gbb, in_=ga, func=AF.Sigmoid, scale=1.5957691216057308),
                     r=[("gla", g)], w=[("glb", g)])
                P.op("dve", lambda e: e.tensor_tensor(out=wts[:, gs], in0=actpre[:, gs], in1=gflat[:, gs],
                                                      op=ALU.mult), r=apk + [kgates], w=[("wts", g)])
                P.op("dve", lambda e: e.tensor_tensor(out=wts[:, gs], in0=wts[:, gs], in1=gbb, op=ALU.mult),
                     r=[("glb", g), ("wts", g)], w=[("wts", g)])
                for j in range(GS):
                    P.op("act", lambda e: e.activation(out=dg[dk][:, j, :], in_=ident_bf[:], func=AF.Copy,
                                                       scale=wts[:, g * GS + j:g * GS + j + 1]),
                         r=[("wts", g), "ident_bf"], w=[("dg", dk)])
                for j in range(GS):
                    sj = g * GS + j
                    bj = (ti * 128 + sj) % NG
                    for half in range(2):
                        mm(ps[accb + half][:], dg[dk][:, j, :], gb[bj][:, D + half * 512:D + (half + 1) * 512],
                           sj == 0, sj == 127, [("dg", dk), ("gb", bj)], accb + half)

            for s in range(128):
                gidx = ti * 128 + s
                b = gidx % NG
                pb_ = gidx % 3
                P.dma("pool", lambda e: e.indirect_dma_start(
                    out=gb[b], out_offset=None, in_=uvb_d,
                    in_offset=bass.IndirectOffsetOnAxis(ap=eidxc[:, s:s + 1], axis=0)),
                    r=[keidx], w=[("gb", b)])
                if s % 4 == 3:
                    P.op("dve", lambda e: e.scalar_tensor_tensor(
                        out=prod[pb_][:], in0=gb[b][:, 0:D], scalar=1.0, in1=hbfc[:], op0=ALU.mult, op1=ALU.mult,
                        accum_out=actpre[:, s:s + 1]),
                        r=[("gb", b), khbf], w=[("prod", pb_), ("actpre", s)])
                else:
                    P.op("dve", lambda e: e.tensor_tensor(out=prod[pb_][:], in0=gb[b][:, 0:D], in1=hbfc[:],
                                                          op=ALU.mult),
                         r=[("gb", b), khbf], w=[("prod", pb_)])
                    P.op("act", lambda e: e.activation(out=prod[pb_][:], in_=prod[pb_][:], func=AF.Copy,
                                                       accum_out=actpre[:, s:s + 1]),
                         r=[("prod", pb_)], w=[("prod", pb_), ("actpre", s)])
                if s % GS == GS - 1:
                    if s // GS >= 1:
                        fin(s // GS - 1)
            fin(128 // GS - 1)
            for half in range(2):
                P.op("dve", lambda e: e.tensor_tensor(out=y[:, half * 512:(half + 1) * 512],
                                                      in0=x1c[:, half * 512:(half + 1) * 512], in1=ps[accb + half][:],
                                                      op=ALU.add), r=[kx1, ("ps", accb + half)], w=["y"])
            if False:
                P.dma("sp", lambda e, t0=t0: e.dma_start(out=dbg_d["x2"][t0:t0 + 128, :], in_=y[:]),
                      r=["y"], w=["dbg_x2"])
        def tail(ti):
            t0 = ti * 128
            rmsnorm(y, "y", ple_b, "ple_b", tbf, "tbf", 2)
            transpose8(tbf, "tbf", hT, "hT", 8, 0)
            for half in range(2):
                bank = 1 + half
                for kc in range(8):
                    P.op("pe", lambda e, half=half, kc=kc, bank=bank: e.matmul(
                        out=ps[bank][:], lhsT=hT[:, kc, :], rhs=gate_w[:, kc, half * 512:(half + 1) * 512],
                        start=(kc == 0), stop=(kc == 7)), r=["hT", "arena"], w=[("ps", bank)])
                P.op("act", lambda e, half=half, bank=bank: e.activation(
                    out=sig[:, half * 512:(half + 1) * 512], in_=ps[bank][:], func=AF.Sigmoid),
                    r=[("ps", bank)], w=[("gb", 1)])
            P.dma("sp", lambda e, t0=t0: e.dma_start(out=pt[:], in_=p_d[t0:t0 + 128, :]), w=["pt"])
            P.op("act", lambda e: e.copy(out=ptbf[:], in_=pt[:]), r=["pt"], w=["ptbf"])
            transpose8(ptbf, "ptbf", pT, "pT", 2, 0)
            for half in range(2):
                bank = 1 + half
                for kc in range(2):
                    P.op("pe", lambda e, half=half, kc=kc, bank=bank: e.matmul(
                        out=ps[bank][:], lhsT=pT[:, kc, :], rhs=proj_w[:, kc, half * 512:(half + 1) * 512],
                        start=(kc == 0), stop=(kc == 1)), r=["pT", "proj_w"], w=[("ps", bank)])
                P.op("dve", lambda e, half=half, bank=bank: e.tensor_tensor(
                    out=x3[:, half * 512:(half + 1) * 512], in0=sig[:, half * 512:(half + 1) * 512],
                    in1=ps[bank][:], op=ALU.mult), r=[("gb", 1), ("ps", bank)], w=[("gb", 2)])
            P.op("dve", lambda e: e.tensor_tensor(out=x3[:], in0=x3[:], in1=y[:], op=ALU.add),
                 r=[("gb", 2), "y"], w=[("gb", 2)])
            rmsnorm(x3, ("gb", 2), fin_b, "fin_b", outt, ("gb", 3), 4)
            P.dma("sp", lambda e, t0=t0: e.dma_start(out=out_d[t0:t0 + 128, :], in_=outt[:]),
                  r=[("gb", 3)], w=["out_d"])
        head(0)
        for ti in range(NT):
            P.begin()
            gather(ti)
            G = P.end()
            H = []
            if ti + 1 < NT:
                P.begin()
                head(ti + 1)
                H = P.end()
            P.merge([G, H], [1.0, 1.0])
            tail(ti)
    P.op("sp", None, r=["out_d"] + [("x1_d", i) for i in range(NT)], w=[])
    P.emit(es)
    es.close()
    return nc


def kernel(**inputs):
    NT = SEQ // 128
    nc = bass.Bass("TRN2", target_bir_lowering=False)
    build(nc, NT)
    shared = core_inputs(inputs, 0)
    in_maps = []
    for b in range(8):
        m = dict(shared)
        m["x"] = np.ascontiguousarray(np.asarray(inputs["x"][b], dtype=np.float32))
        m["p"] = np.ascontiguousarray(np.asarray(inputs["p"][0, b], dtype=np.float32))
        m["positions"] = np.ascontiguousarray(np.asarray(inputs["positions"][b], dtype=np.int32))
        in_maps.append(m)
    res = run_bass_kernel_spmd(nc, in_maps, core_ids=list(range(8)))
    return np.stack([np.asarray(r["out"], dtype=np.float32) for r in res.results], axis=0)


def core_inputs(inputs, b, T=SEQ):
    f = lambda a: np.ascontiguousarray(np.asarray(a, dtype=np.float32))
    w_in = np.asarray(inputs["w_in"][0], dtype=np.float32)
    o_sq = 4 * 512 + 8
    swq = w_in[:, o_sq:o_sq + 512].reshape(D, 8, 64)
    swq_p = np.stack([np.concatenate([swq[:, c], swq[:, 4 + c]], axis=1) for c in range(4)], axis=1).reshape(D, 512)
    w_in_p = np.concatenate([w_in[:, :o_sq], swq_p, w_in[:, o_sq + 512:]], axis=1)
    convT = np.asarray(inputs["conv_w"][0], dtype=np.float32).reshape(4, 12, 128).transpose(2, 1, 0).reshape(128, 48)
    return {
        "x": f(inputs["x"][b, :T]),
        "p": f(inputs["p"][0, b, :T]),
        "positions": np.ascontiguousarray(np.asarray(inputs["positions"][b, :T], dtype=np.int32)),
        "consts": make_consts(),
        "mix_norm": f(inputs["mix_norm"][0]),
        "w_in": f(w_in_p),
        "convT": f(convT),
        "dn_dt_bias": f(inputs["dn_dt_bias"][0]),
        "dn_a_log": f(inputs["dn_a_log"][0]),
        "dn_out_norm": f(inputs["dn_out_norm"][0]),
        "attn_sinks": f(inputs["attn_sinks"][0]),
        "w_out": f(inputs["w_out"][0]),
        "ffn_norm": f(inputs["ffn_norm"][0]),
        "ple_norm": f(inputs["ple_norm"][0]),
        "final_norm": f(inputs["final_norm"]),
        "peer_wq": f(inputs["peer_wq"][0]),
        "peer_keysT": f(np.asarray(inputs["peer_keys"][0]).reshape(16, 128, 128).transpose(0, 2, 1)),
        "peer_uv": f(np.concatenate([np.asarray(inputs["peer_u"][0], dtype=np.float32),
                                     np.asarray(inputs["peer_v"][0], dtype=np.float32)], axis=1)),
        "ple_gate": f(inputs["ple_gate"][0]),
        "ple_proj": f(inputs["ple_proj"][0]),
    }
```
